# Optimizing a Trainium2 kernel written in Bass

```python
import jax, jax.numpy as jnp
from jax import lax
import numpy as np

D_MODEL = 2048
BATCH = 8
SEQ = 2048
DEPTH = 1

GDN_HEADS = 8
GDN_HEAD_DIM = 128
GDN_WIDTH = GDN_HEADS * GDN_HEAD_DIM
CONV_WIDTH = 4
GDN_CHUNK = 64
MOBA_HEADS = 8
MOBA_HEAD_DIM = 128
MOBA_WIDTH = MOBA_HEADS * MOBA_HEAD_DIM
MOBA_BLOCK = 256
MOBA_TOPK = 3
MOBA_QCHUNK = 128

NORM_EPS = 1e-6
NEG_INF = -1e30
SPLITS = (3 * GDN_WIDTH, GDN_WIDTH, GDN_HEADS, GDN_HEADS, 3 * MOBA_WIDTH, MOBA_WIDTH, D_MODEL, D_MODEL)
IN_WIDTH = 3 * GDN_WIDTH + GDN_WIDTH + 2 * GDN_HEADS + 3 * MOBA_WIDTH + MOBA_WIDTH + 2 * D_MODEL

kernel_name = "hybrid_gdn_moba_gated_merge"


def rms_norm(x, w):
    xf = x.astype(jnp.float32)
    y = xf * lax.rsqrt(jnp.mean(xf * xf, axis=-1, keepdims=True) + NORM_EPS)
    return (y * w.astype(jnp.float32)).astype(x.dtype)


def l2_normalize(x):
    return x * lax.rsqrt(jnp.sum(x * x, axis=-1, keepdims=True) + NORM_EPS)


def causal_depthwise_conv(x, w):
    K = w.shape[0]
    S = x.shape[1]
    xp = jnp.pad(x, ((0, 0), (K - 1, 0), (0, 0)))
    out = w[0] * xp[:, 0:S]
    for j in range(1, K):
        out = out + w[j] * xp[:, j:j + S]
    return out


def gated_delta_rule(q, k, v, beta, g):
    B, H, S, dk = q.shape
    dv = v.shape[-1]
    C = GDN_CHUNK
    N = S // C
    q = q * (dk ** -0.5)
    q = q.reshape(B, H, N, C, dk)
    k = k.reshape(B, H, N, C, dk)
    v = v.reshape(B, H, N, C, dv)
    beta = beta.reshape(B, H, N, C)
    g = jnp.cumsum(g.reshape(B, H, N, C), axis=-1)
    kb = k * beta[..., None]
    vb = v * beta[..., None]
    incl = jnp.tril(jnp.ones((C, C), dtype=bool))
    strict = jnp.tril(jnp.ones((C, C), dtype=bool), -1)
    decay = jnp.exp(jnp.where(incl, g[..., :, None] - g[..., None, :], NEG_INF))
    L = jnp.where(strict, jnp.einsum('bhnid,bhnjd->bhnij', kb, k) * decay, 0.0)
    eye = jnp.eye(C, dtype=jnp.float32)
    T = lax.linalg.triangular_solve(L + eye, jnp.broadcast_to(eye, L.shape), left_side=True, lower=True)
    u = jnp.einsum('bhnij,bhnje->bhnie', T, vb)
    w = jnp.einsum('bhnij,bhnjd->bhnid', T, kb * jnp.exp(g)[..., None])
    intra = jnp.where(incl, jnp.einsum('bhnid,bhnjd->bhnij', q, k) * decay, 0.0)
    qg = q * jnp.exp(g)[..., None]
    kdec = k * jnp.exp(g[..., -1:] - g)[..., None]
    g_last = jnp.exp(g[..., -1])
    xs = (jnp.moveaxis(qg, 2, 0), jnp.moveaxis(kdec, 2, 0), jnp.moveaxis(u, 2, 0),
          jnp.moveaxis(w, 2, 0), jnp.moveaxis(intra, 2, 0), jnp.moveaxis(g_last, 2, 0))

    def step(state, inp):
        qg_i, kd_i, u_i, w_i, a_i, gl_i = inp
        v_new = u_i - jnp.einsum('bhcd,bhde->bhce', w_i, state)
        o = jnp.einsum('bhcd,bhde->bhce', qg_i, state) + jnp.einsum('bhij,bhje->bhie', a_i, v_new)
        state = state * gl_i[..., None, None] + jnp.einsum('bhcd,bhce->bhde', kd_i, v_new)
        return state, o

    state0 = jnp.zeros((B, H, dk, dv), jnp.float32)
    _, o = lax.scan(step, state0, xs)
    return jnp.moveaxis(o, 0, 2).reshape(B, H, S, dv)


def moba_attention(q, k, v):
    B, S, H, d = q.shape
    NB = -(-S // MOBA_BLOCK)
    S_pad = NB * MOBA_BLOCK
    QC = MOBA_QCHUNK
    NQ = S // QC
    k_sel = min(MOBA_TOPK, NB)
    pad = ((0, 0), (0, S_pad - S), (0, 0), (0, 0))
    kp = jnp.pad(k, pad).reshape(B, NB, MOBA_BLOCK, H, d).transpose(0, 3, 1, 2, 4)
    vp = jnp.pad(v, pad).reshape(B, NB, MOBA_BLOCK, H, d).transpose(0, 3, 1, 2, 4)
    kmean = jnp.mean(kp.astype(jnp.float32), axis=3)
    qh = q.transpose(0, 2, 1, 3)
    gate = jnp.einsum('bhsd,bhnd->bhsn', qh.astype(jnp.float32), kmean)
    q_blk = jnp.arange(S) // MOBA_BLOCK
    fully_past = jnp.arange(NB)[None, :] < q_blk[:, None]
    gate = jnp.where(fully_past, gate, NEG_INF)
    _, idx = lax.top_k(gate, k_sel)

    q_steps = qh.reshape(B, H, NQ, QC, d).transpose(0, 2, 1, 3, 4).reshape(B * NQ, H, QC, d)
    idx_steps = idx.reshape(B, H, NQ, QC, k_sel).transpose(0, 2, 1, 3, 4).reshape(B * NQ, H, QC, k_sel)
    b_ids = jnp.repeat(jnp.arange(B, dtype=jnp.int32), NQ)
    c_ids = jnp.tile(jnp.arange(NQ, dtype=jnp.int32), B)
    scale = MOBA_HEAD_DIM ** -0.5
    hh = jnp.arange(H)[:, None, None]

    def step(args):
        qc, ic, b, c = args
        kb = kp[b]
        vb = vp[b]
        kg = kb[hh, ic]
        vg = vb[hh, ic]
        own = (c * QC) // MOBA_BLOCK
        ko = lax.dynamic_index_in_dim(kb, own, axis=1, keepdims=False)
        vo = lax.dynamic_index_in_dim(vb, own, axis=1, keepdims=False)
        s_past = jnp.einsum('hqd,hqjpd->hqjp', qc, kg, preferred_element_type=jnp.float32)
        s_past = s_past.reshape(H, QC, k_sel * MOBA_BLOCK)
        s_own = jnp.einsum('hqd,hpd->hqp', qc, ko, preferred_element_type=jnp.float32)
        t_abs = c * QC + jnp.arange(QC)
        slot_ok = jnp.arange(k_sel)[None, :] < (t_abs // MOBA_BLOCK)[:, None]
        slot_ok = jnp.broadcast_to(slot_ok[:, :, None], (QC, k_sel, MOBA_BLOCK)).reshape(QC, k_sel * MOBA_BLOCK)
        own_ok = (own * MOBA_BLOCK + jnp.arange(MOBA_BLOCK))[None, :] <= t_abs[:, None]
        mask = jnp.concatenate([slot_ok, own_ok], axis=-1)[None]
        s = jnp.concatenate([s_past, s_own], axis=-1) * scale
        p = jax.nn.softmax(jnp.where(mask, s, NEG_INF), axis=-1).astype(vg.dtype)
        p_past = p[..., :k_sel * MOBA_BLOCK].reshape(H, QC, k_sel, MOBA_BLOCK)
        p_own = p[..., k_sel * MOBA_BLOCK:]
        return (jnp.einsum('hqjp,hqjpd->hqd', p_past, vg)
                + jnp.einsum('hqp,hpd->hqd', p_own, vo))

    o = lax.map(step, (q_steps, idx_steps, b_ids, c_ids))
    return o.reshape(B, NQ, H, QC, d).transpose(0, 1, 3, 2, 4).reshape(B, S, H * d)


def hybrid_mixer(h, w_in, conv_w, a_log, dt_bias, gdn_norm_w, w_branch_a, w_branch_b, w_out):
    B, S, _ = h.shape
    proj = h @ w_in
    offsets = []
    acc = 0
    for n in SPLITS[:-1]:
        acc += n
        offsets.append(acc)
    gdn_qkv, gdn_z, gdn_b, gdn_a, moba_qkv, moba_z, gate_a, gate_b = jnp.split(proj, offsets, axis=-1)

    qkv = jax.nn.silu(causal_depthwise_conv(gdn_qkv, conv_w)).astype(jnp.float32)
    qa, ka, va = jnp.split(qkv, 3, axis=-1)
    to_heads = lambda t: t.reshape(B, S, GDN_HEADS, GDN_HEAD_DIM).transpose(0, 2, 1, 3)
    qa, ka, va = l2_normalize(to_heads(qa)), l2_normalize(to_heads(ka)), to_heads(va)
    beta = jax.nn.sigmoid(gdn_b.astype(jnp.float32)).transpose(0, 2, 1)
    g = (-jnp.exp(a_log.astype(jnp.float32))
         * jax.nn.softplus(gdn_a.astype(jnp.float32) + dt_bias.astype(jnp.float32))).transpose(0, 2, 1)
    oa = gated_delta_rule(qa, ka, va, beta, g).transpose(0, 2, 1, 3)
    oa = oa * lax.rsqrt(jnp.mean(oa * oa, axis=-1, keepdims=True) + NORM_EPS)
    za = gdn_z.astype(jnp.float32).reshape(B, S, GDN_HEADS, GDN_HEAD_DIM)
    oa = (oa * gdn_norm_w.astype(jnp.float32) * jax.nn.silu(za)).reshape(B, S, GDN_WIDTH).astype(h.dtype)
    u_a = oa @ w_branch_a

    qb, kb, vb = jnp.split(moba_qkv, 3, axis=-1)
    to_mh = lambda t: t.reshape(B, S, MOBA_HEADS, MOBA_HEAD_DIM)
    ob = moba_attention(to_mh(qb), to_mh(kb), to_mh(vb)).astype(h.dtype)
    ob = ob * jax.nn.silu(moba_z)
    u_b = ob @ w_branch_b

    merged = jax.nn.sigmoid(gate_a) * u_a + jax.nn.sigmoid(gate_b) * u_b
    return merged @ w_out


def setup_inputs(seed: int = 0) -> dict:
    key = jax.random.key(seed)
    ks = jax.random.split(key, 12)
    f32 = jnp.float32
    x = jax.random.normal(ks[0], (BATCH, SEQ, D_MODEL), f32)
    pre_norm_w = 1.0 + 0.05 * jax.random.normal(ks[1], (DEPTH, D_MODEL), f32)
    w_in = jax.random.normal(ks[2], (DEPTH, D_MODEL, IN_WIDTH), f32) * (D_MODEL ** -0.5)
    conv_w = jax.random.normal(ks[3], (DEPTH, CONV_WIDTH, 3 * GDN_WIDTH), f32) * (CONV_WIDTH ** -0.5)
    a_log = jnp.log(jax.random.uniform(ks[4], (DEPTH, GDN_HEADS), f32, minval=1.0, maxval=16.0))
    dt = jnp.exp(jax.random.uniform(ks[5], (DEPTH, GDN_HEADS), f32, minval=np.log(1e-3), maxval=np.log(1e-1)))
    dt_bias = dt + jnp.log(-jnp.expm1(-dt))
    gdn_norm_w = 1.0 + 0.05 * jax.random.normal(ks[6], (DEPTH, GDN_HEAD_DIM), f32)
    w_branch_a = jax.random.normal(ks[7], (DEPTH, GDN_WIDTH, D_MODEL), f32) * (GDN_WIDTH ** -0.5)
    w_branch_b = jax.random.normal(ks[8], (DEPTH, MOBA_WIDTH, D_MODEL), f32) * (MOBA_WIDTH ** -0.5)
    w_out = jax.random.normal(ks[9], (DEPTH, D_MODEL, D_MODEL), f32) * (D_MODEL ** -0.5)
    post_norm_w = 1.0 + 0.05 * jax.random.normal(ks[10], (DEPTH, D_MODEL), f32)
    return {"x": x, "pre_norm_w": pre_norm_w, "w_in": w_in, "conv_w": conv_w, "a_log": a_log,
            "dt_bias": dt_bias, "gdn_norm_w": gdn_norm_w, "w_branch_a": w_branch_a,
            "w_branch_b": w_branch_b, "w_out": w_out, "post_norm_w": post_norm_w}


def reference(x, pre_norm_w, w_in, conv_w, a_log, dt_bias, gdn_norm_w, w_branch_a, w_branch_b, w_out, post_norm_w):
    for l in range(DEPTH):
        h = rms_norm(x, pre_norm_w[l])
        y = hybrid_mixer(h, w_in[l], conv_w[l], a_log[l], dt_bias[l], gdn_norm_w[l],
                         w_branch_a[l], w_branch_b[l], w_out[l])
        x = x + rms_norm(y, post_norm_w[l])
    return x
```

```python
from contextlib import ExitStack

import numpy as np
import concourse.bass as bass
import concourse.mybir as mybir
from concourse.bass_utils import run_bass_kernel_spmd

F32 = mybir.dt.float32
BF16 = mybir.dt.bfloat16
AF = mybir.ActivationFunctionType
ALU = mybir.AluOpType
AX = mybir.AxisListType

S = 2048
D = 2048
NT = S // 128
KC = D // 128
H = 8
IN_W = 12304
EPS = 1e-6
NEG = -30000.0
STAGE_LIMIT = 1000
PRE_TAPS = ("u", "wTn", "iT", "Ln", "kdec", "N0")

OFF_GQ, OFF_GK, OFF_GV = 0, 1024, 2048
OFF_GZ = 3072
OFF_GB = 4096
OFF_GA = 4104
OFF_MQ, OFF_MK, OFF_MV = 4112, 4112 + 1024, 4112 + 2048
OFF_MZ = 4112 + 3072
OFF_GATE_A = 8208
OFF_GATE_B = 8208 + 2048

C_ID, C_TRI, C_UG, C_ONES, C_NBS, C_NBI, C_MU0 = 0, 1, 2, 3, 4, 5, 6
NCB = 13


def make_consts():
    r = np.arange(128)[:, None]
    c = np.arange(128)[None, :]
    blocks = [None] * NCB
    blocks[C_ID] = (r == c)
    blocks[C_TRI] = (r <= c)
    blocks[C_UG] = (r > c)
    blocks[C_ONES] = np.ones((128, 128), bool)
    nbs = np.where(r > c, 0.0, NEG)
    nbi = np.where(r <= c, 0.0, NEG)
    out = []
    for k in range(NCB):
        if k == C_NBS:
            out.append(nbs.astype(np.float32))
        elif k == C_NBI:
            out.append(nbi.astype(np.float32))
        elif k >= C_MU0:
            l = k - C_MU0
            b = 1 << l
            m = ((r // (2 * b)) == (c // (2 * b))) & ((r % (2 * b)) < b) & ((c % (2 * b)) >= b)
            out.append(m.astype(np.float32))
        else:
            out.append(blocks[k].astype(np.float32))
    cf = np.concatenate(out, axis=1)
    cb = np.zeros((128, 256), np.float32)
    cb[:, :128] = nbi
    es = np.zeros((128, 8 * 128), np.float32)
    for n in range(8):
        es[n, n * 128:(n + 1) * 128] = 1.0
    return np.ascontiguousarray(cf), np.ascontiguousarray(np.concatenate([cb, es], axis=1))


class Buf:
    __slots__ = ("name", "last_w", "readers", "sem", "dcnt", "iv", "ov", "excl")
    REG = []

    def __init__(self, name, iv=None, excl=False):
        self.excl = excl
        self.name = name
        self.last_w = None
        self.readers = []
        self.sem = None
        self.dcnt = 0
        if iv is not None and not isinstance(iv, list):
            iv = [iv]
        self.iv = iv
        self.ov = [self]
        if iv is not None:
            for o in Buf.REG:
                if any(a[0] == b[0] and a[1] < b[2] and b[1] < a[2] for a in iv for b in o.iv):
                    o.ov.append(self)
                    self.ov.append(o)
            Buf.REG.append(self)


class Op:
    __slots__ = ("eng", "fn", "deps", "ndma", "sig", "sigval", "wbuf")


class Prog:
    ENGS = ("pe", "act", "dve", "pool", "sp")

    def __init__(self, nc, stack):
        self.nc = nc
        self.stack = stack
        self.ops = []
        self.dma_bufs = []

    def add(self, eng, fn, reads=(), writes=(), ndma=0, waw=True):
        idx = len(self.ops)
        deps = set()
        for b0 in reads:
            for b in b0.ov:
                if b.last_w is not None:
                    deps.add(b.last_w)
                if b.excl:
                    for r in b.readers:
                        if self.ops[r].eng != eng:
                            deps.add(r)
        for b0 in writes:
            for b in b0.ov:
                if b.last_w is not None and (waw or b is not b0):
                    deps.add(b.last_w)
                deps.update(b.readers)
        deps.discard(idx)
        for b in reads:
            b.readers.append(idx)
        for b in writes:
            b.last_w = idx
            b.readers = []
        op = Op()
        op.eng = eng
        op.fn = fn
        op.deps = deps
        op.ndma = ndma
        op.sig = False
        op.sigval = 0
        op.wbuf = None
        if ndma:
            assert len(writes) == 1
            op.wbuf = writes[0]
            if op.wbuf.sem is None:
                op.wbuf.sem = True
                self.dma_bufs.append(op.wbuf)
        self.ops.append(op)
        return idx

    def finalize(self):
        nc = self.nc
        ops = self.ops
        for op in ops:
            for d in op.deps:
                p = ops[d]
                if p.eng == "pe" and op.eng == "pe" and not p.ndma:
                    continue
                p.sig = True
        esem = {e: self.stack.enter_context(nc.semaphore("sem_" + e)) for e in self.ENGS}
        for b in self.dma_bufs:
            b.sem = self.stack.enter_context(nc.semaphore("dsem_" + b.name))
        cnt = {e: 0 for e in self.ENGS}
        for op in ops:
            if op.ndma:
                op.wbuf.dcnt += 16 * op.ndma
                op.sigval = op.wbuf.dcnt
            elif op.sig:
                cnt[op.eng] += 1
                op.sigval = cnt[op.eng]

        def emit(ename, e):
            waited = {}
            for op in ops:
                if op.eng != ename:
                    continue
                need = {}
                for d in op.deps:
                    p = ops[d]
                    if p.ndma:
                        sem = p.wbuf.sem
                    else:
                        if p.eng == "pe" and ename == "pe":
                            continue
                        sem = esem[p.eng]
                    k = id(sem)
                    if k not in need or need[k][1] < p.sigval:
                        need[k] = (sem, p.sigval)
                for k, (sem, v) in need.items():
                    if waited.get(k, 0) < v:
                        e.wait_ge(sem, v)
                        waited[k] = v
                ins = op.fn(e)
                if op.ndma:
                    pass
                elif op.sig:
                    ins.then_inc(esem[ename], 1)

        with nc.Block() as block:
            @block.tensor
            def _(e):
                emit("pe", e)

            @block.scalar
            def _(e):
                emit("act", e)

            @block.vector
            def _(e):
                emit("dve", e)

            @block.gpsimd
            def _(e):
                emit("pool", e)

            @block.sync
            def _(e):
                emit("sp", e)


def build(dbg=(), nheads_g=H, nheads_m=H, phases="ABCD"):
    Buf.REG = []
    nc = bass.Bass("TRN2", target_bir_lowering=False)
    stack = ExitStack()
    P = Prog(nc, stack)

    def dram(name, shape, dt=F32, kind="ExternalInput"):
        return nc.dram_tensor(name, list(shape), dt, kind=kind).ap()

    x = dram("x", [S, D])
    pre_w = dram("pre_w", [1, D])
    post_w = dram("post_w", [1, D])
    w_in = dram("w_in", [D, IN_W])
    conv_wT = dram("conv_wT", [3072, 4])
    a_log = dram("a_log16", [1, 128])
    dt_bias = dram("dt_bias16", [1, 128])
    gnw_d = dram("gnw", [128, 1])
    w_a = dram("w_a", [1024, D])
    w_b = dram("w_b", [1024, D])
    w_out = dram("w_out", [D, D])
    cf_d = dram("cf", [128, NCB * 128])
    cm_d = dram("cm", [128, 256 + 1024])
    out = dram("out", [S, D], kind="ExternalOutput")

    def sb(name, shape, dt=F32):
        return stack.enter_context(nc.sbuf_tensor(name, list(shape), dt))

    hT = sb("hT", [128, KC, S], BF16)
    oaT = sb("oaT", [128, H, S], BF16)
    cf = sb("cf_sb", [128, NCB * 128], F32)
    ARENA_BYTES = 100 * 1024
    arena = sb("arena", [128, ARENA_BYTES // 4], F32)
    ps = [stack.enter_context(nc.psum_tensor("ps%d" % i, [128, 512], F32)) for i in range(8)]
    hTb = [Buf("hT%d" % i) for i in range(4)]
    oaTb = [Buf("oaT%d" % i) for i in range(4)]
    cfb = Buf("cf")

    pbank = [Buf("psb%d" % i, excl=True) for i in range(8)]

    def psbuf(bank, c0=0, c1=512):
        return pbank[bank]

    apos = [0]

    def carve(name, free_shape, dt=F32):
        esz = 4 if dt == F32 else 2
        n = int(np.prod(free_shape))
        nb = (n * esz + 31) // 32 * 32
        off = apos[0]
        apos[0] += nb
        assert apos[0] <= ARENA_BYTES, (name, apos[0])
        ap = arena[:, off // 4:(off + nb) // 4]
        if dt != F32:
            ap = ap.bitcast(dt)
        ap = ap[:, 0:n]
        if len(free_shape) == 2:
            ap = ap.rearrange("p (a b) -> p a b", a=free_shape[0])
        elif len(free_shape) == 3:
            ap = ap.rearrange("p (a b c) -> p a b c", a=free_shape[0], b=free_shape[1])
        return ap, Buf(name, iv=("arena", off, off + nb))

    def subbufs(parent, name, pieces):
        base = parent.iv[0][1]
        return [Buf("%s_%d" % (name, i), iv=[("arena", base + lo, base + hi) for lo, hi in pc])
                for i, pc in enumerate(pieces)]

    def cblk(k):
        return cf[:, k * 128:(k + 1) * 128]

    def dma(eng, out_ap, in_ap, wbuf, reads=(), waw=True):
        def fn(e):
            return e.dma_start(out=out_ap, in_=in_ap).then_inc(wbuf.sem, 16)
        P.add(eng, fn, reads=reads, writes=[wbuf], ndma=1, waw=waw)

    def load_w(dst_ap, wdram, c0, ncols, wbuf):
        src = wdram[:, c0:c0 + ncols].rearrange("(kc p) c -> p kc c", p=128)
        dma("pool", dst_ap, src, wbuf)

    outb = Buf("outdram")
    dbg_outs = {}

    def tap(name, ap, shape, dt, rbuf):
        d = dram("dbg_" + name, shape, dt, kind="ExternalOutput")
        dbg_outs[name] = d
        dma("sp", d, ap, outb, reads=rbuf, waw=False)

    dma("sp", cf[:], cf_d, cfb)
    small = sb("small", [128, 1280], F32)
    epsc = small[:, 0:2]
    epsb = Buf("epsc")
    P.add("dve", lambda e: e.memset(small[:, 0:1], EPS), writes=[epsb])
    P.add("dve", lambda e: e.memset(small[:, 1:2], 1.0), writes=[epsb])
    st = small[:, 8:16]
    stb = Buf("st")
    I_ = cblk(C_ID)

    amark = apos[0]
    xs, xsb = zip(*[carve("xs%d" % i, [D]) for i in range(2)])
    xn, xnb = carve("xn", [D])
    prew, prewb = carve("prew", [D])
    dma("sp", prew, pre_w.partition_broadcast(128), prewb)
    for tt in range(NT):
        sl = tt % 2
        dma("sp", xs[sl], x[tt * 128:(tt + 1) * 128, :], xsb[sl])
        P.add("act", lambda e, sl=sl: e.activation(out=xn, in_=xs[sl], func=AF.Square, accum_out=st[:, 0:1]),
              reads=[xsb[sl]], writes=[xnb, stb])
        P.add("act", lambda e: e.activation(out=st[:, 1:2], in_=st[:, 0:1], func=AF.Sqrt, bias=epsc[:, 0:1],
                                            scale=1.0 / D), reads=[stb, epsb], writes=[stb])
        P.add("dve", lambda e: e.reciprocal(out=st[:, 2:3], in_=st[:, 1:2]), reads=[stb], writes=[stb])
        P.add("dve", lambda e, sl=sl: e.scalar_tensor_tensor(out=xn, in0=xs[sl], scalar=st[:, 2:3], in1=prew,
                                                             op0=ALU.mult, op1=ALU.mult),
              reads=[xsb[sl], stb, prewb], writes=[xnb])
        for g in range(4):
            bank = (tt % 2) * 4 + g
            pb = psbuf(bank)

            def tr(e, g=g, bank=bank):
                ins = None
                for j in range(4):
                    kc = g * 4 + j
                    ins = e.transpose(out=ps[bank][:, j * 128:(j + 1) * 128], in_=xn[:, kc * 128:(kc + 1) * 128],
                                      identity=I_)
                return ins
            P.add("pe", tr, reads=[xnb, cfb], writes=[pb])
            dst = hT[:, g * 4:(g + 1) * 4, tt * 128:(tt + 1) * 128]
            src = ps[bank][:].rearrange("p (a b) -> p a b", a=4)
            if g % 2 == 0:
                P.add("act", lambda e, dst=dst, src=src: e.copy(out=dst, in_=src), reads=[pb], writes=[hTb[tt // 4]])
            else:
                P.add("dve", lambda e, dst=dst, src=src: e.tensor_copy(out=dst, in_=src), reads=[pb],
                      writes=[hTb[tt // 4]])
    if "hT" in dbg:
        tap("hT", hT[:], [128, KC, S], BF16, hTb)
    apos[0] = amark

    if "B" in phases:
        bmark = apos[0]
        betas = small[:, 16:144].rearrange("p (t h) -> p t h", t=NT)
        nbetas = small[:, 144:272].rearrange("p (t h) -> p t h", t=NT)
        gs = small[:, 272:400].rearrange("p (t h) -> p t h", t=NT)
        bgs = small[:, 400:528].rearrange("p (t h) -> p t h", t=NT)
        egs = small[:, 528:912].rearrange("p (t c) -> p t c", t=NT)
        negA = small[:, 912:1040]
        dtb = small[:, 1040:1168]
        gnw = small[:, 1168:1169]
        cw = small[:, 1172:1268].rearrange("p (c j) -> p c j", c=24)
        gqb = Buf("gatesq")
        cwb = Buf("cw")
        xa, xab = carve("xa", [128])
        wg, wgb = carve("wg", [KC, 16], BF16)
        load_w(wg, w_in, OFF_GB, 16, wgb)
        dma("sp", negA, a_log.partition_broadcast(128), gqb)
        dma("sp", dtb, dt_bias.partition_broadcast(128), gqb)
        dma("sp", gnw, gnw_d, gqb)
        dma("sp", cw, conv_wT.rearrange("(c p) j -> p c j", p=128), cwb)
        P.add("act", lambda e: e.activation(out=negA, in_=negA, func=AF.Exp), reads=[gqb], writes=[gqb])
        P.add("dve", lambda e: e.tensor_scalar(out=negA, in0=negA, scalar1=-1.0, scalar2=None, op0=ALU.mult),
              reads=[gqb], writes=[gqb])
        pg = psbuf(0)
        for tt in range(NT):
            def mm(e, tt=tt):
                ins = None
                for kc in range(KC):
                    ins = e.matmul(ps[0][:, tt * 16:(tt + 1) * 16], lhsT=hT[:, kc, tt * 128:(tt + 1) * 128],
                                   rhs=wg[:, kc, :], start=(kc == 0), stop=(kc == KC - 1))
                return ins
            P.add("pe", mm, reads=[hTb[tt // 4], wgb], writes=[pg])
        pgv = ps[0][:, 0:256].rearrange("p (t c) -> p t c", t=NT)
        P.add("act", lambda e: e.activation(out=betas, in_=pgv[:, :, 0:8], func=AF.Sigmoid), reads=[pg], writes=[gqb])
        P.add("dve", lambda e: e.tensor_tensor(out=xa.rearrange("p (t h) -> p t h", t=NT), in0=pgv[:, :, 8:16],
                                               in1=dtb.rearrange("p (t h) -> p t h", t=NT), op=ALU.add),
              reads=[pg, gqb], writes=[xab])
        P.add("act", lambda e: e.activation(out=xa, in_=xa, func=AF.Exp), reads=[xab], writes=[xab])
        P.add("act", lambda e: e.activation(out=xa, in_=xa, func=AF.Ln, bias=epsc[:, 1:2]), reads=[xab, epsb],
              writes=[xab])
        P.add("dve", lambda e: e.tensor_tensor(out=small[:, 272:400], in0=xa, in1=negA, op=ALU.mult),
              reads=[xab, gqb], writes=[gqb])
        P.add("dve", lambda e: e.tensor_scalar(out=small[:, 144:272], in0=small[:, 16:144], scalar1=-1.0, scalar2=None,
                                               op0=ALU.mult), reads=[gqb], writes=[gqb])
        pg2 = psbuf(1)
        for tt in range(NT):
            def mm2(e, tt=tt):
                e.matmul(ps[1][:, tt * 24:tt * 24 + 8], lhsT=cblk(C_TRI), rhs=gs[:, tt, :], start=True, stop=True)
                e.matmul(ps[1][:, tt * 24 + 8:tt * 24 + 16], lhsT=cblk(C_UG), rhs=gs[:, tt, :], start=True, stop=True)
                return e.matmul(ps[1][:, tt * 24 + 16:tt * 24 + 24], lhsT=cblk(C_ONES), rhs=gs[:, tt, :], start=True,
                                stop=True)
            P.add("pe", mm2, reads=[gqb, cfb], writes=[pg2])
        P.add("act", lambda e: e.activation(out=small[:, 528:912], in_=ps[1][:, 0:384], func=AF.Exp), reads=[pg2],
              writes=[gqb])
        P.add("dve", lambda e: e.tensor_tensor(out=bgs, in0=betas, in1=egs[:, :, 0:8], op=ALU.mult), reads=[gqb],
              writes=[gqb])
        if "gates" in dbg:
            tap("gates", small[:, 16:912], [128, 896], F32, [gqb])

        if "0" in phases:
            nheads_g = 0
        NW = 4
        wr, wrb = zip(*[carve("wr%d" % i, [KC, 128], BF16) for i in range(NW)])
        xraw, xrp = carve("xraw", [3 + S])
        xrb = subbufs(xrp, "xraw", [[(0, 12)]] + [[(12 + i * 2048, 12 + (i + 1) * 2048)] for i in range(4)])
        qT, qTp = carve("qT", [S])
        kT, kTp = carve("kT", [S])
        vT, vTp = carve("vT", [S])
        seg4 = [[(i * 2048, (i + 1) * 2048)] for i in range(4)]
        tTb = {"q": subbufs(qTp, "qT", seg4), "k": subbufs(kTp, "kT", seg4), "v": subbufs(vTp, "vT", seg4)}
        szT, szp = carve("szT", [S], BF16)
        szb = subbufs(szp, "szT", [[(i * 1024, (i + 1) * 1024)] for i in range(4)])
        sqt, sqtb = zip(*[carve("sqt%d" % i, [512]) for i in range(2)])
        rs, rsb = carve("rs", [512])
        Ssb, Sb = carve("S", [128])
        NSL = 4
        slot = []
        for i in range(NSL):
            d_ = {}
            for nm in ("Gt", "Ln", "N0", "N1", "M0", "M1", "Xs", "kbg", "vb"):
                d_[nm] = carve("%s_%d" % (nm, i), [128])
            for nm in ("iT", "kdec", "u", "wTn"):
                d_[nm] = [carve("%s_%d_%d" % (nm, i, p_), [128]) for p_ in range(2)]
            d_["EE"] = carve("EE_%d" % i, [256])
            slot.append(d_)
        vnew, vnewb = carve("vnew", [128])
        t1, t1b = carve("t1", [128])
        osb, osbb = carve("osb", [128])
        onb_, onbb = carve("on", [128])
        P.add("dve", lambda e: e.memset(xraw[:, 0:3], 0.0), writes=[xrb[0]])

        pproj = [psbuf(0), psbuf(1)]
        pl2 = psbuf(7)
        pscan = psbuf(2)
        pT = psbuf(7)

        def q4(bank, q):
            return ps[bank][:, q * 128:(q + 1) * 128]

        wcount = [0]

        def proj_fm(col0, dst_fn, pidx):
            wi = wcount[0] % NW
            wcount[0] += 1
            load_w(wr[wi], w_in, col0, 128, wrb[wi])
            for seg in range(4):
                bank = (pidx[0]) % 2
                pidx[0] += 1

                def mm(e, seg=seg, bank=bank, wi=wi):
                    ins = None
                    for kc in range(KC):
                        ins = e.matmul(ps[bank][:, :], lhsT=wr[wi][:, kc, :], rhs=hT[:, kc, seg * 512:(seg + 1) * 512],
                                       start=(kc == 0), stop=(kc == KC - 1))
                    return ins
                P.add("pe", mm, reads=[hTb[seg], wrb[wi]], writes=[pproj[bank]])
                nxt = dst_fn(seg, bank)
                while l2pend:
                    l2pend.pop(0)()
                if nxt is not None:
                    l2pend.append(nxt)

        pidx = [0]
        l2pend = []
        SCQ = 128.0 ** -0.5
        for h in range(nheads_g):
            for ti, nm in enumerate("qkv"):
                ci = ti * 8 + h
                tT = {"q": qT, "k": kT, "v": vT}[nm]

                def consume(seg, bank, nm=nm, ci=ci, tT=tT):
                    s0 = seg * 512
                    P.add("act", lambda e: e.copy(out=xraw[:, 3 + s0:3 + s0 + 512], in_=ps[bank][:, :]),
                          reads=[pproj[bank]], writes=[xrb[seg + 1]])
                    dst = tT[:, s0:s0 + 512]
                    db = tTb[nm][seg]
                    P.add("dve", lambda e: e.tensor_scalar(out=dst, in0=xraw[:, 3 + s0:3 + s0 + 512],
                                                           scalar1=cw[:, ci, 3:4], scalar2=None, op0=ALU.mult),
                          reads=[xrb[seg + 1], cwb], writes=[db])
                    for j in range(3):
                        P.add("dve", lambda e, j=j: e.scalar_tensor_tensor(out=dst, in0=xraw[:, j + s0:j + s0 + 512],
                                                                           scalar=cw[:, ci, j:j + 1], in1=dst,
                                                                           op0=ALU.mult, op1=ALU.add),
                              reads=[xrb[seg + 1], xrb[seg], cwb, db], writes=[db])
                    P.add("act", lambda e: e.activation(out=dst, in_=dst, func=AF.Silu), reads=[db], writes=[db])
                    if nm in "qk":
                        si = seg % 2
                        P.add("act", lambda e: e.activation(out=sqt[si], in_=dst, func=AF.Square), reads=[db],
                              writes=[sqtb[si]])

                        def tail():
                            P.add("pe", lambda e: e.matmul(ps[7][:, :], lhsT=cblk(C_ONES), rhs=sqt[si], start=True,
                                                           stop=True), reads=[sqtb[si], cfb], writes=[pl2])
                            P.add("act", lambda e: e.activation(out=rs, in_=ps[7][:, :], func=AF.Sqrt,
                                                                bias=epsc[:, 0:1]), reads=[pl2, epsb], writes=[rsb])
                            P.add("dve", lambda e: e.reciprocal(out=rs, in_=rs), reads=[rsb], writes=[rsb])
                            if nm == "q":
                                P.add("dve", lambda e: e.scalar_tensor_tensor(out=dst, in0=dst, scalar=SCQ, in1=rs,
                                                                              op0=ALU.mult, op1=ALU.mult),
                                      reads=[db, rsb], writes=[db])
                            else:
                                P.add("dve", lambda e: e.tensor_tensor(out=dst, in0=dst, in1=rs, op=ALU.mult),
                                      reads=[db, rsb], writes=[db])
                        return tail
                    return None
                proj_fm([OFF_GQ, OFF_GK, OFF_GV][ti] + h * 128, consume, pidx)

            def consume_z(seg, bank):
                s0 = seg * 512
                P.add("act", lambda e: e.activation(out=szT[:, s0:s0 + 512], in_=ps[bank][:, :], func=AF.Silu),
                      reads=[pproj[bank]], writes=[szb[seg]])
            proj_fm(OFF_GZ + h * 128, consume_z, pidx)
            while l2pend:
                l2pend.pop(0)()
            if "qkv" in dbg and h == 0:
                tap("qT", qT, [128, S], F32, tTb["q"])
                tap("kT", kT, [128, S], F32, tTb["k"])
                tap("vT", vT, [128, S], F32, tTb["v"])

            P.add("dve", lambda e: e.memset(Ssb, 0.0), writes=[Sb])

            def pre_stages(tt, sl, par, h=h):
                sd = dict(slot[sl])
                for nm_ in ("iT", "kdec", "u", "wTn"):
                    sd[nm_] = slot[sl][nm_][par]
                ts = slice(tt * 128, (tt + 1) * 128)
                sg = tt // 4
                bk = 3 + sl
                pb = pbank[bk]
                stages = []

                def s0():
                    def trKV(e):
                        e.matmul(q4(bk, 0), lhsT=kT[:, ts], rhs=I_, start=True, stop=True)
                        return e.matmul(q4(bk, 1), lhsT=vT[:, ts], rhs=I_, start=True, stop=True)
                    P.add("pe", trKV, reads=[tTb["k"][sg], tTb["v"][sg], cfb], writes=[pb])
                    P.add("dve", lambda e: e.tensor_scalar(out=sd["kbg"][0], in0=q4(bk, 0), scalar1=bgs[:, tt, h:h + 1],
                                                           scalar2=None, op0=ALU.mult),
                          reads=[pb, gqb], writes=[sd["kbg"][1]])
                    P.add("dve", lambda e: e.tensor_scalar(out=sd["kdec"][0], in0=q4(bk, 0),
                                                           scalar1=egs[:, tt, 8 + h:9 + h], scalar2=None, op0=ALU.mult),
                          reads=[pb, gqb], writes=[sd["kdec"][1]])
                    P.add("dve", lambda e: e.tensor_scalar(out=sd["vb"][0], in0=q4(bk, 1), scalar1=betas[:, tt, h:h + 1],
                                                           scalar2=None, op0=ALU.mult),
                          reads=[pb, gqb], writes=[sd["vb"][1]])
                    P.add("dve", lambda e: e.tensor_scalar(out=sd["Gt"][0], in0=cblk(C_TRI), scalar1=gs[:, tt, h:h + 1],
                                                           scalar2=None, op0=ALU.mult),
                          reads=[cfb, gqb], writes=[sd["Gt"][1]])
                stages.append(s0)

                def s1():
                    def mmD(e):
                        e.matmul(q4(bk, 2), lhsT=sd["Gt"][0], rhs=cblk(C_UG), start=True, stop=False)
                        e.matmul(q4(bk, 2), lhsT=I_, rhs=cblk(C_NBS), start=False, stop=True)
                        e.matmul(q4(bk, 3), lhsT=cblk(C_UG), rhs=sd["Gt"][0], start=True, stop=False)
                        e.matmul(q4(bk, 3), lhsT=I_, rhs=cblk(C_NBI), start=False, stop=True)
                        e.matmul(q4(bk, 0), lhsT=kT[:, ts], rhs=kT[:, ts], start=True, stop=True)
                        return e.matmul(q4(bk, 1), lhsT=kT[:, ts], rhs=qT[:, ts], start=True, stop=True)
                    P.add("pe", mmD, reads=[sd["Gt"][1], cfb, tTb["k"][sg], tTb["q"][sg]], writes=[pb])
                    P.add("act", lambda e: e.activation(out=sd["EE"][0], in_=ps[bk][:, 256:512], func=AF.Exp),
                          reads=[pb], writes=[sd["EE"][1]])
                    P.add("dve", lambda e: e.scalar_tensor_tensor(out=sd["Ln"][0], in0=q4(bk, 0),
                                                                  scalar=nbetas[:, tt, h:h + 1], in1=sd["EE"][0][:, 0:128],
                                                                  op0=ALU.mult, op1=ALU.mult),
                          reads=[pb, gqb, sd["EE"][1]], writes=[sd["Ln"][1]])
                    P.add("dve", lambda e: e.tensor_tensor(out=sd["iT"][0], in0=q4(bk, 1), in1=sd["EE"][0][:, 128:256],
                                                           op=ALU.mult),
                          reads=[pb, sd["EE"][1]], writes=[sd["iT"][1]])
                stages.append(s1)

                cur = {"N": (I_, cfb), "M": (I_, cfb)}
                for l in range(7):
                    def sA(l=l):
                        Ncur, Nb = cur["N"]
                        P.add("pe", lambda e: e.matmul(q4(bk, 0), lhsT=sd["Ln"][0], rhs=Ncur, start=True, stop=True),
                              reads=[sd["Ln"][1], Nb], writes=[pb])
                        P.add("dve", lambda e: e.tensor_tensor(out=sd["Xs"][0], in0=q4(bk, 0), in1=cblk(C_MU0 + l),
                                                               op=ALU.mult),
                              reads=[pb, cfb], writes=[sd["Xs"][1]])
                    stages.append(sA)

                    def sB(l=l):
                        Ncur, Nb = cur["N"]
                        Mcur, Mb = cur["M"]
                        Nn = sd["N%d" % (l % 2)]
                        Mn = sd["M%d" % (l % 2)]

                        def mmNM(e):
                            e.matmul(q4(bk, 1), lhsT=I_, rhs=Ncur, start=True, stop=False)
                            ins = e.matmul(q4(bk, 1), lhsT=Mcur, rhs=sd["Xs"][0], start=False, stop=True)
                            if l < 6:
                                ins = e.matmul(q4(bk, 2), lhsT=sd["Xs"][0], rhs=Mcur, start=True, stop=True)
                            return ins
                        P.add("pe", mmNM, reads=[Nb, Mb, sd["Xs"][1], cfb], writes=[pb])
                        P.add("act", lambda e: e.copy(out=Nn[0], in_=q4(bk, 1)), reads=[pb], writes=[Nn[1]])
                        if l < 6:
                            P.add("dve", lambda e: e.tensor_tensor(out=Mn[0], in0=q4(bk, 2), in1=Mcur, op=ALU.add),
                                  reads=[pb, Mb], writes=[Mn[1]])
                        cur["N"] = Nn
                        cur["M"] = Mn
                    stages.append(sB)

                def sF():
                    Nf, Nfb = cur["N"]

                    def mmUW(e):
                        e.matmul(q4(bk, 0), lhsT=Nf, rhs=sd["vb"][0], start=True, stop=True)
                        return e.matmul(q4(bk, 1), lhsT=sd["kbg"][0], rhs=Nf, start=True, stop=True)
                    P.add("pe", mmUW, reads=[Nfb, sd["vb"][1], sd["kbg"][1]], writes=[pb])
                    P.add("act", lambda e: e.copy(out=sd["u"][0], in_=q4(bk, 0)), reads=[pb], writes=[sd["u"][1]])
                    P.add("act", lambda e: e.mul(out=sd["wTn"][0], in_=q4(bk, 1), mul=-1.0), reads=[pb],
                          writes=[sd["wTn"][1]])
                stages.append(sF)
                return stages

            def scan(tt, sl, par, h=h):
                sd = dict(slot[sl])
                for nm_ in ("iT", "kdec", "u", "wTn"):
                    sd[nm_] = slot[sl][nm_][par]
                ts = slice(tt * 128, (tt + 1) * 128)
                sg = tt // 4
                subs = []

                def subA():
                    P.add("pe", mmV, reads=[sd["u"][1], sd["wTn"][1], Sb, cfb], writes=[pscan])
                    P.add("act", lambda e: e.copy(out=vnew, in_=q4(2, 0)), reads=[pscan], writes=[vnewb])

                def subB():
                    P.add("pe", mmO, reads=[tTb["q"][sg], Sb, sd["iT"][1], sd["kdec"][1], vnewb], writes=[pscan])
                    P.add("act", lambda e: e.mul(out=t1, in_=q4(2, 1), mul=egs[:, tt, h:h + 1]), reads=[pscan, gqb],
                          writes=[t1b])
                    P.add("dve", lambda e: e.tensor_tensor(out=osb, in0=t1, in1=q4(2, 2), op=ALU.add),
                          reads=[t1b, pscan], writes=[osbb])
                    P.add("dve", lambda e: e.scalar_tensor_tensor(out=Ssb, in0=Ssb, scalar=egs[:, tt, 16 + h:17 + h],
                                                                  in1=q4(2, 3), op0=ALU.mult, op1=ALU.add),
                          reads=[Sb, gqb, pscan], writes=[Sb])

                def subC():
                    P.add("act", lambda e: e.activation(out=t1, in_=osb, func=AF.Square, accum_out=st[:, 4:5]),
                          reads=[osbb], writes=[t1b, stb])
                    P.add("act", lambda e: e.activation(out=st[:, 5:6], in_=st[:, 4:5], func=AF.Sqrt, bias=epsc[:, 0:1],
                                                        scale=1.0 / 128), reads=[stb, epsb], writes=[stb])
                    P.add("dve", lambda e: e.reciprocal(out=st[:, 6:7], in_=st[:, 5:6]), reads=[stb], writes=[stb])
                    P.add("act", lambda e: e.mul(out=onb_, in_=osb, mul=st[:, 6:7]), reads=[osbb, stb], writes=[onbb])

                def subD():
                    P.add("pe", lambda e: e.matmul(q4(7, 0), lhsT=onb_, rhs=I_, start=True, stop=True),
                          reads=[onbb, cfb], writes=[pT])
                    P.add("dve", lambda e: e.scalar_tensor_tensor(out=oaT[:, h, ts], in0=q4(7, 0), scalar=gnw,
                                                                  in1=szT[:, ts], op0=ALU.mult, op1=ALU.mult),
                          reads=[pT, gqb, szb[sg]], writes=[oaTb[sg]])

                def mmV(e):
                    e.matmul(q4(2, 0), lhsT=I_, rhs=sd["u"][0], start=True, stop=False)
                    return e.matmul(q4(2, 0), lhsT=sd["wTn"][0], rhs=Ssb, start=False, stop=True)

                def mmO(e):
                    e.matmul(q4(2, 1), lhsT=qT[:, ts], rhs=Ssb, start=True, stop=True)
                    e.matmul(q4(2, 2), lhsT=sd["iT"][0], rhs=vnew, start=True, stop=True)
                    return e.matmul(q4(2, 3), lhsT=sd["kdec"][0], rhs=vnew, start=True, stop=True)
                return [subA, subB, subC, subD]

            pend = []
            for gi, g0 in enumerate(range(0, NT, NSL)):
                sts = [pre_stages(g0 + i, i, gi % 2) for i in range(NSL)]
                for k in range(len(sts[0])):
                    for i in range(NSL):
                        sts[i][k]()
                    if pend:
                        pend.pop(0)()
                while pend:
                    pend.pop(0)()
                for i in range(NSL):
                    pend.extend(scan(g0 + i, i, gi % 2))
            while pend:
                pend.pop(0)()
        print("phase B arena bytes", apos[0])
        if "oaT" in dbg:
            tap("oaT", oaT[:], [128, H, S], BF16, oaTb)
        if "scan" in dbg:
            tap("osb", osb, [128, 128], F32, [osbb])
            tap("S", Ssb, [128, 128], F32, [Sb])
            tap("vnew", vnew, [128, 128], F32, [vnewb])
        apos[0] = bmark

    obT, obTp = carve("obT", [H, S], BF16)
    obTb = subbufs(obTp, "obT", [[(hh * 4096 + q * 1024, hh * 4096 + (q + 1) * 1024) for hh in range(H)] for q in range(4)])
    cmark = apos[0]
    if "C" in phases:
        NWC = 4
        cwr, cwrb = zip(*[carve("cwr%d" % i, [KC, 128], BF16) for i in range(NWC)])
        cmf, cmfb = carve("cmf", [1280])
        cmb, cmbb = carve("cmb", [1536], BF16)
        dma("sp", cmf, cm_d, cmfb)
        if "7" not in phases:
            P.add("dve", lambda e: e.tensor_copy(out=cmb[:, 0:1280], in_=cmf), reads=[cmfb], writes=[cmbb])
        P.add("dve", lambda e: e.tensor_copy(out=cmb[:, 1280:1408], in_=cblk(C_ID)), reads=[cfb], writes=[cmbb])
        P.add("dve", lambda e: e.tensor_copy(out=cmb[:, 1408:1536], in_=cblk(C_ONES)), reads=[cfb], writes=[cmbb])
        cbias = cmb[:, 0:256]
        Ib = cmb[:, 1280:1408]
        onesb = cmb[:, 1408:1536]
        qTb_, qTbb = carve("mqTb", [S], BF16)
        kTb_, kTbb = carve("mkTb", [S], BF16)
        qTf, qTfb = carve("mqTf", [S])
        Vt, Vtb = carve("mV", [NT, 128], BF16)
        mszT, mszb = carve("mszT", [S], BF16)
        nbT, nbTb = carve("nbT", [S], BF16)
        kmT, kmTb = carve("kmT", [8])
        gm, gmb = carve("gm", [NT, 8])
        nb, nbb = carve("nb", [NT, 8])
        top8, top8b = carve("top8", [8])
        pTr, pTrb = zip(*[carve("pT%d" % i, [256], BF16) for i in range(4)])
        rr, rrb = carve("rr", [256])
        zz, zzb = carve("zz", [256])
        cwcount = [0]

        def proj_c(col0, consume):
            wi = cwcount[0] % NWC
            cwcount[0] += 1
            load_w(cwr[wi], w_in, col0, 128, cwrb[wi])
            for seg in range(4):
                bank = seg % 2

                def mm(e, seg=seg, bank=bank, wi=wi):
                    ins = None
                    for kc in range(KC):
                        ins = e.matmul(ps[bank][:, :], lhsT=cwr[wi][:, kc, :], rhs=hT[:, kc, seg * 512:(seg + 1) * 512],
                                       start=(kc == 0), stop=(kc == KC - 1))
                    return ins
                P.add("pe", mm, reads=[hTb[seg], cwrb[wi]], writes=[pbank[bank]])
                consume(seg, bank)

        for h in range(nheads_m):
            def cons_q(seg, bank):
                s0 = seg * 512
                P.add("act", lambda e: e.mul(out=qTf[:, s0:s0 + 512], in_=ps[bank][:, :], mul=SCQ_M),
                      reads=[pbank[bank]], writes=[qTfb])
                P.add("dve", lambda e: e.tensor_copy(out=qTb_[:, s0:s0 + 512], in_=qTf[:, s0:s0 + 512]),
                      reads=[qTfb], writes=[qTbb])
            SCQ_M = 128.0 ** -0.5
            proj_c(OFF_MQ + h * 128, cons_q)

            def cons_k(seg, bank):
                s0 = seg * 512
                P.add("act", lambda e: e.copy(out=kTb_[:, s0:s0 + 512], in_=ps[bank][:, :]),
                      reads=[pbank[bank]], writes=[kTbb])
                P.add("dve", lambda e: e.tensor_reduce(out=kmT[:, seg * 2:seg * 2 + 2],
                                                       in_=ps[bank][:, :].rearrange("p (a b) -> p a b", a=2),
                                                       axis=AX.X, op=ALU.add),
                      reads=[pbank[bank]], writes=[kmTb])
            proj_c(OFF_MK + h * 128, cons_k)

            def cons_z(seg, bank):
                s0 = seg * 512
                P.add("act", lambda e: e.activation(out=mszT[:, s0:s0 + 512], in_=ps[bank][:, :], func=AF.Silu),
                      reads=[pbank[bank]], writes=[mszb])
            proj_c(OFF_MZ + h * 128, cons_z)

            wi = cwcount[0] % NWC
            cwcount[0] += 1
            load_w(cwr[wi], w_in, OFF_MV + h * 128, 128, cwrb[wi])
            for g in range(4):
                bank = 2 + g % 2

                def mmv(e, g=g, bank=bank, wi=wi):
                    ins = None
                    for j in range(4):
                        tt = g * 4 + j
                        for kc in range(KC):
                            ins = e.matmul(ps[bank][:, j * 128:(j + 1) * 128], lhsT=hT[:, kc, tt * 128:(tt + 1) * 128],
                                           rhs=cwr[wi][:, kc, :], start=(kc == 0), stop=(kc == KC - 1))
                    return ins
                P.add("pe", mmv, reads=[hTb[g], cwrb[wi]], writes=[pbank[bank]])
                P.add("act", lambda e, g=g, bank=bank: e.copy(out=Vt[:, g * 4:(g + 1) * 4, :],
                                                              in_=ps[bank][:, :].rearrange("p (a b) -> p a b", a=4)),
                      reads=[pbank[bank]], writes=[Vtb])

            P.add("dve", lambda e: e.memset(gm, -1e30), writes=[gmb])
            P.add("dve", lambda e: e.memset(nb, 0.0), writes=[nbb])

            def mmg(e):
                ins = None
                for tt in range(8, NT):
                    ins = e.matmul(ps[4][:, tt * 8:tt * 8 + 8], lhsT=qTf[:, tt * 128:(tt + 1) * 128], rhs=kmT,
                                   start=True, stop=True)
                return ins
            P.add("pe", mmg, reads=[qTfb, kmTb], writes=[pbank[4]])
            for tt in range(8, NT):
                qb = tt // 2
                P.add("dve", lambda e, tt=tt, qb=qb: e.tensor_copy(out=gm[:, tt, 0:qb], in_=ps[4][:, tt * 8:tt * 8 + qb]),
                      reads=[pbank[4]], writes=[gmb])
                P.add("dve", lambda e, tt=tt: e.max(out=top8, in_=gm[:, tt, :]), reads=[gmb], writes=[top8b])
                P.add("dve", lambda e, tt=tt: e.tensor_scalar(out=nb[:, tt, :], in0=gm[:, tt, :], scalar1=top8[:, 2:3],
                                                              scalar2=NEG, op0=ALU.is_lt, op1=ALU.mult),
                      reads=[gmb, top8b], writes=[nbb])
            for g in range(4):
                bank = 5 + g % 2

                def mmt(e, g=g, bank=bank):
                    ins = None
                    for j in range(4):
                        tt = g * 4 + j
                        ins = e.matmul(ps[bank][0:8, j * 128:(j + 1) * 128], lhsT=nb[:, tt, :], rhs=I_, start=True,
                                       stop=True)
                    return ins
                P.add("pe", mmt, reads=[nbb, cfb], writes=[pbank[bank]])
                P.add("act", lambda e, g=g, bank=bank: e.copy(out=nbT[0:8, g * 512:(g + 1) * 512], in_=ps[bank][0:8, :]),
                      reads=[pbank[bank]], writes=[nbTb])
            if "moba_pre" in dbg and h == 0:
                tap("nbT", nbT[0:8, :], [8, S], BF16, [nbTb])
                tap("mqT", qTf, [128, S], F32, [qTfb])
                tap("mV", Vt, [128, NT, 128], BF16, [Vtb])

            for qb in range(8):
                q0 = qb * 256
                nk = 2 * qb + 2
                ob_bank = 6 + qb % 2
                pob = pbank[ob_bank]

                def emit_S(kt, qb=qb, q0=q0):
                    n = kt // 2
                    sbank = kt % 4
                    if kt == 2 * qb + 1:
                        def mm(e):
                            e.matmul(ps[sbank][:, 128:256], lhsT=kTb_[:, kt * 128:(kt + 1) * 128],
                                     rhs=qTb_[:, q0 + 128:q0 + 256], start=True, stop=False)
                            return e.matmul(ps[sbank][:, 128:256], lhsT=Ib, rhs=cbias[:, 0:128], start=False, stop=True)
                    elif kt == 2 * qb:
                        def mm(e):
                            e.matmul(ps[sbank][:, 0:256], lhsT=kTb_[:, kt * 128:(kt + 1) * 128], rhs=qTb_[:, q0:q0 + 256],
                                     start=True, stop=False)
                            return e.matmul(ps[sbank][:, 0:256], lhsT=Ib, rhs=cbias, start=False, stop=True)
                    else:
                        def mm(e):
                            e.matmul(ps[sbank][:, 0:256], lhsT=kTb_[:, kt * 128:(kt + 1) * 128], rhs=qTb_[:, q0:q0 + 256],
                                     start=True, stop=False)
                            return e.matmul(ps[sbank][:, 0:256], lhsT=cmb[0:8, 256 + n * 128:256 + (n + 1) * 128],
                                            rhs=nbT[0:8, q0:q0 + 256], start=False, stop=True)
                    P.add("pe", mm, reads=[kTbb, qTbb, cmbb, nbTb], writes=[pbank[sbank]])

                def emit_PV(kt, qb=qb, q0=q0, nk=nk, ob_bank=ob_bank, pob=pob):
                    sbank = kt % 4
                    c0 = 128 if kt == 2 * qb + 1 else 0
                    pt = pTr[kt % 4]
                    P.add("act", lambda e: e.activation(out=pt[:, c0:256], in_=ps[sbank][:, c0:256], func=AF.Exp),
                          reads=[pbank[sbank]], writes=[pTrb[kt % 4]])

                    def mm(e):
                        e.matmul(ps[ob_bank][:, c0:256], lhsT=Vt[:, kt, :], rhs=pt[:, c0:256], start=(kt == 0),
                                 stop=(kt == nk - 1))
                        return e.matmul(ps[ob_bank][:, 256 + c0:512], lhsT=onesb, rhs=pt[:, c0:256], start=False,
                                        stop=(kt == nk - 1), skip_group_check=True)
                    P.add("pe", mm, reads=[Vtb, pTrb[kt % 4], cmbb], writes=[pob])

                emit_S(0)
                emit_S(1)
                for kt in range(nk):
                    emit_PV(kt)
                    if kt + 2 < nk:
                        emit_S(kt + 2)
                P.add("dve", lambda e, ob_bank=ob_bank: e.reciprocal(out=rr, in_=ps[ob_bank][:, 256:512]),
                      reads=[pob], writes=[rrb])
                P.add("dve", lambda e, q0=q0: e.tensor_tensor(out=zz, in0=rr, in1=mszT[:, q0:q0 + 256], op=ALU.mult),
                      reads=[rrb, mszb], writes=[zzb])
                P.add("dve", lambda e, q0=q0, ob_bank=ob_bank, h=h: e.tensor_tensor(out=obT[:, h, q0:q0 + 256],
                                                                                   in0=ps[ob_bank][:, 0:256], in1=zz,
                                                                                   op=ALU.mult),
                      reads=[pob, zzb], writes=[obTb[qb // 2]])
        if "obT" in dbg:
            tap("obT", obT, [128, H, S], BF16, obTb)
    apos[0] = cmark

    if "D" in phases:
        dmark = apos[0]
        mT1, mT1p = carve("mT1", [KC, 1024], BF16)
        mT1b = subbufs(mT1p, "mT1", [[(c * 2048 + q * 1024, c * 2048 + (q + 1) * 1024) for c in range(KC)] for q in range(2)])
        ymark = apos[0]
        NWD = 2
        wga, wgab = zip(*[carve("wga%d" % i, [KC, 128], BF16) for i in range(NWD)])
        wgb_, wgbb = zip(*[carve("wgb%d" % i, [KC, 128], BF16) for i in range(NWD)])
        wba, wbab = zip(*[carve("wba%d" % i, [8, 128], BF16) for i in range(NWD)])
        wbb, wbbb = zip(*[carve("wbb%d" % i, [8, 128], BF16) for i in range(NWD)])
        sga, sgab = zip(*[carve("sga%d" % i, [512]) for i in range(2)])
        sgb, sgbb = zip(*[carve("sgb%d" % i, [512]) for i in range(2)])
        it = 0
        for hf in range(2):
            for c in range(KC):
                wi = it % NWD
                load_w(wga[wi], w_in, OFF_GATE_A + c * 128, 128, wgab[wi])
                load_w(wgb_[wi], w_in, OFF_GATE_B + c * 128, 128, wgbb[wi])
                load_w(wba[wi], w_a, c * 128, 128, wbab[wi])
                load_w(wbb[wi], w_b, c * 128, 128, wbbb[wi])
                for seg in range(2):
                    t0 = hf * 1024 + seg * 512
                    hq = t0 // 512
                    bs = (it * 2 + seg) % 2 * 4
                    si = seg

                    def mmg(e, w, bank, t0=t0):
                        ins = None
                        for kc in range(KC):
                            ins = e.matmul(ps[bank][:, :], lhsT=w[:, kc, :], rhs=hT[:, kc, t0:t0 + 512], start=(kc == 0),
                                           stop=(kc == KC - 1))
                        return ins

                    def mmb(e, w, src, bank, t0=t0):
                        ins = None
                        for kc in range(8):
                            ins = e.matmul(ps[bank][:, :], lhsT=w[:, kc, :], rhs=src[:, kc, t0:t0 + 512], start=(kc == 0),
                                           stop=(kc == 7))
                        return ins
                    P.add("pe", lambda e, wi=wi, bs=bs, mmg=mmg: mmg(e, wga[wi], bs), reads=[hTb[hq], wgab[wi]],
                          writes=[pbank[bs]])
                    P.add("act", lambda e, bs=bs, si=si: e.activation(out=sga[si], in_=ps[bs][:, :], func=AF.Sigmoid),
                          reads=[pbank[bs]], writes=[sgab[si]])
                    P.add("pe", lambda e, wi=wi, bs=bs, mmg=mmg: mmg(e, wgb_[wi], bs + 1), reads=[hTb[hq], wgbb[wi]],
                          writes=[pbank[bs + 1]])
                    P.add("act", lambda e, bs=bs, si=si: e.activation(out=sgb[si], in_=ps[bs + 1][:, :], func=AF.Sigmoid),
                          reads=[pbank[bs + 1]], writes=[sgbb[si]])
                    P.add("pe", lambda e, wi=wi, bs=bs, mmb=mmb: mmb(e, wba[wi], oaT, bs + 2), reads=[oaTb[hq], wbab[wi]],
                          writes=[pbank[bs + 2]])
                    P.add("dve", lambda e, bs=bs, si=si: e.tensor_tensor(out=sga[si], in0=ps[bs + 2][:, :], in1=sga[si],
                                                                         op=ALU.mult),
                          reads=[pbank[bs + 2], sgab[si]], writes=[sgab[si]])
                    P.add("pe", lambda e, wi=wi, bs=bs, mmb=mmb: mmb(e, wbb[wi], obT, bs + 3), reads=[obTb[hq], wbbb[wi]],
                          writes=[pbank[bs + 3]])
                    P.add("dve", lambda e, bs=bs, si=si: e.tensor_tensor(out=sgb[si], in0=ps[bs + 3][:, :], in1=sgb[si],
                                                                         op=ALU.mult),
                          reads=[pbank[bs + 3], sgbb[si]], writes=[sgbb[si]])
                    if hf == 0:
                        dst = mT1[:, c, seg * 512:(seg + 1) * 512]
                        dbuf = mT1b[seg]
                    else:
                        dst = hT[:, c, seg * 512:(seg + 1) * 512]
                        dbuf = hTb[seg]
                    P.add("dve", lambda e, si=si, dst=dst: e.tensor_tensor(out=dst, in0=sga[si], in1=sgb[si], op=ALU.add),
                          reads=[sgab[si], sgbb[si]], writes=[dbuf])
                it += 1
        if "mT" in dbg:
            tap("mT1", mT1, [128, KC, 1024], BF16, mT1b)
            tap("mT2", hT[:, :, 0:1024], [128, KC, 1024], BF16, hTb[0:2])
        apos[0] = ymark
        postw, postwb = carve("postw", [D])
        xn_junk, xnjb = carve("xnj", [D], BF16)
        xt, xtb = zip(*[carve("xt%d" % i, [D]) for i in range(2)])
        ysb, ysbb = carve("ysb", [D])
        dma("sp", postw, post_w.partition_broadcast(128), postwb)
        for q in range(4):
            dma("pool", oaT[:, :, q * 512:(q + 1) * 512],
                w_out[0:1024, q * 512:(q + 1) * 512].rearrange("(kc p) c -> p kc c", p=128), oaTb[q])
            dma("pool", obT[:, :, q * 512:(q + 1) * 512],
                w_out[1024:2048, q * 512:(q + 1) * 512].rearrange("(kc p) c -> p kc c", p=128), obTb[q])
        for tt in range(NT):
            sl = tt % 2
            dma("sp", xt[sl], x[tt * 128:(tt + 1) * 128, :], xtb[sl])
            for ct in range(4):
                bank = (tt % 2) * 4 + ct

                def mmy(e, tt=tt, ct=ct, bank=bank):
                    ins = None
                    for kc in range(KC):
                        if tt < 8:
                            lt = mT1[:, kc, tt * 128:(tt + 1) * 128]
                        else:
                            lt = hT[:, kc, (tt - 8) * 128:(tt - 7) * 128]
                        wsrc = oaT if kc < 8 else obT
                        ins = e.matmul(ps[bank][:, :], lhsT=lt, rhs=wsrc[:, kc % 8, ct * 512:(ct + 1) * 512],
                                       start=(kc == 0), stop=(kc == KC - 1))
                    return ins
                mb = mT1b[tt // 4] if tt < 8 else hTb[(tt - 8) // 4]
                P.add("pe", mmy, reads=[mb, oaTb[ct], obTb[ct]], writes=[pbank[bank]])
                P.add("act", lambda e, ct=ct, bank=bank: e.copy(out=ysb[:, ct * 512:(ct + 1) * 512], in_=ps[bank][:, :]),
                      reads=[pbank[bank]], writes=[ysbb])
            P.add("act", lambda e, sl=sl: e.activation(out=xn_junk, in_=ysb, func=AF.Square, accum_out=st[:, 0:1]),
                  reads=[ysbb], writes=[xnjb, stb])
            P.add("act", lambda e: e.activation(out=st[:, 1:2], in_=st[:, 0:1], func=AF.Sqrt, bias=epsc[:, 0:1],
                                                scale=1.0 / D), reads=[stb, epsb], writes=[stb])
            P.add("dve", lambda e: e.reciprocal(out=st[:, 2:3], in_=st[:, 1:2]), reads=[stb], writes=[stb])
            P.add("dve", lambda e: e.scalar_tensor_tensor(out=ysb, in0=ysb, scalar=st[:, 2:3], in1=postw, op0=ALU.mult,
                                                          op1=ALU.mult), reads=[ysbb, stb, postwb], writes=[ysbb])
            P.add("dve", lambda e, sl=sl: e.tensor_tensor(out=xt[sl], in0=xt[sl], in1=ysb, op=ALU.add),
                  reads=[xtb[sl], ysbb], writes=[xtb[sl]])
            dma("sp", out[tt * 128:(tt + 1) * 128, :], xt[sl], outb, reads=[xtb[sl]], waw=False)
        apos[0] = dmark

    P.add("sp", lambda e: e.nop(), reads=[outb])
    P.finalize()
    return nc, dbg_outs


def _in_maps(x, pre_norm_w, w_in, conv_w, a_log, dt_bias, gdn_norm_w, w_branch_a, w_branch_b, w_out, post_norm_w):
    cf, cm = make_consts()
    f = lambda a: np.ascontiguousarray(np.asarray(a, dtype=np.float32))
    shared = {
        "pre_w": f(pre_norm_w[0][None, :]),
        "post_w": f(post_norm_w[0][None, :]),
        "w_in": f(w_in[0]),
        "conv_wT": f(np.asarray(conv_w[0]).T),
        "a_log16": f(np.tile(np.asarray(a_log[0]), 16)[None, :]),
        "dt_bias16": f(np.tile(np.asarray(dt_bias[0]), 16)[None, :]),
        "gnw": f(np.asarray(gdn_norm_w[0])[:, None]),
        "w_a": f(w_branch_a[0]),
        "w_b": f(w_branch_b[0]),
        "w_out": f(w_out[0]),
        "cf": cf,
        "cm": cm,
    }
    return [dict(shared, x=f(x[b])) for b in range(x.shape[0])]


def kernel(**inputs):
    maps = _in_maps(**inputs)
    import os
    nc, _ = build(phases=os.environ.get("KPHASES", "ABCD"))
    res = run_bass_kernel_spmd(nc, maps, core_ids=list(range(len(maps))))
    return np.stack([np.asarray(r["out"], dtype=np.float32) for r in res.results], axis=0)
```

```python
from contextlib import ExitStack

import numpy as np
import concourse.bass as bass
import concourse.mybir as mybir
from concourse.bass_utils import run_bass_kernel_spmd

F32 = mybir.dt.float32
BF16 = mybir.dt.bfloat16
AF = mybir.ActivationFunctionType
ALU = mybir.AluOpType
AX = mybir.AxisListType

S = 2048
D = 2048
NT = S // 128
KC = D // 128
H = 8
IN_W = 12304
EPS = 1e-6
NEG = -30000.0
STAGE_LIMIT = 1000
PRE_TAPS = ("u", "wTn", "iT", "Ln", "kdec", "N0")

OFF_GQ, OFF_GK, OFF_GV = 0, 1024, 2048
OFF_GZ = 3072
OFF_GB = 4096
OFF_GA = 4104
OFF_MQ, OFF_MK, OFF_MV = 4112, 4112 + 1024, 4112 + 2048
OFF_MZ = 4112 + 3072
OFF_GATE_A = 8208
OFF_GATE_B = 8208 + 2048

C_ID, C_TRI, C_UG, C_ONES, C_NBS, C_NBI, C_MU0 = 0, 1, 2, 3, 4, 5, 6
NCB = 13


def make_consts():
    r = np.arange(128)[:, None]
    c = np.arange(128)[None, :]
    blocks = [None] * NCB
    blocks[C_ID] = (r == c)
    blocks[C_TRI] = (r <= c)
    blocks[C_UG] = (r > c)
    blocks[C_ONES] = np.ones((128, 128), bool)
    nbs = np.where(r > c, 0.0, NEG)
    nbi = np.where(r <= c, 0.0, NEG)
    out = []
    for k in range(NCB):
        if k == C_NBS:
            out.append(nbs.astype(np.float32))
        elif k == C_NBI:
            out.append(nbi.astype(np.float32))
        elif k >= C_MU0:
            l = k - C_MU0
            b = 1 << l
            m = ((r // (2 * b)) == (c // (2 * b))) & ((r % (2 * b)) < b) & ((c % (2 * b)) >= b)
            out.append(m.astype(np.float32))
        else:
            out.append(blocks[k].astype(np.float32))
    cf = np.concatenate(out, axis=1)
    cb = np.zeros((128, 256), np.float32)
    cb[:, :128] = nbi
    es = np.zeros((128, 8 * 128), np.float32)
    for n in range(8):
        es[n, n * 128:(n + 1) * 128] = 1.0
    return np.ascontiguousarray(cf), np.ascontiguousarray(np.concatenate([cb, es], axis=1))


class Buf:
    __slots__ = ("name", "last_w", "readers", "sem", "dcnt", "iv", "ov", "excl")
    REG = []

    def __init__(self, name, iv=None, excl=False):
        self.excl = excl
        self.name = name
        self.last_w = None
        self.readers = []
        self.sem = None
        self.dcnt = 0
        if iv is not None and not isinstance(iv, list):
            iv = [iv]
        self.iv = iv
        self.ov = [self]
        if iv is not None:
            for o in Buf.REG:
                if any(a[0] == b[0] and a[1] < b[2] and b[1] < a[2] for a in iv for b in o.iv):
                    o.ov.append(self)
                    self.ov.append(o)
            Buf.REG.append(self)


class Op:
    __slots__ = ("eng", "fn", "deps", "ndma", "sig", "sigval", "wbuf")


class Prog:
    ENGS = ("pe", "act", "dve", "pool", "sp")

    def __init__(self, nc, stack):
        self.nc = nc
        self.stack = stack
        self.ops = []
        self.dma_bufs = []

    def add(self, eng, fn, reads=(), writes=(), ndma=0, waw=True):
        idx = len(self.ops)
        deps = set()
        for b0 in reads:
            for b in b0.ov:
                if b.last_w is not None:
                    deps.add(b.last_w)
                if b.excl:
                    for r in b.readers:
                        if self.ops[r].eng != eng:
                            deps.add(r)
        for b0 in writes:
            for b in b0.ov:
                if b.last_w is not None and (waw or b is not b0):
                    deps.add(b.last_w)
                deps.update(b.readers)
        deps.discard(idx)
        for b in reads:
            b.readers.append(idx)
        for b in writes:
            b.last_w = idx
            b.readers = []
        op = Op()
        op.eng = eng
        op.fn = fn
        op.deps = deps
        op.ndma = ndma
        op.sig = False
        op.sigval = 0
        op.wbuf = None
        if ndma:
            assert len(writes) == 1
            op.wbuf = writes[0]
            if op.wbuf.sem is None:
                op.wbuf.sem = True
                self.dma_bufs.append(op.wbuf)
        self.ops.append(op)
        return idx

    def finalize(self):
        nc = self.nc
        ops = self.ops
        for op in ops:
            for d in op.deps:
                p = ops[d]
                if p.eng == "pe" and op.eng == "pe" and not p.ndma:
                    continue
                p.sig = True
        esem = {e: self.stack.enter_context(nc.semaphore("sem_" + e)) for e in self.ENGS}
        for b in self.dma_bufs:
            b.sem = self.stack.enter_context(nc.semaphore("dsem_" + b.name))
        cnt = {e: 0 for e in self.ENGS}
        for op in ops:
            if op.ndma:
                op.wbuf.dcnt += 16 * op.ndma
                op.sigval = op.wbuf.dcnt
            elif op.sig:
                cnt[op.eng] += 1
                op.sigval = cnt[op.eng]

        def emit(ename, e):
            waited = {}
            for op in ops:
                if op.eng != ename:
                    continue
                need = {}
                for d in op.deps:
                    p = ops[d]
                    if p.ndma:
                        sem = p.wbuf.sem
                    else:
                        if p.eng == "pe" and ename == "pe":
                            continue
                        sem = esem[p.eng]
                    k = id(sem)
                    if k not in need or need[k][1] < p.sigval:
                        need[k] = (sem, p.sigval)
                for k, (sem, v) in need.items():
                    if waited.get(k, 0) < v:
                        e.wait_ge(sem, v)
                        waited[k] = v
                ins = op.fn(e)
                if op.ndma:
                    pass
                elif op.sig:
                    ins.then_inc(esem[ename], 1)

        with nc.Block() as block:
            @block.tensor
            def _(e):
                emit("pe", e)

            @block.scalar
            def _(e):
                emit("act", e)

            @block.vector
            def _(e):
                emit("dve", e)

            @block.gpsimd
            def _(e):
                emit("pool", e)

            @block.sync
            def _(e):
                emit("sp", e)


def build(dbg=(), nheads_g=H, nheads_m=H, phases="ABCD"):
    Buf.REG = []
    nc = bass.Bass("TRN2", target_bir_lowering=False)
    stack = ExitStack()
    P = Prog(nc, stack)

    def dram(name, shape, dt=F32, kind="ExternalInput"):
        return nc.dram_tensor(name, list(shape), dt, kind=kind).ap()

    x = dram("x", [S, D])
    pre_w = dram("pre_w", [1, D])
    post_w = dram("post_w", [1, D])
    w_in = dram("w_in", [D, IN_W])
    conv_wT = dram("conv_wT", [3072, 4])
    a_log = dram("a_log16", [1, 128])
    dt_bias = dram("dt_bias16", [1, 128])
    gnw_d = dram("gnw", [128, 1])
    w_a = dram("w_a", [1024, D])
    w_b = dram("w_b", [1024, D])
    w_out = dram("w_out", [D, D])
    cf_d = dram("cf", [128, NCB * 128])
    cm_d = dram("cm", [128, 256 + 1024])
    out = dram("out", [S, D], kind="ExternalOutput")

    def sb(name, shape, dt=F32):
        return stack.enter_context(nc.sbuf_tensor(name, list(shape), dt))

    hT = sb("hT", [128, KC, S], BF16)
    oaT = sb("oaT", [128, H, S], BF16)
    cf = sb("cf_sb", [128, NCB * 128], F32)
    ARENA_BYTES = 100 * 1024
    arena = sb("arena", [128, ARENA_BYTES // 4], F32)
    ps = [stack.enter_context(nc.psum_tensor("ps%d" % i, [128, 512], F32)) for i in range(8)]
    hTb = [Buf("hT%d" % i) for i in range(4)]
    oaTb = [Buf("oaT%d" % i) for i in range(4)]
    cfb = Buf("cf")

    pbank = [Buf("psb%d" % i, excl=True) for i in range(8)]

    def psbuf(bank, c0=0, c1=512):
        return pbank[bank]

    apos = [0]

    def carve(name, free_shape, dt=F32):
        esz = 4 if dt == F32 else 2
        n = int(np.prod(free_shape))
        nb = (n * esz + 31) // 32 * 32
        off = apos[0]
        apos[0] += nb
        assert apos[0] <= ARENA_BYTES, (name, apos[0])
        ap = arena[:, off // 4:(off + nb) // 4]
        if dt != F32:
            ap = ap.bitcast(dt)
        ap = ap[:, 0:n]
        if len(free_shape) == 2:
            ap = ap.rearrange("p (a b) -> p a b", a=free_shape[0])
        elif len(free_shape) == 3:
            ap = ap.rearrange("p (a b c) -> p a b c", a=free_shape[0], b=free_shape[1])
        return ap, Buf(name, iv=("arena", off, off + nb))

    def subbufs(parent, name, pieces):
        base = parent.iv[0][1]
        return [Buf("%s_%d" % (name, i), iv=[("arena", base + lo, base + hi) for lo, hi in pc])
                for i, pc in enumerate(pieces)]

    def cblk(k):
        return cf[:, k * 128:(k + 1) * 128]

    def dma(eng, out_ap, in_ap, wbuf, reads=(), waw=True):
        def fn(e):
            return e.dma_start(out=out_ap, in_=in_ap).then_inc(wbuf.sem, 16)
        P.add(eng, fn, reads=reads, writes=[wbuf], ndma=1, waw=waw)

    def load_w(dst_ap, wdram, c0, ncols, wbuf):
        src = wdram[:, c0:c0 + ncols].rearrange("(kc p) c -> p kc c", p=128)
        dma("pool", dst_ap, src, wbuf)

    outb = Buf("outdram")
    outsl = []
    dbg_outs = {}

    def tap(name, ap, shape, dt, rbuf):
        d = dram("dbg_" + name, shape, dt, kind="ExternalOutput")
        dbg_outs[name] = d
        dma("sp", d, ap, outb, reads=rbuf, waw=False)

    dma("sp", cf[:], cf_d, cfb)
    small = sb("small", [128, 1280], F32)
    epsc = small[:, 0:2]
    epsb = Buf("epsc")
    P.add("dve", lambda e: e.memset(small[:, 0:1], EPS), writes=[epsb])
    P.add("dve", lambda e: e.memset(small[:, 1:2], 1.0), writes=[epsb])
    st = small[:, 8:16]
    stb = Buf("st")
    I_ = cblk(C_ID)

    amark = apos[0]
    xs, xsb = zip(*[carve("xs%d" % i, [D]) for i in range(2)])
    xn, xnb = carve("xn", [D])
    prew, prewb = carve("prew", [D])
    dma("sp", prew, pre_w.partition_broadcast(128), prewb)
    for tt in range(NT):
        sl = tt % 2
        dma("sp", xs[sl], x[tt * 128:(tt + 1) * 128, :], xsb[sl])
        P.add("act", lambda e, sl=sl: e.activation(out=xn, in_=xs[sl], func=AF.Square, accum_out=st[:, 0:1]),
              reads=[xsb[sl]], writes=[xnb, stb])
        P.add("act", lambda e: e.activation(out=st[:, 1:2], in_=st[:, 0:1], func=AF.Sqrt, bias=epsc[:, 0:1],
                                            scale=1.0 / D), reads=[stb, epsb], writes=[stb])
        P.add("dve", lambda e: e.reciprocal(out=st[:, 2:3], in_=st[:, 1:2]), reads=[stb], writes=[stb])
        P.add("dve", lambda e, sl=sl: e.scalar_tensor_tensor(out=xn, in0=xs[sl], scalar=st[:, 2:3], in1=prew,
                                                             op0=ALU.mult, op1=ALU.mult),
              reads=[xsb[sl], stb, prewb], writes=[xnb])
        for g in range(4):
            bank = (tt % 2) * 4 + g
            pb = psbuf(bank)

            def tr(e, g=g, bank=bank):
                ins = None
                for j in range(4):
                    kc = g * 4 + j
                    ins = e.transpose(out=ps[bank][:, j * 128:(j + 1) * 128], in_=xn[:, kc * 128:(kc + 1) * 128],
                                      identity=I_)
                return ins
            P.add("pe", tr, reads=[xnb, cfb], writes=[pb])
            dst = hT[:, g * 4:(g + 1) * 4, tt * 128:(tt + 1) * 128]
            src = ps[bank][:].rearrange("p (a b) -> p a b", a=4)
            if g % 2 == 0:
                P.add("act", lambda e, dst=dst, src=src: e.copy(out=dst, in_=src), reads=[pb], writes=[hTb[tt // 4]])
            else:
                P.add("dve", lambda e, dst=dst, src=src: e.tensor_copy(out=dst, in_=src), reads=[pb],
                      writes=[hTb[tt // 4]])
    if "hT" in dbg:
        tap("hT", hT[:], [128, KC, S], BF16, hTb)
    apos[0] = amark

    if "B" in phases:
        bmark = apos[0]
        betas = small[:, 16:144].rearrange("p (t h) -> p t h", t=NT)
        nbetas = small[:, 144:272].rearrange("p (t h) -> p t h", t=NT)
        gs = small[:, 272:400].rearrange("p (t h) -> p t h", t=NT)
        bgs = small[:, 400:528].rearrange("p (t h) -> p t h", t=NT)
        egs = small[:, 528:912].rearrange("p (t c) -> p t c", t=NT)
        negA = small[:, 912:1040]
        dtb = small[:, 1040:1168]
        gnw = small[:, 1168:1169]
        cw = small[:, 1172:1268].rearrange("p (c j) -> p c j", c=24)
        gqb = Buf("gatesq")
        cwb = Buf("cw")
        xa, xab = carve("xa", [128])
        wg, wgb = carve("wg", [KC, 16], BF16)
        load_w(wg, w_in, OFF_GB, 16, wgb)
        dma("sp", negA, a_log.partition_broadcast(128), gqb)
        dma("sp", dtb, dt_bias.partition_broadcast(128), gqb)
        dma("sp", gnw, gnw_d, gqb)
        dma("sp", cw, conv_wT.rearrange("(c p) j -> p c j", p=128), cwb)
        P.add("act", lambda e: e.activation(out=negA, in_=negA, func=AF.Exp), reads=[gqb], writes=[gqb])
        P.add("dve", lambda e: e.tensor_scalar(out=negA, in0=negA, scalar1=-1.0, scalar2=None, op0=ALU.mult),
              reads=[gqb], writes=[gqb])
        pg = psbuf(0)
        for tt in range(NT):
            def mm(e, tt=tt):
                ins = None
                for kc in range(KC):
                    ins = e.matmul(ps[0][:, tt * 16:(tt + 1) * 16], lhsT=hT[:, kc, tt * 128:(tt + 1) * 128],
                                   rhs=wg[:, kc, :], start=(kc == 0), stop=(kc == KC - 1))
                return ins
            P.add("pe", mm, reads=[hTb[tt // 4], wgb], writes=[pg])
        pgv = ps[0][:, 0:256].rearrange("p (t c) -> p t c", t=NT)
        P.add("act", lambda e: e.activation(out=betas, in_=pgv[:, :, 0:8], func=AF.Sigmoid), reads=[pg], writes=[gqb])
        P.add("dve", lambda e: e.tensor_tensor(out=xa.rearrange("p (t h) -> p t h", t=NT), in0=pgv[:, :, 8:16],
                                               in1=dtb.rearrange("p (t h) -> p t h", t=NT), op=ALU.add),
              reads=[pg, gqb], writes=[xab])
        P.add("act", lambda e: e.activation(out=xa, in_=xa, func=AF.Exp), reads=[xab], writes=[xab])
        P.add("act", lambda e: e.activation(out=xa, in_=xa, func=AF.Ln, bias=epsc[:, 1:2]), reads=[xab, epsb],
              writes=[xab])
        P.add("dve", lambda e: e.tensor_tensor(out=small[:, 272:400], in0=xa, in1=negA, op=ALU.mult),
              reads=[xab, gqb], writes=[gqb])
        P.add("dve", lambda e: e.tensor_scalar(out=small[:, 144:272], in0=small[:, 16:144], scalar1=-1.0, scalar2=None,
                                               op0=ALU.mult), reads=[gqb], writes=[gqb])
        pg2 = psbuf(1)
        for tt in range(NT):
            def mm2(e, tt=tt):
                e.matmul(ps[1][:, tt * 24:tt * 24 + 8], lhsT=cblk(C_TRI), rhs=gs[:, tt, :], start=True, stop=True)
                e.matmul(ps[1][:, tt * 24 + 8:tt * 24 + 16], lhsT=cblk(C_UG), rhs=gs[:, tt, :], start=True, stop=True)
                return e.matmul(ps[1][:, tt * 24 + 16:tt * 24 + 24], lhsT=cblk(C_ONES), rhs=gs[:, tt, :], start=True,
                                stop=True)
            P.add("pe", mm2, reads=[gqb, cfb], writes=[pg2])
        P.add("act", lambda e: e.activation(out=small[:, 528:912], in_=ps[1][:, 0:384], func=AF.Exp), reads=[pg2],
              writes=[gqb])
        P.add("dve", lambda e: e.tensor_tensor(out=bgs, in0=betas, in1=egs[:, :, 0:8], op=ALU.mult), reads=[gqb],
              writes=[gqb])
        if "gates" in dbg:
            tap("gates", small[:, 16:912], [128, 896], F32, [gqb])

        if "0" in phases:
            nheads_g = 0
        NW = 4
        wr, wrb = zip(*[carve("wr%d" % i, [KC, 128], BF16) for i in range(NW)])
        xraw, xrp = carve("xraw", [3 + S])
        xrb = subbufs(xrp, "xraw", [[(0, 12)]] + [[(12 + i * 2048, 12 + (i + 1) * 2048)] for i in range(4)])
        qT, qTp = carve("qT", [S])
        kT, kTp = carve("kT", [S])
        vT, vTp = carve("vT", [S])
        seg4 = [[(i * 2048, (i + 1) * 2048)] for i in range(4)]
        tTb = {"q": subbufs(qTp, "qT", seg4), "k": subbufs(kTp, "kT", seg4), "v": subbufs(vTp, "vT", seg4)}
        szT, szp = carve("szT", [S], BF16)
        szb = subbufs(szp, "szT", [[(i * 1024, (i + 1) * 1024)] for i in range(4)])
        sqt, sqtb = zip(*[carve("sqt%d" % i, [512]) for i in range(2)])
        rs, rsb = carve("rs", [512])
        Ssb, Sb = carve("S", [128])
        NSL = 4
        slot = []
        for i in range(NSL):
            d_ = {}
            for nm in ("Gt", "Ln", "N0", "N1", "M0", "M1", "Xs", "kbg", "vb"):
                d_[nm] = carve("%s_%d" % (nm, i), [128])
            for nm in ("iT", "kdec", "u", "wTn"):
                d_[nm] = [carve("%s_%d_%d" % (nm, i, p_), [128]) for p_ in range(2)]
            d_["EE"] = carve("EE_%d" % i, [256])
            slot.append(d_)
        vnew, vnewb = carve("vnew", [128])
        t1, t1b = carve("t1", [128])
        osb, osbb = carve("osb", [128])
        onb_, onbb = carve("on", [128])
        P.add("dve", lambda e: e.memset(xraw[:, 0:3], 0.0), writes=[xrb[0]])

        pproj = [psbuf(0), psbuf(1)]
        pl2 = psbuf(7)
        pscan = psbuf(2)
        pT = psbuf(7)

        def q4(bank, q):
            return ps[bank][:, q * 128:(q + 1) * 128]

        wcount = [0]

        def proj_fm(col0, dst_fn, pidx):
            wi = wcount[0] % NW
            wcount[0] += 1
            load_w(wr[wi], w_in, col0, 128, wrb[wi])
            for seg in range(4):
                bank = (pidx[0]) % 2
                pidx[0] += 1

                def mm(e, seg=seg, bank=bank, wi=wi):
                    ins = None
                    for kc in range(KC):
                        ins = e.matmul(ps[bank][:, :], lhsT=wr[wi][:, kc, :], rhs=hT[:, kc, seg * 512:(seg + 1) * 512],
                                       start=(kc == 0), stop=(kc == KC - 1))
                    return ins
                P.add("pe", mm, reads=[hTb[seg], wrb[wi]], writes=[pproj[bank]])
                nxt = dst_fn(seg, bank)
                while l2pend:
                    l2pend.pop(0)()
                if nxt is not None:
                    l2pend.append(nxt)

        pidx = [0]
        l2pend = []
        SCQ = 128.0 ** -0.5
        for h in range(nheads_g):
            for ti, nm in enumerate("qkv"):
                ci = ti * 8 + h
                tT = {"q": qT, "k": kT, "v": vT}[nm]

                def consume(seg, bank, nm=nm, ci=ci, tT=tT):
                    s0 = seg * 512
                    P.add("act", lambda e: e.copy(out=xraw[:, 3 + s0:3 + s0 + 512], in_=ps[bank][:, :]),
                          reads=[pproj[bank]], writes=[xrb[seg + 1]])
                    dst = tT[:, s0:s0 + 512]
                    db = tTb[nm][seg]
                    P.add("dve", lambda e: e.tensor_scalar(out=dst, in0=xraw[:, 3 + s0:3 + s0 + 512],
                                                           scalar1=cw[:, ci, 3:4], scalar2=None, op0=ALU.mult),
                          reads=[xrb[seg + 1], cwb], writes=[db])
                    for j in range(3):
                        P.add("dve", lambda e, j=j: e.scalar_tensor_tensor(out=dst, in0=xraw[:, j + s0:j + s0 + 512],
                                                                           scalar=cw[:, ci, j:j + 1], in1=dst,
                                                                           op0=ALU.mult, op1=ALU.add),
                              reads=[xrb[seg + 1], xrb[seg], cwb, db], writes=[db])
                    P.add("act", lambda e: e.activation(out=dst, in_=dst, func=AF.Silu), reads=[db], writes=[db])
                    if nm in "qk":
                        si = seg % 2
                        P.add("act", lambda e: e.activation(out=sqt[si], in_=dst, func=AF.Square), reads=[db],
                              writes=[sqtb[si]])

                        def tail():
                            P.add("pe", lambda e: e.matmul(ps[7][:, :], lhsT=cblk(C_ONES), rhs=sqt[si], start=True,
                                                           stop=True), reads=[sqtb[si], cfb], writes=[pl2])
                            P.add("act", lambda e: e.activation(out=rs, in_=ps[7][:, :], func=AF.Sqrt,
                                                                bias=epsc[:, 0:1]), reads=[pl2, epsb], writes=[rsb])
                            P.add("dve", lambda e: e.reciprocal(out=rs, in_=rs), reads=[rsb], writes=[rsb])
                            if nm == "q":
                                P.add("dve", lambda e: e.scalar_tensor_tensor(out=dst, in0=dst, scalar=SCQ, in1=rs,
                                                                              op0=ALU.mult, op1=ALU.mult),
                                      reads=[db, rsb], writes=[db])
                            else:
                                P.add("dve", lambda e: e.tensor_tensor(out=dst, in0=dst, in1=rs, op=ALU.mult),
                                      reads=[db, rsb], writes=[db])
                        return tail
                    return None
                proj_fm([OFF_GQ, OFF_GK, OFF_GV][ti] + h * 128, consume, pidx)

            def consume_z(seg, bank):
                s0 = seg * 512
                P.add("act", lambda e: e.activation(out=szT[:, s0:s0 + 512], in_=ps[bank][:, :], func=AF.Silu),
                      reads=[pproj[bank]], writes=[szb[seg]])
            proj_fm(OFF_GZ + h * 128, consume_z, pidx)
            while l2pend:
                l2pend.pop(0)()
            if "qkv" in dbg and h == 0:
                tap("qT", qT, [128, S], F32, tTb["q"])
                tap("kT", kT, [128, S], F32, tTb["k"])
                tap("vT", vT, [128, S], F32, tTb["v"])

            P.add("dve", lambda e: e.memset(Ssb, 0.0), writes=[Sb])

            def pre_stages(tt, sl, par, h=h):
                sd = dict(slot[sl])
                for nm_ in ("iT", "kdec", "u", "wTn"):
                    sd[nm_] = slot[sl][nm_][par]
                ts = slice(tt * 128, (tt + 1) * 128)
                sg = tt // 4
                bk = 3 + sl
                pb = pbank[bk]
                stages = []

                def s0():
                    def trKV(e):
                        e.matmul(q4(bk, 0), lhsT=kT[:, ts], rhs=I_, start=True, stop=True)
                        return e.matmul(q4(bk, 1), lhsT=vT[:, ts], rhs=I_, start=True, stop=True)
                    P.add("pe", trKV, reads=[tTb["k"][sg], tTb["v"][sg], cfb], writes=[pb])
                    P.add("dve", lambda e: e.tensor_scalar(out=sd["kbg"][0], in0=q4(bk, 0), scalar1=bgs[:, tt, h:h + 1],
                                                           scalar2=None, op0=ALU.mult),
                          reads=[pb, gqb], writes=[sd["kbg"][1]])
                    P.add("dve", lambda e: e.tensor_scalar(out=sd["kdec"][0], in0=q4(bk, 0),
                                                           scalar1=egs[:, tt, 8 + h:9 + h], scalar2=None, op0=ALU.mult),
                          reads=[pb, gqb], writes=[sd["kdec"][1]])
                    P.add("dve", lambda e: e.tensor_scalar(out=sd["vb"][0], in0=q4(bk, 1), scalar1=betas[:, tt, h:h + 1],
                                                           scalar2=None, op0=ALU.mult),
                          reads=[pb, gqb], writes=[sd["vb"][1]])
                    P.add("dve", lambda e: e.tensor_scalar(out=sd["Gt"][0], in0=cblk(C_TRI), scalar1=gs[:, tt, h:h + 1],
                                                           scalar2=None, op0=ALU.mult),
                          reads=[cfb, gqb], writes=[sd["Gt"][1]])
                stages.append(s0)

                def s1():
                    def mmD(e):
                        e.matmul(q4(bk, 2), lhsT=sd["Gt"][0], rhs=cblk(C_UG), start=True, stop=True)
                        e.matmul(q4(bk, 3), lhsT=cblk(C_UG), rhs=sd["Gt"][0], start=True, stop=False)
                        e.matmul(q4(bk, 3), lhsT=I_, rhs=cblk(C_NBI), start=False, stop=True)
                        e.matmul(q4(bk, 0), lhsT=kT[:, ts], rhs=kT[:, ts], start=True, stop=True)
                        return e.matmul(q4(bk, 1), lhsT=kT[:, ts], rhs=qT[:, ts], start=True, stop=True)
                    P.add("pe", mmD, reads=[sd["Gt"][1], cfb, tTb["k"][sg], tTb["q"][sg]], writes=[pb])
                    P.add("act", lambda e: e.activation(out=sd["EE"][0], in_=ps[bk][:, 256:512], func=AF.Exp),
                          reads=[pb], writes=[sd["EE"][1]])
                    P.add("dve", lambda e: e.scalar_tensor_tensor(out=sd["Ln"][0], in0=q4(bk, 0),
                                                                  scalar=nbetas[:, tt, h:h + 1], in1=sd["EE"][0][:, 0:128],
                                                                  op0=ALU.mult, op1=ALU.mult),
                          reads=[pb, gqb, sd["EE"][1]], writes=[sd["Ln"][1]])
                    P.add("dve", lambda e: e.tensor_tensor(out=sd["iT"][0], in0=q4(bk, 1), in1=sd["EE"][0][:, 128:256],
                                                           op=ALU.mult),
                          reads=[pb, sd["EE"][1]], writes=[sd["iT"][1]])
                stages.append(s1)

                cur = {"N": (I_, cfb), "M": (I_, cfb)}
                for l in range(7):
                    def sA(l=l):
                        Ncur, Nb = cur["N"]
                        P.add("pe", lambda e: e.matmul(q4(bk, 0), lhsT=sd["Ln"][0], rhs=Ncur, start=True, stop=True),
                              reads=[sd["Ln"][1], Nb], writes=[pb])
                        P.add("dve", lambda e: e.tensor_tensor(out=sd["Xs"][0], in0=q4(bk, 0), in1=cblk(C_MU0 + l),
                                                               op=ALU.mult),
                              reads=[pb, cfb], writes=[sd["Xs"][1]])
                    stages.append(sA)

                    def sB(l=l):
                        Ncur, Nb = cur["N"]
                        Mcur, Mb = cur["M"]
                        Nn = sd["N%d" % (l % 2)]
                        Mn = sd["M%d" % (l % 2)]

                        def mmNM(e):
                            ins = None
                            if l > 0:
                                ins = e.matmul(q4(bk, 1), lhsT=Mcur, rhs=sd["Xs"][0], start=True, stop=True)
                            if l < 6:
                                ins = e.matmul(q4(bk, 2), lhsT=sd["Xs"][0], rhs=Mcur, start=True, stop=True)
                            return ins
                        P.add("pe", mmNM, reads=[Nb, Mb, sd["Xs"][1], cfb], writes=[pb])
                        if l > 0:
                            P.add("dve", lambda e: e.tensor_tensor(out=Nn[0], in0=q4(bk, 1), in1=Ncur, op=ALU.add),
                                  reads=[pb, Nb], writes=[Nn[1]])
                        else:
                            P.add("dve", lambda e: e.tensor_tensor(out=Nn[0], in0=sd["Xs"][0], in1=Ncur, op=ALU.add),
                                  reads=[sd["Xs"][1], Nb], writes=[Nn[1]])
                        if l < 6:
                            P.add("dve", lambda e: e.tensor_tensor(out=Mn[0], in0=q4(bk, 2), in1=Mcur, op=ALU.add),
                                  reads=[pb, Mb], writes=[Mn[1]])
                        cur["N"] = Nn
                        cur["M"] = Mn
                    stages.append(sB)

                def sF():
                    Nf, Nfb = cur["N"]

                    def mmUW(e):
                        e.matmul(q4(bk, 0), lhsT=Nf, rhs=sd["vb"][0], start=True, stop=True)
                        return e.matmul(q4(bk, 1), lhsT=sd["kbg"][0], rhs=Nf, start=True, stop=True)
                    P.add("pe", mmUW, reads=[Nfb, sd["vb"][1], sd["kbg"][1]], writes=[pb])
                    P.add("act", lambda e: e.copy(out=sd["u"][0], in_=q4(bk, 0)), reads=[pb], writes=[sd["u"][1]])
                    P.add("act", lambda e: e.mul(out=sd["wTn"][0], in_=q4(bk, 1), mul=-1.0), reads=[pb],
                          writes=[sd["wTn"][1]])
                stages.append(sF)
                return stages

            def scan(tt, sl, par, h=h):
                sd = dict(slot[sl])
                for nm_ in ("iT", "kdec", "u", "wTn"):
                    sd[nm_] = slot[sl][nm_][par]
                ts = slice(tt * 128, (tt + 1) * 128)
                sg = tt // 4
                subs = []

                def subA():
                    P.add("pe", mmV, reads=[sd["u"][1], sd["wTn"][1], Sb, cfb], writes=[pscan])
                    P.add("dve", lambda e: e.tensor_tensor(out=vnew, in0=q4(2, 0), in1=sd["u"][0], op=ALU.add),
                          reads=[pscan, sd["u"][1]], writes=[vnewb])

                def subB():
                    P.add("pe", mmO, reads=[tTb["q"][sg], Sb, sd["iT"][1], sd["kdec"][1], vnewb], writes=[pscan])
                    P.add("act", lambda e: e.mul(out=t1, in_=q4(2, 1), mul=egs[:, tt, h:h + 1]), reads=[pscan, gqb],
                          writes=[t1b])
                    P.add("dve", lambda e: e.tensor_tensor(out=osb, in0=t1, in1=q4(2, 2), op=ALU.add),
                          reads=[t1b, pscan], writes=[osbb])
                    P.add("dve", lambda e: e.scalar_tensor_tensor(out=Ssb, in0=Ssb, scalar=egs[:, tt, 16 + h:17 + h],
                                                                  in1=q4(2, 3), op0=ALU.mult, op1=ALU.add),
                          reads=[Sb, gqb, pscan], writes=[Sb])

                def subC():
                    P.add("act", lambda e: e.activation(out=t1, in_=osb, func=AF.Square, accum_out=st[:, 4:5]),
                          reads=[osbb], writes=[t1b, stb])
                    P.add("act", lambda e: e.activation(out=st[:, 5:6], in_=st[:, 4:5], func=AF.Sqrt, bias=epsc[:, 0:1],
                                                        scale=1.0 / 128), reads=[stb, epsb], writes=[stb])
                    P.add("dve", lambda e: e.reciprocal(out=st[:, 6:7], in_=st[:, 5:6]), reads=[stb], writes=[stb])
                    P.add("act", lambda e: e.mul(out=onb_, in_=osb, mul=st[:, 6:7]), reads=[osbb, stb], writes=[onbb])

                def subD():
                    P.add("pe", lambda e: e.matmul(q4(7, 0), lhsT=onb_, rhs=I_, start=True, stop=True),
                          reads=[onbb, cfb], writes=[pT])
                    P.add("dve", lambda e: e.scalar_tensor_tensor(out=oaT[:, h, ts], in0=q4(7, 0), scalar=gnw,
                                                                  in1=szT[:, ts], op0=ALU.mult, op1=ALU.mult),
                          reads=[pT, gqb, szb[sg]], writes=[oaTb[sg]])

                def mmV(e):
                    return e.matmul(q4(2, 0), lhsT=sd["wTn"][0], rhs=Ssb, start=True, stop=True)

                def mmO(e):
                    e.matmul(q4(2, 1), lhsT=qT[:, ts], rhs=Ssb, start=True, stop=True)
                    e.matmul(q4(2, 2), lhsT=sd["iT"][0], rhs=vnew, start=True, stop=True)
                    return e.matmul(q4(2, 3), lhsT=sd["kdec"][0], rhs=vnew, start=True, stop=True)
                return [subA, subB, subC, subD]

            pend = []
            for gi, g0 in enumerate(range(0, NT, NSL)):
                sts = [pre_stages(g0 + i, i, gi % 2) for i in range(NSL)]
                for k in range(len(sts[0])):
                    for i in range(NSL):
                        sts[i][k]()
                    if pend:
                        pend.pop(0)()
                while pend:
                    pend.pop(0)()
                for i in range(NSL):
                    pend.extend(scan(g0 + i, i, gi % 2))
            while pend:
                pend.pop(0)()
        print("phase B arena bytes", apos[0])
        if "oaT" in dbg:
            tap("oaT", oaT[:], [128, H, S], BF16, oaTb)
        if "scan" in dbg:
            tap("osb", osb, [128, 128], F32, [osbb])
            tap("S", Ssb, [128, 128], F32, [Sb])
            tap("vnew", vnew, [128, 128], F32, [vnewb])
        apos[0] = bmark

    obT, obTp = carve("obT", [H, S], BF16)
    obTb = subbufs(obTp, "obT", [[(hh * 4096 + q * 1024, hh * 4096 + (q + 1) * 1024) for hh in range(H)] for q in range(4)])
    cmark = apos[0]
    if "C" in phases:
        NWC = 4
        cwr, cwrb = zip(*[carve("cwr%d" % i, [KC, 128], BF16) for i in range(NWC)])
        cmf, cmfb = carve("cmf", [1280])
        cmb, cmbb = carve("cmb", [1536], BF16)
        dma("sp", cmf, cm_d, cmfb)
        if "7" not in phases:
            P.add("dve", lambda e: e.tensor_copy(out=cmb[:, 0:1280], in_=cmf), reads=[cmfb], writes=[cmbb])
        P.add("dve", lambda e: e.tensor_copy(out=cmb[:, 1280:1408], in_=cblk(C_ID)), reads=[cfb], writes=[cmbb])
        P.add("dve", lambda e: e.tensor_copy(out=cmb[:, 1408:1536], in_=cblk(C_ONES)), reads=[cfb], writes=[cmbb])
        cbias = cmb[:, 0:256]
        Ib = cmb[:, 1280:1408]
        onesb = cmb[:, 1408:1536]
        qTb_, qTbb = carve("mqTb", [S], BF16)
        kTb_, kTbb = carve("mkTb", [S], BF16)
        qTf, qTfb = carve("mqTf", [S])
        Vt, Vtb = carve("mV", [NT, 128], BF16)
        mszT, mszb = carve("mszT", [S], BF16)
        nbT, nbTb = carve("nbT", [S], BF16)
        kmT, kmTb = carve("kmT", [8])
        gm, gmb = carve("gm", [NT, 8])
        nb, nbb = carve("nb", [NT, 8])
        top8, top8b = carve("top8", [8])
        pTr, pTrb = zip(*[carve("pT%d" % i, [256], BF16) for i in range(4)])
        rr, rrb = carve("rr", [256])
        zz, zzb = carve("zz", [256])
        cwcount = [0]

        def proj_c(col0, consume):
            wi = cwcount[0] % NWC
            cwcount[0] += 1
            load_w(cwr[wi], w_in, col0, 128, cwrb[wi])
            for seg in range(4):
                bank = seg % 2

                def mm(e, seg=seg, bank=bank, wi=wi):
                    ins = None
                    for kc in range(KC):
                        ins = e.matmul(ps[bank][:, :], lhsT=cwr[wi][:, kc, :], rhs=hT[:, kc, seg * 512:(seg + 1) * 512],
                                       start=(kc == 0), stop=(kc == KC - 1))
                    return ins
                P.add("pe", mm, reads=[hTb[seg], cwrb[wi]], writes=[pbank[bank]])
                consume(seg, bank)

        for h in range(nheads_m):
            def cons_q(seg, bank):
                s0 = seg * 512
                P.add("act", lambda e: e.mul(out=qTf[:, s0:s0 + 512], in_=ps[bank][:, :], mul=SCQ_M),
                      reads=[pbank[bank]], writes=[qTfb])
                P.add("dve", lambda e: e.tensor_copy(out=qTb_[:, s0:s0 + 512], in_=qTf[:, s0:s0 + 512]),
                      reads=[qTfb], writes=[qTbb])
            SCQ_M = 128.0 ** -0.5
            proj_c(OFF_MQ + h * 128, cons_q)

            def cons_k(seg, bank):
                s0 = seg * 512
                P.add("act", lambda e: e.copy(out=kTb_[:, s0:s0 + 512], in_=ps[bank][:, :]),
                      reads=[pbank[bank]], writes=[kTbb])
                P.add("dve", lambda e: e.tensor_reduce(out=kmT[:, seg * 2:seg * 2 + 2],
                                                       in_=ps[bank][:, :].rearrange("p (a b) -> p a b", a=2),
                                                       axis=AX.X, op=ALU.add),
                      reads=[pbank[bank]], writes=[kmTb])
            proj_c(OFF_MK + h * 128, cons_k)

            def cons_z(seg, bank):
                s0 = seg * 512
                P.add("act", lambda e: e.activation(out=mszT[:, s0:s0 + 512], in_=ps[bank][:, :], func=AF.Silu),
                      reads=[pbank[bank]], writes=[mszb])
            proj_c(OFF_MZ + h * 128, cons_z)

            wi = cwcount[0] % NWC
            cwcount[0] += 1
            load_w(cwr[wi], w_in, OFF_MV + h * 128, 128, cwrb[wi])
            for g in range(4):
                bank = 2 + g % 2

                def mmv(e, g=g, bank=bank, wi=wi):
                    ins = None
                    for j in range(4):
                        tt = g * 4 + j
                        for kc in range(KC):
                            ins = e.matmul(ps[bank][:, j * 128:(j + 1) * 128], lhsT=hT[:, kc, tt * 128:(tt + 1) * 128],
                                           rhs=cwr[wi][:, kc, :], start=(kc == 0), stop=(kc == KC - 1))
                    return ins
                P.add("pe", mmv, reads=[hTb[g], cwrb[wi]], writes=[pbank[bank]])
                P.add("act", lambda e, g=g, bank=bank: e.copy(out=Vt[:, g * 4:(g + 1) * 4, :],
                                                              in_=ps[bank][:, :].rearrange("p (a b) -> p a b", a=4)),
                      reads=[pbank[bank]], writes=[Vtb])

            P.add("dve", lambda e: e.memset(gm, -1e30), writes=[gmb])
            P.add("dve", lambda e: e.memset(nb, 0.0), writes=[nbb])

            def mmg(e):
                ins = None
                for tt in range(8, NT):
                    ins = e.matmul(ps[4][:, tt * 8:tt * 8 + 8], lhsT=qTf[:, tt * 128:(tt + 1) * 128], rhs=kmT,
                                   start=True, stop=True)
                return ins
            P.add("pe", mmg, reads=[qTfb, kmTb], writes=[pbank[4]])
            for tt in range(8, NT):
                qb = tt // 2
                P.add("dve", lambda e, tt=tt, qb=qb: e.tensor_copy(out=gm[:, tt, 0:qb], in_=ps[4][:, tt * 8:tt * 8 + qb]),
                      reads=[pbank[4]], writes=[gmb])
                P.add("dve", lambda e, tt=tt: e.max(out=top8, in_=gm[:, tt, :]), reads=[gmb], writes=[top8b])
                P.add("dve", lambda e, tt=tt: e.tensor_scalar(out=nb[:, tt, :], in0=gm[:, tt, :], scalar1=top8[:, 2:3],
                                                              scalar2=NEG, op0=ALU.is_lt, op1=ALU.mult),
                      reads=[gmb, top8b], writes=[nbb])
            for g in range(4):
                bank = 5 + g % 2

                def mmt(e, g=g, bank=bank):
                    ins = None
                    for j in range(4):
                        tt = g * 4 + j
                        ins = e.matmul(ps[bank][0:8, j * 128:(j + 1) * 128], lhsT=nb[:, tt, :], rhs=I_, start=True,
                                       stop=True)
                    return ins
                P.add("pe", mmt, reads=[nbb, cfb], writes=[pbank[bank]])
                P.add("act", lambda e, g=g, bank=bank: e.copy(out=nbT[0:8, g * 512:(g + 1) * 512], in_=ps[bank][0:8, :]),
                      reads=[pbank[bank]], writes=[nbTb])
            if "moba_pre" in dbg and h == 0:
                tap("nbT", nbT[0:8, :], [8, S], BF16, [nbTb])
                tap("mqT", qTf, [128, S], F32, [qTfb])
                tap("mV", Vt, [128, NT, 128], BF16, [Vtb])

            for qb in range(8):
                q0 = qb * 256
                nk = 2 * qb + 2
                ob_bank = 6 + qb % 2
                pob = pbank[ob_bank]

                def emit_S(kt, qb=qb, q0=q0):
                    n = kt // 2
                    sbank = kt % 4
                    if kt == 2 * qb + 1:
                        def mm(e):
                            e.matmul(ps[sbank][:, 128:256], lhsT=kTb_[:, kt * 128:(kt + 1) * 128],
                                     rhs=qTb_[:, q0 + 128:q0 + 256], start=True, stop=False)
                            return e.matmul(ps[sbank][:, 128:256], lhsT=Ib, rhs=cbias[:, 0:128], start=False, stop=True)
                    elif kt == 2 * qb:
                        def mm(e):
                            e.matmul(ps[sbank][:, 0:256], lhsT=kTb_[:, kt * 128:(kt + 1) * 128], rhs=qTb_[:, q0:q0 + 256],
                                     start=True, stop=False)
                            return e.matmul(ps[sbank][:, 0:256], lhsT=Ib, rhs=cbias, start=False, stop=True)
                    else:
                        def mm(e):
                            e.matmul(ps[sbank][:, 0:256], lhsT=kTb_[:, kt * 128:(kt + 1) * 128], rhs=qTb_[:, q0:q0 + 256],
                                     start=True, stop=False)
                            return e.matmul(ps[sbank][:, 0:256], lhsT=cmb[0:8, 256 + n * 128:256 + (n + 1) * 128],
                                            rhs=nbT[0:8, q0:q0 + 256], start=False, stop=True)
                    P.add("pe", mm, reads=[kTbb, qTbb, cmbb, nbTb], writes=[pbank[sbank]])

                def emit_PV(kt, qb=qb, q0=q0, nk=nk, ob_bank=ob_bank, pob=pob):
                    sbank = kt % 4
                    c0 = 128 if kt == 2 * qb + 1 else 0
                    pt = pTr[kt % 4]
                    P.add("act", lambda e: e.activation(out=pt[:, c0:256], in_=ps[sbank][:, c0:256], func=AF.Exp),
                          reads=[pbank[sbank]], writes=[pTrb[kt % 4]])

                    def mm(e):
                        e.matmul(ps[ob_bank][:, c0:256], lhsT=Vt[:, kt, :], rhs=pt[:, c0:256], start=(kt == 0),
                                 stop=(kt == nk - 1))
                        return e.matmul(ps[ob_bank][:, 256 + c0:512], lhsT=onesb, rhs=pt[:, c0:256], start=False,
                                        stop=(kt == nk - 1), skip_group_check=True)
                    P.add("pe", mm, reads=[Vtb, pTrb[kt % 4], cmbb], writes=[pob])

                emit_S(0)
                emit_S(1)
                for kt in range(nk):
                    emit_PV(kt)
                    if kt + 2 < nk:
                        emit_S(kt + 2)
                P.add("dve", lambda e, ob_bank=ob_bank: e.reciprocal(out=rr, in_=ps[ob_bank][:, 256:512]),
                      reads=[pob], writes=[rrb])
                P.add("dve", lambda e, q0=q0: e.tensor_tensor(out=zz, in0=rr, in1=mszT[:, q0:q0 + 256], op=ALU.mult),
                      reads=[rrb, mszb], writes=[zzb])
                P.add("dve", lambda e, q0=q0, ob_bank=ob_bank, h=h: e.tensor_tensor(out=obT[:, h, q0:q0 + 256],
                                                                                   in0=ps[ob_bank][:, 0:256], in1=zz,
                                                                                   op=ALU.mult),
                      reads=[pob, zzb], writes=[obTb[qb // 2]])
        if "obT" in dbg:
            tap("obT", obT, [128, H, S], BF16, obTb)
    apos[0] = cmark

    if "D" in phases:
        dmark = apos[0]
        mT1, mT1p = carve("mT1", [KC, 1024], BF16)
        mT1b = subbufs(mT1p, "mT1", [[(c * 2048 + q * 1024, c * 2048 + (q + 1) * 1024) for c in range(KC)] for q in range(2)])
        ymark = apos[0]
        NWD = 2
        wga, wgab = zip(*[carve("wga%d" % i, [KC, 128], BF16) for i in range(NWD)])
        wgb_, wgbb = zip(*[carve("wgb%d" % i, [KC, 128], BF16) for i in range(NWD)])
        wba, wbab = zip(*[carve("wba%d" % i, [8, 128], BF16) for i in range(NWD)])
        wbb, wbbb = zip(*[carve("wbb%d" % i, [8, 128], BF16) for i in range(NWD)])
        sga, sgab = zip(*[carve("sga%d" % i, [512]) for i in range(2)])
        sgb, sgbb = zip(*[carve("sgb%d" % i, [512]) for i in range(2)])
        it = 0
        for hf in range(2):
            for c in range(KC):
                wi = it % NWD
                load_w(wga[wi], w_in, OFF_GATE_A + c * 128, 128, wgab[wi])
                load_w(wgb_[wi], w_in, OFF_GATE_B + c * 128, 128, wgbb[wi])
                load_w(wba[wi], w_a, c * 128, 128, wbab[wi])
                load_w(wbb[wi], w_b, c * 128, 128, wbbb[wi])
                for seg in range(2):
                    t0 = hf * 1024 + seg * 512
                    hq = t0 // 512
                    bs = (it * 2 + seg) % 2 * 4
                    si = seg

                    def mmg(e, w, bank, t0=t0):
                        ins = None
                        for kc in range(KC):
                            ins = e.matmul(ps[bank][:, :], lhsT=w[:, kc, :], rhs=hT[:, kc, t0:t0 + 512], start=(kc == 0),
                                           stop=(kc == KC - 1))
                        return ins

                    def mmb(e, w, src, bank, t0=t0):
                        ins = None
                        for kc in range(8):
                            ins = e.matmul(ps[bank][:, :], lhsT=w[:, kc, :], rhs=src[:, kc, t0:t0 + 512], start=(kc == 0),
                                           stop=(kc == 7))
                        return ins
                    P.add("pe", lambda e, wi=wi, bs=bs, mmg=mmg: mmg(e, wga[wi], bs), reads=[hTb[hq], wgab[wi]],
                          writes=[pbank[bs]])
                    P.add("act", lambda e, bs=bs, si=si: e.activation(out=sga[si], in_=ps[bs][:, :], func=AF.Sigmoid),
                          reads=[pbank[bs]], writes=[sgab[si]])
                    P.add("pe", lambda e, wi=wi, bs=bs, mmg=mmg: mmg(e, wgb_[wi], bs + 1), reads=[hTb[hq], wgbb[wi]],
                          writes=[pbank[bs + 1]])
                    P.add("act", lambda e, bs=bs, si=si: e.activation(out=sgb[si], in_=ps[bs + 1][:, :], func=AF.Sigmoid),
                          reads=[pbank[bs + 1]], writes=[sgbb[si]])
                    P.add("pe", lambda e, wi=wi, bs=bs, mmb=mmb: mmb(e, wba[wi], oaT, bs + 2), reads=[oaTb[hq], wbab[wi]],
                          writes=[pbank[bs + 2]])
                    P.add("dve", lambda e, bs=bs, si=si: e.tensor_tensor(out=sga[si], in0=ps[bs + 2][:, :], in1=sga[si],
                                                                         op=ALU.mult),
                          reads=[pbank[bs + 2], sgab[si]], writes=[sgab[si]])
                    P.add("pe", lambda e, wi=wi, bs=bs, mmb=mmb: mmb(e, wbb[wi], obT, bs + 3), reads=[obTb[hq], wbbb[wi]],
                          writes=[pbank[bs + 3]])
                    P.add("dve", lambda e, bs=bs, si=si: e.tensor_tensor(out=sgb[si], in0=ps[bs + 3][:, :], in1=sgb[si],
                                                                         op=ALU.mult),
                          reads=[pbank[bs + 3], sgbb[si]], writes=[sgbb[si]])
                    if hf == 0:
                        dst = mT1[:, c, seg * 512:(seg + 1) * 512]
                        dbuf = mT1b[seg]
                    else:
                        dst = hT[:, c, seg * 512:(seg + 1) * 512]
                        dbuf = hTb[seg]
                    P.add("dve", lambda e, si=si, dst=dst: e.tensor_tensor(out=dst, in0=sga[si], in1=sgb[si], op=ALU.add),
                          reads=[sgab[si], sgbb[si]], writes=[dbuf])
                it += 1
        if "mT" in dbg:
            tap("mT1", mT1, [128, KC, 1024], BF16, mT1b)
            tap("mT2", hT[:, :, 0:1024], [128, KC, 1024], BF16, hTb[0:2])
        apos[0] = ymark
        outsl = [Buf("outdram_s%d" % i) for i in range(2)]
        postw, postwb = carve("postw", [D])
        xn_junk, xnjb = carve("xnj", [D], BF16)
        xt, xtb = zip(*[carve("xt%d" % i, [D]) for i in range(2)])
        ysb, ysbb = carve("ysb", [D])
        dma("sp", postw, post_w.partition_broadcast(128), postwb)
        for q in range(4):
            dma("pool", oaT[:, :, q * 512:(q + 1) * 512],
                w_out[0:1024, q * 512:(q + 1) * 512].rearrange("(kc p) c -> p kc c", p=128), oaTb[q])
            dma("pool", obT[:, :, q * 512:(q + 1) * 512],
                w_out[1024:2048, q * 512:(q + 1) * 512].rearrange("(kc p) c -> p kc c", p=128), obTb[q])
        for tt in range(NT):
            sl = tt % 2
            dma("sp", xt[sl], x[tt * 128:(tt + 1) * 128, :], xtb[sl])
            for ct in range(4):
                bank = (tt % 2) * 4 + ct

                def mmy(e, tt=tt, ct=ct, bank=bank):
                    ins = None
                    for kc in range(KC):
                        if tt < 8:
                            lt = mT1[:, kc, tt * 128:(tt + 1) * 128]
                        else:
                            lt = hT[:, kc, (tt - 8) * 128:(tt - 7) * 128]
                        wsrc = oaT if kc < 8 else obT
                        ins = e.matmul(ps[bank][:, :], lhsT=lt, rhs=wsrc[:, kc % 8, ct * 512:(ct + 1) * 512],
                                       start=(kc == 0), stop=(kc == KC - 1))
                    return ins
                mb = mT1b[tt // 4] if tt < 8 else hTb[(tt - 8) // 4]
                P.add("pe", mmy, reads=[mb, oaTb[ct], obTb[ct]], writes=[pbank[bank]])
                P.add("act", lambda e, ct=ct, bank=bank: e.copy(out=ysb[:, ct * 512:(ct + 1) * 512], in_=ps[bank][:, :]),
                      reads=[pbank[bank]], writes=[ysbb])
            P.add("act", lambda e, sl=sl: e.activation(out=xn_junk, in_=ysb, func=AF.Square, accum_out=st[:, 0:1]),
                  reads=[ysbb], writes=[xnjb, stb])
            P.add("act", lambda e: e.activation(out=st[:, 1:2], in_=st[:, 0:1], func=AF.Sqrt, bias=epsc[:, 0:1],
                                                scale=1.0 / D), reads=[stb, epsb], writes=[stb])
            P.add("dve", lambda e: e.reciprocal(out=st[:, 2:3], in_=st[:, 1:2]), reads=[stb], writes=[stb])
            P.add("dve", lambda e: e.scalar_tensor_tensor(out=ysb, in0=ysb, scalar=st[:, 2:3], in1=postw, op0=ALU.mult,
                                                          op1=ALU.mult), reads=[ysbb, stb, postwb], writes=[ysbb])
            P.add("dve", lambda e, sl=sl: e.tensor_tensor(out=xt[sl], in0=xt[sl], in1=ysb, op=ALU.add),
                  reads=[xtb[sl], ysbb], writes=[xtb[sl]])
            dma("sp", out[tt * 128:(tt + 1) * 128, :], xt[sl], outsl[sl], reads=[xtb[sl]])
        apos[0] = dmark

    P.add("sp", lambda e: e.nop(), reads=[outb] + (outsl if "D" in phases else []))
    P.finalize()
    return nc, dbg_outs


def _in_maps(x, pre_norm_w, w_in, conv_w, a_log, dt_bias, gdn_norm_w, w_branch_a, w_branch_b, w_out, post_norm_w):
    cf, cm = make_consts()
    f = lambda a: np.ascontiguousarray(np.asarray(a, dtype=np.float32))
    shared = {
        "pre_w": f(pre_norm_w[0][None, :]),
        "post_w": f(post_norm_w[0][None, :]),
        "w_in": f(w_in[0]),
        "conv_wT": f(np.asarray(conv_w[0]).T),
        "a_log16": f(np.tile(np.asarray(a_log[0]), 16)[None, :]),
        "dt_bias16": f(np.tile(np.asarray(dt_bias[0]), 16)[None, :]),
        "gnw": f(np.asarray(gdn_norm_w[0])[:, None]),
        "w_a": f(w_branch_a[0]),
        "w_b": f(w_branch_b[0]),
        "w_out": f(w_out[0]),
        "cf": cf,
        "cm": cm,
    }
    return [dict(shared, x=f(x[b])) for b in range(x.shape[0])]


def kernel(**inputs):
    maps = _in_maps(**inputs)
    import os
    nc, _ = build(phases=os.environ.get("KPHASES", "ABCD"))
    res = run_bass_kernel_spmd(nc, maps, core_ids=list(range(len(maps))))
    return np.stack([np.asarray(r["out"], dtype=np.float32) for r in res.results], axis=0)
```

```python
from contextlib import ExitStack

import numpy as np
import concourse.bass as bass
import concourse.mybir as mybir
from concourse.bass_utils import run_bass_kernel_spmd

F32 = mybir.dt.float32
BF16 = mybir.dt.bfloat16
AF = mybir.ActivationFunctionType
ALU = mybir.AluOpType
AX = mybir.AxisListType

S = 2048
D = 2048
NT = S // 128
KC = D // 128
H = 8
IN_W = 12304
EPS = 1e-6
NEG = -30000.0
STAGE_LIMIT = 1000
PRE_TAPS = ("u", "wTn", "iT", "Ln", "kdec", "N0")

OFF_GQ, OFF_GK, OFF_GV = 0, 1024, 2048
OFF_GZ = 3072
OFF_GB = 4096
OFF_GA = 4104
OFF_MQ, OFF_MK, OFF_MV = 4112, 4112 + 1024, 4112 + 2048
OFF_MZ = 4112 + 3072
OFF_GATE_A = 8208
OFF_GATE_B = 8208 + 2048

C_ID, C_TRI, C_UG, C_ONES, C_NBS, C_NBI, C_MU0 = 0, 1, 2, 3, 4, 5, 6
NCB = 13


def make_consts():
    r = np.arange(128)[:, None]
    c = np.arange(128)[None, :]
    blocks = [None] * NCB
    blocks[C_ID] = (r == c)
    blocks[C_TRI] = (r <= c)
    blocks[C_UG] = (r > c)
    blocks[C_ONES] = np.ones((128, 128), bool)
    nbs = np.where(r > c, 0.0, NEG)
    nbi = np.where(r <= c, 0.0, NEG)
    out = []
    for k in range(NCB):
        if k == C_NBS:
            out.append(nbs.astype(np.float32))
        elif k == C_NBI:
            out.append(nbi.astype(np.float32))
        elif k >= C_MU0:
            l = k - C_MU0
            b = 1 << l
            m = ((r // (2 * b)) == (c // (2 * b))) & ((r % (2 * b)) < b) & ((c % (2 * b)) >= b)
            out.append(m.astype(np.float32))
        else:
            out.append(blocks[k].astype(np.float32))
    cf = np.concatenate(out, axis=1)
    cb = np.zeros((128, 256), np.float32)
    cb[:, :128] = nbi
    es = np.zeros((128, 8 * 128), np.float32)
    for n in range(8):
        es[n, n * 128:(n + 1) * 128] = 1.0
    return np.ascontiguousarray(cf), np.ascontiguousarray(np.concatenate([cb, es], axis=1))


class Buf:
    __slots__ = ("name", "last_w", "readers", "sem", "dcnt", "iv", "ov", "excl")
    REG = []

    def __init__(self, name, iv=None, excl=False):
        self.excl = excl
        self.name = name
        self.last_w = None
        self.readers = []
        self.sem = None
        self.dcnt = 0
        if iv is not None and not isinstance(iv, list):
            iv = [iv]
        self.iv = iv
        self.ov = [self]
        if iv is not None:
            for o in Buf.REG:
                if any(a[0] == b[0] and a[1] < b[2] and b[1] < a[2] for a in iv for b in o.iv):
                    o.ov.append(self)
                    self.ov.append(o)
            Buf.REG.append(self)


class Op:
    __slots__ = ("eng", "fn", "deps", "ndma", "sig", "sigval", "wbuf")


class Prog:
    ENGS = ("pe", "act", "dve", "pool", "sp")

    def __init__(self, nc, stack):
        self.nc = nc
        self.stack = stack
        self.ops = []
        self.dma_bufs = []

    def add(self, eng, fn, reads=(), writes=(), ndma=0, waw=True):
        idx = len(self.ops)
        deps = set()
        for b0 in reads:
            for b in b0.ov:
                if b.last_w is not None:
                    deps.add(b.last_w)
                if b.excl:
                    for r in b.readers:
                        if self.ops[r].eng != eng:
                            deps.add(r)
        for b0 in writes:
            for b in b0.ov:
                if b.last_w is not None and (waw or b is not b0):
                    deps.add(b.last_w)
                deps.update(b.readers)
        deps.discard(idx)
        for b in reads:
            b.readers.append(idx)
        for b in writes:
            b.last_w = idx
            b.readers = []
        op = Op()
        op.eng = eng
        op.fn = fn
        op.deps = deps
        op.ndma = ndma
        op.sig = False
        op.sigval = 0
        op.wbuf = None
        if ndma:
            assert len(writes) == 1
            op.wbuf = writes[0]
            if op.wbuf.sem is None:
                op.wbuf.sem = True
                self.dma_bufs.append(op.wbuf)
        self.ops.append(op)
        return idx

    def finalize(self):
        nc = self.nc
        ops = self.ops
        for op in ops:
            for d in op.deps:
                p = ops[d]
                if p.eng == "pe" and op.eng == "pe" and not p.ndma:
                    continue
                p.sig = True
        esem = {e: self.stack.enter_context(nc.semaphore("sem_" + e)) for e in self.ENGS}
        for b in self.dma_bufs:
            b.sem = self.stack.enter_context(nc.semaphore("dsem_" + b.name))
        cnt = {e: 0 for e in self.ENGS}
        for op in ops:
            if op.ndma:
                op.wbuf.dcnt += 16 * op.ndma
                op.sigval = op.wbuf.dcnt
            elif op.sig:
                cnt[op.eng] += 1
                op.sigval = cnt[op.eng]

        def emit(ename, e):
            waited = {}
            for op in ops:
                if op.eng != ename:
                    continue
                need = {}
                for d in op.deps:
                    p = ops[d]
                    if p.ndma:
                        sem = p.wbuf.sem
                    else:
                        if p.eng == "pe" and ename == "pe":
                            continue
                        sem = esem[p.eng]
                    k = id(sem)
                    if k not in need or need[k][1] < p.sigval:
                        need[k] = (sem, p.sigval)
                for k, (sem, v) in need.items():
                    if waited.get(k, 0) < v:
                        e.wait_ge(sem, v)
                        waited[k] = v
                ins = op.fn(e)
                if op.ndma:
                    pass
                elif op.sig:
                    ins.then_inc(esem[ename], 1)

        with nc.Block() as block:
            @block.tensor
            def _(e):
                emit("pe", e)

            @block.scalar
            def _(e):
                emit("act", e)

            @block.vector
            def _(e):
                emit("dve", e)

            @block.gpsimd
            def _(e):
                emit("pool", e)

            @block.sync
            def _(e):
                emit("sp", e)


def build(dbg=(), nheads_g=H, nheads_m=H, phases="ABCD"):
    Buf.REG = []
    nc = bass.Bass("TRN2", target_bir_lowering=False)
    stack = ExitStack()
    P = Prog(nc, stack)

    def dram(name, shape, dt=F32, kind="ExternalInput"):
        return nc.dram_tensor(name, list(shape), dt, kind=kind).ap()

    x = dram("x", [S, D])
    pre_w = dram("pre_w", [1, D])
    post_w = dram("post_w", [1, D])
    w_in = dram("w_in", [D, IN_W])
    conv_wT = dram("conv_wT", [3072, 4])
    a_log = dram("a_log16", [1, 128])
    dt_bias = dram("dt_bias16", [1, 128])
    gnw_d = dram("gnw", [128, 1])
    w_a = dram("w_a", [1024, D])
    w_b = dram("w_b", [1024, D])
    w_out = dram("w_out", [D, D])
    cf_d = dram("cf", [128, NCB * 128])
    cm_d = dram("cm", [128, 256 + 1024])
    out = dram("out", [S, D], kind="ExternalOutput")

    def sb(name, shape, dt=F32):
        return stack.enter_context(nc.sbuf_tensor(name, list(shape), dt))

    hT = sb("hT", [128, KC, S], BF16)
    oaT = sb("oaT", [128, H, S], BF16)
    cf = sb("cf_sb", [128, NCB * 128], F32)
    ARENA_BYTES = 100 * 1024
    arena = sb("arena", [128, ARENA_BYTES // 4], F32)
    ps = [stack.enter_context(nc.psum_tensor("ps%d" % i, [128, 512], F32)) for i in range(8)]
    hTb = [Buf("hT%d" % i) for i in range(4)]
    oaTb = [Buf("oaT%d" % i) for i in range(4)]
    cfb = Buf("cf")

    pbank = [Buf("psb%d" % i, excl=True) for i in range(8)]

    def psbuf(bank, c0=0, c1=512):
        return pbank[bank]

    apos = [0]

    def carve(name, free_shape, dt=F32):
        esz = 4 if dt == F32 else 2
        n = int(np.prod(free_shape))
        nb = (n * esz + 31) // 32 * 32
        off = apos[0]
        apos[0] += nb
        assert apos[0] <= ARENA_BYTES, (name, apos[0])
        ap = arena[:, off // 4:(off + nb) // 4]
        if dt != F32:
            ap = ap.bitcast(dt)
        ap = ap[:, 0:n]
        if len(free_shape) == 2:
            ap = ap.rearrange("p (a b) -> p a b", a=free_shape[0])
        elif len(free_shape) == 3:
            ap = ap.rearrange("p (a b c) -> p a b c", a=free_shape[0], b=free_shape[1])
        return ap, Buf(name, iv=("arena", off, off + nb))

    def subbufs(parent, name, pieces):
        base = parent.iv[0][1]
        return [Buf("%s_%d" % (name, i), iv=[("arena", base + lo, base + hi) for lo, hi in pc])
                for i, pc in enumerate(pieces)]

    def cblk(k):
        return cf[:, k * 128:(k + 1) * 128]

    def dma(eng, out_ap, in_ap, wbuf, reads=(), waw=True):
        def fn(e):
            return e.dma_start(out=out_ap, in_=in_ap).then_inc(wbuf.sem, 16)
        P.add(eng, fn, reads=reads, writes=[wbuf], ndma=1, waw=waw)

    def load_w(dst_ap, wdram, c0, ncols, wbuf):
        src = wdram[:, c0:c0 + ncols].rearrange("(kc p) c -> p kc c", p=128)
        dma("pool", dst_ap, src, wbuf)

    outb = Buf("outdram")
    outsl = []
    dbg_outs = {}

    def tap(name, ap, shape, dt, rbuf):
        d = dram("dbg_" + name, shape, dt, kind="ExternalOutput")
        dbg_outs[name] = d
        dma("sp", d, ap, outb, reads=rbuf, waw=False)

    dma("sp", cf[:], cf_d, cfb)
    small = sb("small", [128, 1280], F32)
    epsc = small[:, 0:2]
    epsb = Buf("epsc")
    P.add("dve", lambda e: e.memset(small[:, 0:1], EPS), writes=[epsb])
    P.add("dve", lambda e: e.memset(small[:, 1:2], 1.0), writes=[epsb])
    st = small[:, 8:16]
    stb = Buf("st")
    I_ = cblk(C_ID)

    amark = apos[0]
    xs, xsb = zip(*[carve("xs%d" % i, [D]) for i in range(2)])
    xn, xnb = carve("xn", [D])
    prew, prewb = carve("prew", [D])
    dma("sp", prew, pre_w.partition_broadcast(128), prewb)
    for tt in range(NT):
        sl = tt % 2
        dma("sp", xs[sl], x[tt * 128:(tt + 1) * 128, :], xsb[sl])
        P.add("act", lambda e, sl=sl: e.activation(out=xn, in_=xs[sl], func=AF.Square, accum_out=st[:, 0:1]),
              reads=[xsb[sl]], writes=[xnb, stb])
        P.add("act", lambda e: e.activation(out=st[:, 1:2], in_=st[:, 0:1], func=AF.Sqrt, bias=epsc[:, 0:1],
                                            scale=1.0 / D), reads=[stb, epsb], writes=[stb])
        P.add("dve", lambda e: e.reciprocal(out=st[:, 2:3], in_=st[:, 1:2]), reads=[stb], writes=[stb])
        P.add("dve", lambda e, sl=sl: e.scalar_tensor_tensor(out=xn, in0=xs[sl], scalar=st[:, 2:3], in1=prew,
                                                             op0=ALU.mult, op1=ALU.mult),
              reads=[xsb[sl], stb, prewb], writes=[xnb])
        for g in range(4):
            bank = (tt % 2) * 4 + g
            pb = psbuf(bank)

            def tr(e, g=g, bank=bank):
                ins = None
                for j in range(4):
                    kc = g * 4 + j
                    ins = e.transpose(out=ps[bank][:, j * 128:(j + 1) * 128], in_=xn[:, kc * 128:(kc + 1) * 128],
                                      identity=I_)
                return ins
            P.add("pe", tr, reads=[xnb, cfb], writes=[pb])
            dst = hT[:, g * 4:(g + 1) * 4, tt * 128:(tt + 1) * 128]
            src = ps[bank][:].rearrange("p (a b) -> p a b", a=4)
            if g % 2 == 0:
                P.add("act", lambda e, dst=dst, src=src: e.copy(out=dst, in_=src), reads=[pb], writes=[hTb[tt // 4]])
            else:
                P.add("dve", lambda e, dst=dst, src=src: e.tensor_copy(out=dst, in_=src), reads=[pb],
                      writes=[hTb[tt // 4]])
    if "hT" in dbg:
        tap("hT", hT[:], [128, KC, S], BF16, hTb)
    apos[0] = amark

    if "B" in phases:
        bmark = apos[0]
        betas = small[:, 16:144].rearrange("p (t h) -> p t h", t=NT)
        nbetas = small[:, 144:272].rearrange("p (t h) -> p t h", t=NT)
        gs = small[:, 272:400].rearrange("p (t h) -> p t h", t=NT)
        bgs = small[:, 400:528].rearrange("p (t h) -> p t h", t=NT)
        egs = small[:, 528:912].rearrange("p (t c) -> p t c", t=NT)
        negA = small[:, 912:1040]
        dtb = small[:, 1040:1168]
        gnw = small[:, 1168:1169]
        cw = small[:, 1172:1268].rearrange("p (c j) -> p c j", c=24)
        gqb = Buf("gatesq")
        cwb = Buf("cw")
        xa, xab = carve("xa", [128])
        wg, wgb = carve("wg", [KC, 16], BF16)
        load_w(wg, w_in, OFF_GB, 16, wgb)
        dma("sp", negA, a_log.partition_broadcast(128), gqb)
        dma("sp", dtb, dt_bias.partition_broadcast(128), gqb)
        dma("sp", gnw, gnw_d, gqb)
        dma("sp", cw, conv_wT.rearrange("(c p) j -> p c j", p=128), cwb)
        P.add("act", lambda e: e.activation(out=negA, in_=negA, func=AF.Exp), reads=[gqb], writes=[gqb])
        P.add("dve", lambda e: e.tensor_scalar(out=negA, in0=negA, scalar1=-1.0, scalar2=None, op0=ALU.mult),
              reads=[gqb], writes=[gqb])
        pg = psbuf(0)
        for tt in range(NT):
            def mm(e, tt=tt):
                ins = None
                for kc in range(KC):
                    ins = e.matmul(ps[0][:, tt * 16:(tt + 1) * 16], lhsT=hT[:, kc, tt * 128:(tt + 1) * 128],
                                   rhs=wg[:, kc, :], start=(kc == 0), stop=(kc == KC - 1))
                return ins
            P.add("pe", mm, reads=[hTb[tt // 4], wgb], writes=[pg])
        pgv = ps[0][:, 0:256].rearrange("p (t c) -> p t c", t=NT)
        P.add("act", lambda e: e.activation(out=betas, in_=pgv[:, :, 0:8], func=AF.Sigmoid), reads=[pg], writes=[gqb])
        P.add("dve", lambda e: e.tensor_tensor(out=xa.rearrange("p (t h) -> p t h", t=NT), in0=pgv[:, :, 8:16],
                                               in1=dtb.rearrange("p (t h) -> p t h", t=NT), op=ALU.add),
              reads=[pg, gqb], writes=[xab])
        P.add("act", lambda e: e.activation(out=xa, in_=xa, func=AF.Exp), reads=[xab], writes=[xab])
        P.add("act", lambda e: e.activation(out=xa, in_=xa, func=AF.Ln, bias=epsc[:, 1:2]), reads=[xab, epsb],
              writes=[xab])
        P.add("dve", lambda e: e.tensor_tensor(out=small[:, 272:400], in0=xa, in1=negA, op=ALU.mult),
              reads=[xab, gqb], writes=[gqb])
        P.add("dve", lambda e: e.tensor_scalar(out=small[:, 144:272], in0=small[:, 16:144], scalar1=-1.0, scalar2=None,
                                               op0=ALU.mult), reads=[gqb], writes=[gqb])
        pg2 = psbuf(1)
        for tt in range(NT):
            def mm2(e, tt=tt):
                e.matmul(ps[1][:, tt * 24:tt * 24 + 8], lhsT=cblk(C_TRI), rhs=gs[:, tt, :], start=True, stop=True)
                e.matmul(ps[1][:, tt * 24 + 8:tt * 24 + 16], lhsT=cblk(C_UG), rhs=gs[:, tt, :], start=True, stop=True)
                return e.matmul(ps[1][:, tt * 24 + 16:tt * 24 + 24], lhsT=cblk(C_ONES), rhs=gs[:, tt, :], start=True,
                                stop=True)
            P.add("pe", mm2, reads=[gqb, cfb], writes=[pg2])
        P.add("act", lambda e: e.activation(out=small[:, 528:912], in_=ps[1][:, 0:384], func=AF.Exp), reads=[pg2],
              writes=[gqb])
        P.add("dve", lambda e: e.tensor_tensor(out=bgs, in0=betas, in1=egs[:, :, 0:8], op=ALU.mult), reads=[gqb],
              writes=[gqb])
        if "gates" in dbg:
            tap("gates", small[:, 16:912], [128, 896], F32, [gqb])

        if "0" in phases:
            nheads_g = 0
        NW = 4
        wr, wrb = zip(*[carve("wr%d" % i, [KC, 128], BF16) for i in range(NW)])
        xraw, xrp = carve("xraw", [3 + S])
        xrb = subbufs(xrp, "xraw", [[(0, 12)]] + [[(12 + i * 2048, 12 + (i + 1) * 2048)] for i in range(4)])
        qT, qTp = carve("qT", [S])
        kT, kTp = carve("kT", [S])
        vT, vTp = carve("vT", [S])
        seg4 = [[(i * 2048, (i + 1) * 2048)] for i in range(4)]
        tTb = {"q": subbufs(qTp, "qT", seg4), "k": subbufs(kTp, "kT", seg4), "v": subbufs(vTp, "vT", seg4)}
        szT, szp = carve("szT", [S], BF16)
        szb = subbufs(szp, "szT", [[(i * 1024, (i + 1) * 1024)] for i in range(4)])
        sqt, sqtb = zip(*[carve("sqt%d" % i, [512]) for i in range(2)])
        rs, rsb = carve("rs", [512])
        Ssb, Sb = carve("S", [128])
        NSL = 4
        slot = []
        for i in range(NSL):
            d_ = {}
            for nm in ("Gt", "Ln", "N0", "N1", "M0", "M1", "Xs", "kbg", "vb"):
                d_[nm] = carve("%s_%d" % (nm, i), [128])
            for nm in ("iT", "kdec", "u", "wTn"):
                d_[nm] = [carve("%s_%d_%d" % (nm, i, p_), [128]) for p_ in range(2)]
            d_["EE"] = carve("EE_%d" % i, [256])
            slot.append(d_)
        vnew, vnewb = carve("vnew", [128])
        t1, t1b = carve("t1", [128])
        osb, osbb = carve("osb", [128])
        onb_, onbb = carve("on", [128])
        P.add("dve", lambda e: e.memset(xraw[:, 0:3], 0.0), writes=[xrb[0]])

        PBANKS = [0, 1, 3, 4, 5, 6]
        pl2 = psbuf(7)
        pscan = psbuf(2)
        pT = psbuf(7)

        def q4(bank, q):
            return ps[bank][:, q * 128:(q + 1) * 128]

        wcount = [0]

        def proj_fm(col0, dst_fn, pidx):
            wi = wcount[0] % NW
            wcount[0] += 1
            load_w(wr[wi], w_in, col0, 128, wrb[wi])
            for seg in range(4):
                bank = PBANKS[pidx[0] % len(PBANKS)]
                pidx[0] += 1

                def mm(e, seg=seg, bank=bank, wi=wi):
                    ins = None
                    for kc in range(KC):
                        ins = e.matmul(ps[bank][:, :], lhsT=wr[wi][:, kc, :], rhs=hT[:, kc, seg * 512:(seg + 1) * 512],
                                       start=(kc == 0), stop=(kc == KC - 1))
                    return ins
                P.add("pe", mm, reads=[hTb[seg], wrb[wi]], writes=[pbank[bank]])
                nxt = dst_fn(seg, bank)
                while l2pend:
                    l2pend.pop(0)()
                if nxt is not None:
                    l2pend.append(nxt)

        pidx = [0]
        l2pend = []
        SCQ = 128.0 ** -0.5
        for h in range(nheads_g):
            for ti, nm in enumerate("qkv"):
                ci = ti * 8 + h
                tT = {"q": qT, "k": kT, "v": vT}[nm]

                def consume(seg, bank, nm=nm, ci=ci, tT=tT):
                    s0 = seg * 512
                    P.add("act", lambda e: e.copy(out=xraw[:, 3 + s0:3 + s0 + 512], in_=ps[bank][:, :]),
                          reads=[pbank[bank]], writes=[xrb[seg + 1]])
                    dst = tT[:, s0:s0 + 512]
                    db = tTb[nm][seg]
                    P.add("dve", lambda e: e.tensor_scalar(out=dst, in0=xraw[:, 3 + s0:3 + s0 + 512],
                                                           scalar1=cw[:, ci, 3:4], scalar2=None, op0=ALU.mult),
                          reads=[xrb[seg + 1], cwb], writes=[db])
                    for j in range(3):
                        P.add("dve", lambda e, j=j: e.scalar_tensor_tensor(out=dst, in0=xraw[:, j + s0:j + s0 + 512],
                                                                           scalar=cw[:, ci, j:j + 1], in1=dst,
                                                                           op0=ALU.mult, op1=ALU.add),
                              reads=[xrb[seg + 1], xrb[seg], cwb, db], writes=[db])
                    P.add("act", lambda e: e.activation(out=dst, in_=dst, func=AF.Silu), reads=[db], writes=[db])
                    if nm in "qk":
                        si = seg % 2
                        P.add("act", lambda e: e.activation(out=sqt[si], in_=dst, func=AF.Square), reads=[db],
                              writes=[sqtb[si]])

                        def tail():
                            P.add("pe", lambda e: e.matmul(ps[7][:, :], lhsT=cblk(C_ONES), rhs=sqt[si], start=True,
                                                           stop=True), reads=[sqtb[si], cfb], writes=[pl2])
                            P.add("act", lambda e: e.activation(out=rs, in_=ps[7][:, :], func=AF.Sqrt,
                                                                bias=epsc[:, 0:1]), reads=[pl2, epsb], writes=[rsb])
                            P.add("dve", lambda e: e.reciprocal(out=rs, in_=rs), reads=[rsb], writes=[rsb])
                            if nm == "q":
                                P.add("dve", lambda e: e.scalar_tensor_tensor(out=dst, in0=dst, scalar=SCQ, in1=rs,
                                                                              op0=ALU.mult, op1=ALU.mult),
                                      reads=[db, rsb], writes=[db])
                            else:
                                P.add("dve", lambda e: e.tensor_tensor(out=dst, in0=dst, in1=rs, op=ALU.mult),
                                      reads=[db, rsb], writes=[db])
                        return tail
                    return None
                proj_fm([OFF_GQ, OFF_GK, OFF_GV][ti] + h * 128, consume, pidx)

            def consume_z(seg, bank):
                s0 = seg * 512
                P.add("act", lambda e: e.activation(out=szT[:, s0:s0 + 512], in_=ps[bank][:, :], func=AF.Silu),
                      reads=[pbank[bank]], writes=[szb[seg]])
            proj_fm(OFF_GZ + h * 128, consume_z, pidx)
            while l2pend:
                l2pend.pop(0)()
            if "qkv" in dbg and h == 0:
                tap("qT", qT, [128, S], F32, tTb["q"])
                tap("kT", kT, [128, S], F32, tTb["k"])
                tap("vT", vT, [128, S], F32, tTb["v"])

            P.add("dve", lambda e: e.memset(Ssb, 0.0), writes=[Sb])

            def pre_stages(tt, sl, par, h=h):
                sd = dict(slot[sl])
                for nm_ in ("iT", "kdec", "u", "wTn"):
                    sd[nm_] = slot[sl][nm_][par]
                ts = slice(tt * 128, (tt + 1) * 128)
                sg = tt // 4
                bk = 3 + sl
                pb = pbank[bk]
                stages = []

                def s0():
                    def trKV(e):
                        e.matmul(q4(bk, 0), lhsT=kT[:, ts], rhs=I_, start=True, stop=True)
                        return e.matmul(q4(bk, 1), lhsT=vT[:, ts], rhs=I_, start=True, stop=True)
                    P.add("pe", trKV, reads=[tTb["k"][sg], tTb["v"][sg], cfb], writes=[pb])
                    P.add("dve", lambda e: e.tensor_scalar(out=sd["kbg"][0], in0=q4(bk, 0), scalar1=bgs[:, tt, h:h + 1],
                                                           scalar2=None, op0=ALU.mult),
                          reads=[pb, gqb], writes=[sd["kbg"][1]])
                    P.add("dve", lambda e: e.tensor_scalar(out=sd["kdec"][0], in0=q4(bk, 0),
                                                           scalar1=egs[:, tt, 8 + h:9 + h], scalar2=None, op0=ALU.mult),
                          reads=[pb, gqb], writes=[sd["kdec"][1]])
                    P.add("dve", lambda e: e.tensor_scalar(out=sd["vb"][0], in0=q4(bk, 1), scalar1=betas[:, tt, h:h + 1],
                                                           scalar2=None, op0=ALU.mult),
                          reads=[pb, gqb], writes=[sd["vb"][1]])
                    P.add("dve", lambda e: e.tensor_scalar(out=sd["Gt"][0], in0=cblk(C_TRI), scalar1=gs[:, tt, h:h + 1],
                                                           scalar2=None, op0=ALU.mult),
                          reads=[cfb, gqb], writes=[sd["Gt"][1]])
                stages.append(s0)

                def s1():
                    def mmD(e):
                        e.matmul(q4(bk, 2), lhsT=sd["Gt"][0], rhs=cblk(C_UG), start=True, stop=True)
                        e.matmul(q4(bk, 3), lhsT=cblk(C_UG), rhs=sd["Gt"][0], start=True, stop=False)
                        e.matmul(q4(bk, 3), lhsT=I_, rhs=cblk(C_NBI), start=False, stop=True)
                        e.matmul(q4(bk, 0), lhsT=kT[:, ts], rhs=kT[:, ts], start=True, stop=True)
                        return e.matmul(q4(bk, 1), lhsT=kT[:, ts], rhs=qT[:, ts], start=True, stop=True)
                    P.add("pe", mmD, reads=[sd["Gt"][1], cfb, tTb["k"][sg], tTb["q"][sg]], writes=[pb])
                    P.add("act", lambda e: e.activation(out=sd["EE"][0], in_=ps[bk][:, 256:512], func=AF.Exp),
                          reads=[pb], writes=[sd["EE"][1]])
                    P.add("dve", lambda e: e.scalar_tensor_tensor(out=sd["Ln"][0], in0=q4(bk, 0),
                                                                  scalar=nbetas[:, tt, h:h + 1], in1=sd["EE"][0][:, 0:128],
                                                                  op0=ALU.mult, op1=ALU.mult),
                          reads=[pb, gqb, sd["EE"][1]], writes=[sd["Ln"][1]])
                    P.add("dve", lambda e: e.tensor_tensor(out=sd["iT"][0], in0=q4(bk, 1), in1=sd["EE"][0][:, 128:256],
                                                           op=ALU.mult),
                          reads=[pb, sd["EE"][1]], writes=[sd["iT"][1]])
                stages.append(s1)

                cur = {"N": (I_, cfb), "M": (I_, cfb)}
                for l in range(7):
                    def sA(l=l):
                        Ncur, Nb = cur["N"]
                        P.add("pe", lambda e: e.matmul(q4(bk, 0), lhsT=sd["Ln"][0], rhs=Ncur, start=True, stop=True),
                              reads=[sd["Ln"][1], Nb], writes=[pb])
                        P.add("dve", lambda e: e.tensor_tensor(out=sd["Xs"][0], in0=q4(bk, 0), in1=cblk(C_MU0 + l),
                                                               op=ALU.mult),
                              reads=[pb, cfb], writes=[sd["Xs"][1]])
                    stages.append(sA)

                    def sB(l=l):
                        Ncur, Nb = cur["N"]
                        Mcur, Mb = cur["M"]
                        Nn = sd["N%d" % (l % 2)]
                        Mn = sd["M%d" % (l % 2)]

                        def mmNM(e):
                            ins = None
                            if l > 0:
                                ins = e.matmul(q4(bk, 1), lhsT=Mcur, rhs=sd["Xs"][0], start=True, stop=True)
                            if l < 6:
                                ins = e.matmul(q4(bk, 2), lhsT=sd["Xs"][0], rhs=Mcur, start=True, stop=True)
                            return ins
                        P.add("pe", mmNM, reads=[Nb, Mb, sd["Xs"][1], cfb], writes=[pb])
                        if l > 0:
                            P.add("dve", lambda e: e.tensor_tensor(out=Nn[0], in0=q4(bk, 1), in1=Ncur, op=ALU.add),
                                  reads=[pb, Nb], writes=[Nn[1]])
                        else:
                            P.add("dve", lambda e: e.tensor_tensor(out=Nn[0], in0=sd["Xs"][0], in1=Ncur, op=ALU.add),
                                  reads=[sd["Xs"][1], Nb], writes=[Nn[1]])
                        if l < 6:
                            P.add("dve", lambda e: e.tensor_tensor(out=Mn[0], in0=q4(bk, 2), in1=Mcur, op=ALU.add),
                                  reads=[pb, Mb], writes=[Mn[1]])
                        cur["N"] = Nn
                        cur["M"] = Mn
                    stages.append(sB)

                def sF():
                    Nf, Nfb = cur["N"]

                    def mmUW(e):
                        e.matmul(q4(bk, 0), lhsT=Nf, rhs=sd["vb"][0], start=True, stop=True)
                        return e.matmul(q4(bk, 1), lhsT=sd["kbg"][0], rhs=Nf, start=True, stop=True)
                    P.add("pe", mmUW, reads=[Nfb, sd["vb"][1], sd["kbg"][1]], writes=[pb])
                    P.add("act", lambda e: e.copy(out=sd["u"][0], in_=q4(bk, 0)), reads=[pb], writes=[sd["u"][1]])
                    P.add("act", lambda e: e.mul(out=sd["wTn"][0], in_=q4(bk, 1), mul=-1.0), reads=[pb],
                          writes=[sd["wTn"][1]])
                stages.append(sF)
                return stages

            def scan(tt, sl, par, h=h):
                sd = dict(slot[sl])
                for nm_ in ("iT", "kdec", "u", "wTn"):
                    sd[nm_] = slot[sl][nm_][par]
                ts = slice(tt * 128, (tt + 1) * 128)
                sg = tt // 4
                subs = []

                def subA():
                    P.add("pe", mmV, reads=[sd["u"][1], sd["wTn"][1], Sb, cfb], writes=[pscan])
                    P.add("dve", lambda e: e.tensor_tensor(out=vnew, in0=q4(2, 0), in1=sd["u"][0], op=ALU.add),
                          reads=[pscan, sd["u"][1]], writes=[vnewb])

                def subB():
                    P.add("pe", mmO, reads=[tTb["q"][sg], Sb, sd["iT"][1], sd["kdec"][1], vnewb], writes=[pscan])
                    P.add("act", lambda e: e.mul(out=t1, in_=q4(2, 1), mul=egs[:, tt, h:h + 1]), reads=[pscan, gqb],
                          writes=[t1b])
                    P.add("dve", lambda e: e.tensor_tensor(out=osb, in0=t1, in1=q4(2, 2), op=ALU.add),
                          reads=[t1b, pscan], writes=[osbb])
                    P.add("dve", lambda e: e.scalar_tensor_tensor(out=Ssb, in0=Ssb, scalar=egs[:, tt, 16 + h:17 + h],
                                                                  in1=q4(2, 3), op0=ALU.mult, op1=ALU.add),
                          reads=[Sb, gqb, pscan], writes=[Sb])

                def subC():
                    P.add("act", lambda e: e.activation(out=t1, in_=osb, func=AF.Square, accum_out=st[:, 4:5]),
                          reads=[osbb], writes=[t1b, stb])
                    P.add("act", lambda e: e.activation(out=st[:, 5:6], in_=st[:, 4:5], func=AF.Sqrt, bias=epsc[:, 0:1],
                                                        scale=1.0 / 128), reads=[stb, epsb], writes=[stb])
                    P.add("dve", lambda e: e.reciprocal(out=st[:, 6:7], in_=st[:, 5:6]), reads=[stb], writes=[stb])
                    P.add("act", lambda e: e.mul(out=onb_, in_=osb, mul=st[:, 6:7]), reads=[osbb, stb], writes=[onbb])

                def subD():
                    P.add("pe", lambda e: e.matmul(q4(7, 0), lhsT=onb_, rhs=I_, start=True, stop=True),
                          reads=[onbb, cfb], writes=[pT])
                    P.add("dve", lambda e: e.scalar_tensor_tensor(out=oaT[:, h, ts], in0=q4(7, 0), scalar=gnw,
                                                                  in1=szT[:, ts], op0=ALU.mult, op1=ALU.mult),
                          reads=[pT, gqb, szb[sg]], writes=[oaTb[sg]])

                def mmV(e):
                    return e.matmul(q4(2, 0), lhsT=sd["wTn"][0], rhs=Ssb, start=True, stop=True)

                def mmO(e):
                    e.matmul(q4(2, 1), lhsT=qT[:, ts], rhs=Ssb, start=True, stop=True)
                    e.matmul(q4(2, 2), lhsT=sd["iT"][0], rhs=vnew, start=True, stop=True)
                    return e.matmul(q4(2, 3), lhsT=sd["kdec"][0], rhs=vnew, start=True, stop=True)
                return [subA, subB, subC, subD]

            pend = []
            for gi, g0 in enumerate(range(0, NT, NSL)):
                sts = [pre_stages(g0 + i, i, gi % 2) for i in range(NSL)]
                for k in range(len(sts[0])):
                    for i in range(NSL):
                        sts[i][k]()
                    if pend:
                        pend.pop(0)()
                while pend:
                    pend.pop(0)()
                for i in range(NSL):
                    pend.extend(scan(g0 + i, i, gi % 2))
            while pend:
                pend.pop(0)()
        print("phase B arena bytes", apos[0])
        if "oaT" in dbg:
            tap("oaT", oaT[:], [128, H, S], BF16, oaTb)
        if "scan" in dbg:
            tap("osb", osb, [128, 128], F32, [osbb])
            tap("S", Ssb, [128, 128], F32, [Sb])
            tap("vnew", vnew, [128, 128], F32, [vnewb])
        apos[0] = bmark

    obT, obTp = carve("obT", [H, S], BF16)
    obTb = subbufs(obTp, "obT", [[(hh * 4096 + q * 1024, hh * 4096 + (q + 1) * 1024) for hh in range(H)] for q in range(4)])
    cmark = apos[0]
    if "C" in phases:
        NWC = 4
        cwr, cwrb = zip(*[carve("cwr%d" % i, [KC, 128], BF16) for i in range(NWC)])
        cmf, cmfb = carve("cmf", [1280])
        cmb, cmbb = carve("cmb", [1536], BF16)
        dma("sp", cmf, cm_d, cmfb)
        if "7" not in phases:
            P.add("dve", lambda e: e.tensor_copy(out=cmb[:, 0:1280], in_=cmf), reads=[cmfb], writes=[cmbb])
        P.add("dve", lambda e: e.tensor_copy(out=cmb[:, 1280:1408], in_=cblk(C_ID)), reads=[cfb], writes=[cmbb])
        P.add("dve", lambda e: e.tensor_copy(out=cmb[:, 1408:1536], in_=cblk(C_ONES)), reads=[cfb], writes=[cmbb])
        cbias = cmb[:, 0:256]
        Ib = cmb[:, 1280:1408]
        onesb = cmb[:, 1408:1536]
        qTb_, qTbb = carve("mqTb", [S], BF16)
        kTb_, kTbb = carve("mkTb", [S], BF16)
        qTf, qTfb = carve("mqTf", [S])
        Vt, Vtb = carve("mV", [NT, 128], BF16)
        mszT, mszb = carve("mszT", [S], BF16)
        nbT, nbTb = carve("nbT", [S], BF16)
        kmT, kmTb = carve("kmT", [8])
        gm, gmb = carve("gm", [NT, 8])
        nb, nbb = carve("nb", [NT, 8])
        top8, top8b = carve("top8", [8])
        pTr, pTrb = zip(*[carve("pT%d" % i, [256], BF16) for i in range(4)])
        rr, rrb = carve("rr", [256])
        zz, zzb = carve("zz", [256])
        cwcount = [0]

        def proj_c(col0, consume):
            wi = cwcount[0] % NWC
            cwcount[0] += 1
            load_w(cwr[wi], w_in, col0, 128, cwrb[wi])
            for seg in range(4):
                bank = seg % 2

                def mm(e, seg=seg, bank=bank, wi=wi):
                    ins = None
                    for kc in range(KC):
                        ins = e.matmul(ps[bank][:, :], lhsT=cwr[wi][:, kc, :], rhs=hT[:, kc, seg * 512:(seg + 1) * 512],
                                       start=(kc == 0), stop=(kc == KC - 1))
                    return ins
                P.add("pe", mm, reads=[hTb[seg], cwrb[wi]], writes=[pbank[bank]])
                consume(seg, bank)

        for h in range(nheads_m):
            def cons_q(seg, bank):
                s0 = seg * 512
                P.add("act", lambda e: e.mul(out=qTf[:, s0:s0 + 512], in_=ps[bank][:, :], mul=SCQ_M),
                      reads=[pbank[bank]], writes=[qTfb])
                P.add("dve", lambda e: e.tensor_copy(out=qTb_[:, s0:s0 + 512], in_=qTf[:, s0:s0 + 512]),
                      reads=[qTfb], writes=[qTbb])
            SCQ_M = 128.0 ** -0.5
            proj_c(OFF_MQ + h * 128, cons_q)

            def cons_k(seg, bank):
                s0 = seg * 512
                P.add("act", lambda e: e.copy(out=kTb_[:, s0:s0 + 512], in_=ps[bank][:, :]),
                      reads=[pbank[bank]], writes=[kTbb])
                P.add("dve", lambda e: e.tensor_reduce(out=kmT[:, seg * 2:seg * 2 + 2],
                                                       in_=ps[bank][:, :].rearrange("p (a b) -> p a b", a=2),
                                                       axis=AX.X, op=ALU.add),
                      reads=[pbank[bank]], writes=[kmTb])
            proj_c(OFF_MK + h * 128, cons_k)

            def cons_z(seg, bank):
                s0 = seg * 512
                P.add("act", lambda e: e.activation(out=mszT[:, s0:s0 + 512], in_=ps[bank][:, :], func=AF.Silu),
                      reads=[pbank[bank]], writes=[mszb])
            proj_c(OFF_MZ + h * 128, cons_z)

            wi = cwcount[0] % NWC
            cwcount[0] += 1
            load_w(cwr[wi], w_in, OFF_MV + h * 128, 128, cwrb[wi])
            for g in range(4):
                bank = 2 + g % 2

                def mmv(e, g=g, bank=bank, wi=wi):
                    ins = None
                    for j in range(4):
                        tt = g * 4 + j
                        for kc in range(KC):
                            ins = e.matmul(ps[bank][:, j * 128:(j + 1) * 128], lhsT=hT[:, kc, tt * 128:(tt + 1) * 128],
                                           rhs=cwr[wi][:, kc, :], start=(kc == 0), stop=(kc == KC - 1))
                    return ins
                P.add("pe", mmv, reads=[hTb[g], cwrb[wi]], writes=[pbank[bank]])
                P.add("act", lambda e, g=g, bank=bank: e.copy(out=Vt[:, g * 4:(g + 1) * 4, :],
                                                              in_=ps[bank][:, :].rearrange("p (a b) -> p a b", a=4)),
                      reads=[pbank[bank]], writes=[Vtb])

            P.add("dve", lambda e: e.memset(gm, -1e30), writes=[gmb])
            P.add("dve", lambda e: e.memset(nb, 0.0), writes=[nbb])

            def mmg(e):
                ins = None
                for tt in range(8, NT):
                    ins = e.matmul(ps[4][:, tt * 8:tt * 8 + 8], lhsT=qTf[:, tt * 128:(tt + 1) * 128], rhs=kmT,
                                   start=True, stop=True)
                return ins
            P.add("pe", mmg, reads=[qTfb, kmTb], writes=[pbank[4]])
            for tt in range(8, NT):
                qb = tt // 2
                P.add("dve", lambda e, tt=tt, qb=qb: e.tensor_copy(out=gm[:, tt, 0:qb], in_=ps[4][:, tt * 8:tt * 8 + qb]),
                      reads=[pbank[4]], writes=[gmb])
                P.add("dve", lambda e, tt=tt: e.max(out=top8, in_=gm[:, tt, :]), reads=[gmb], writes=[top8b])
                P.add("dve", lambda e, tt=tt: e.tensor_scalar(out=nb[:, tt, :], in0=gm[:, tt, :], scalar1=top8[:, 2:3],
                                                              scalar2=NEG, op0=ALU.is_lt, op1=ALU.mult),
                      reads=[gmb, top8b], writes=[nbb])
            for g in range(4):
                bank = 5 + g % 2

                def mmt(e, g=g, bank=bank):
                    ins = None
                    for j in range(4):
                        tt = g * 4 + j
                        ins = e.matmul(ps[bank][0:8, j * 128:(j + 1) * 128], lhsT=nb[:, tt, :], rhs=I_, start=True,
                                       stop=True)
                    return ins
                P.add("pe", mmt, reads=[nbb, cfb], writes=[pbank[bank]])
                P.add("act", lambda e, g=g, bank=bank: e.copy(out=nbT[0:8, g * 512:(g + 1) * 512], in_=ps[bank][0:8, :]),
                      reads=[pbank[bank]], writes=[nbTb])
            if "moba_pre" in dbg and h == 0:
                tap("nbT", nbT[0:8, :], [8, S], BF16, [nbTb])
                tap("mqT", qTf, [128, S], F32, [qTfb])
                tap("mV", Vt, [128, NT, 128], BF16, [Vtb])

            for qb in range(8):
                q0 = qb * 256
                nk = 2 * qb + 2
                ob_bank = 6 + qb % 2
                pob = pbank[ob_bank]

                def emit_S(kt, qb=qb, q0=q0):
                    n = kt // 2
                    sbank = kt % 4
                    if kt == 2 * qb + 1:
                        def mm(e):
                            e.matmul(ps[sbank][:, 128:256], lhsT=kTb_[:, kt * 128:(kt + 1) * 128],
                                     rhs=qTb_[:, q0 + 128:q0 + 256], start=True, stop=False)
                            return e.matmul(ps[sbank][:, 128:256], lhsT=Ib, rhs=cbias[:, 0:128], start=False, stop=True)
                    elif kt == 2 * qb:
                        def mm(e):
                            e.matmul(ps[sbank][:, 0:256], lhsT=kTb_[:, kt * 128:(kt + 1) * 128], rhs=qTb_[:, q0:q0 + 256],
                                     start=True, stop=False)
                            return e.matmul(ps[sbank][:, 0:256], lhsT=Ib, rhs=cbias, start=False, stop=True)
                    elif qb <= 3:
                        def mm(e):
                            return e.matmul(ps[sbank][:, 0:256], lhsT=kTb_[:, kt * 128:(kt + 1) * 128],
                                            rhs=qTb_[:, q0:q0 + 256], start=True, stop=True)
                    else:
                        def mm(e):
                            e.matmul(ps[sbank][:, 0:256], lhsT=kTb_[:, kt * 128:(kt + 1) * 128], rhs=qTb_[:, q0:q0 + 256],
                                     start=True, stop=False)
                            return e.matmul(ps[sbank][:, 0:256], lhsT=cmb[0:8, 256 + n * 128:256 + (n + 1) * 128],
                                            rhs=nbT[0:8, q0:q0 + 256], start=False, stop=True)
                    P.add("pe", mm, reads=[kTbb, qTbb, cmbb, nbTb], writes=[pbank[sbank]])

                def emit_PV(kt, qb=qb, q0=q0, nk=nk, ob_bank=ob_bank, pob=pob):
                    sbank = kt % 4
                    c0 = 128 if kt == 2 * qb + 1 else 0
                    pt = pTr[kt % 4]
                    P.add("act", lambda e: e.activation(out=pt[:, c0:256], in_=ps[sbank][:, c0:256], func=AF.Exp),
                          reads=[pbank[sbank]], writes=[pTrb[kt % 4]])

                    def mm(e):
                        e.matmul(ps[ob_bank][:, c0:256], lhsT=Vt[:, kt, :], rhs=pt[:, c0:256], start=(kt == 0),
                                 stop=(kt == nk - 1))
                        return e.matmul(ps[ob_bank][:, 256 + c0:512], lhsT=onesb, rhs=pt[:, c0:256], start=False,
                                        stop=(kt == nk - 1), skip_group_check=True)
                    P.add("pe", mm, reads=[Vtb, pTrb[kt % 4], cmbb], writes=[pob])

                emit_S(0)
                emit_S(1)
                for kt in range(nk):
                    emit_PV(kt)
                    if kt + 2 < nk:
                        emit_S(kt + 2)
                P.add("dve", lambda e, ob_bank=ob_bank: e.reciprocal(out=rr, in_=ps[ob_bank][:, 256:512]),
                      reads=[pob], writes=[rrb])
                P.add("dve", lambda e, q0=q0: e.tensor_tensor(out=zz, in0=rr, in1=mszT[:, q0:q0 + 256], op=ALU.mult),
                      reads=[rrb, mszb], writes=[zzb])
                P.add("dve", lambda e, q0=q0, ob_bank=ob_bank, h=h: e.tensor_tensor(out=obT[:, h, q0:q0 + 256],
                                                                                   in0=ps[ob_bank][:, 0:256], in1=zz,
                                                                                   op=ALU.mult),
                      reads=[pob, zzb], writes=[obTb[qb // 2]])
        if "obT" in dbg:
            tap("obT", obT, [128, H, S], BF16, obTb)
    apos[0] = cmark

    if "D" in phases:
        dmark = apos[0]
        mT1, mT1p = carve("mT1", [KC, 1024], BF16)
        mT1b = subbufs(mT1p, "mT1", [[(c * 2048 + q * 1024, c * 2048 + (q + 1) * 1024) for c in range(KC)] for q in range(2)])
        ymark = apos[0]
        NWD = 2
        wga, wgab = zip(*[carve("wga%d" % i, [KC, 128], BF16) for i in range(NWD)])
        wgb_, wgbb = zip(*[carve("wgb%d" % i, [KC, 128], BF16) for i in range(NWD)])
        wba, wbab = zip(*[carve("wba%d" % i, [8, 128], BF16) for i in range(NWD)])
        wbb, wbbb = zip(*[carve("wbb%d" % i, [8, 128], BF16) for i in range(NWD)])
        sga, sgab = zip(*[carve("sga%d" % i, [512]) for i in range(2)])
        sgb, sgbb = zip(*[carve("sgb%d" % i, [512]) for i in range(2)])
        it = 0
        for hf in range(2):
            for c in range(KC):
                wi = it % NWD
                load_w(wga[wi], w_in, OFF_GATE_A + c * 128, 128, wgab[wi])
                load_w(wgb_[wi], w_in, OFF_GATE_B + c * 128, 128, wgbb[wi])
                load_w(wba[wi], w_a, c * 128, 128, wbab[wi])
                load_w(wbb[wi], w_b, c * 128, 128, wbbb[wi])
                for seg in range(2):
                    t0 = hf * 1024 + seg * 512
                    hq = t0 // 512
                    bs = (it * 2 + seg) % 2 * 4
                    si = seg

                    def mmg(e, w, bank, t0=t0):
                        ins = None
                        for kc in range(KC):
                            ins = e.matmul(ps[bank][:, :], lhsT=w[:, kc, :], rhs=hT[:, kc, t0:t0 + 512], start=(kc == 0),
                                           stop=(kc == KC - 1))
                        return ins

                    def mmb(e, w, src, bank, t0=t0):
                        ins = None
                        for kc in range(8):
                            ins = e.matmul(ps[bank][:, :], lhsT=w[:, kc, :], rhs=src[:, kc, t0:t0 + 512], start=(kc == 0),
                                           stop=(kc == 7))
                        return ins
                    P.add("pe", lambda e, wi=wi, bs=bs, mmg=mmg: mmg(e, wga[wi], bs), reads=[hTb[hq], wgab[wi]],
                          writes=[pbank[bs]])
                    P.add("act", lambda e, bs=bs, si=si: e.activation(out=sga[si], in_=ps[bs][:, :], func=AF.Sigmoid),
                          reads=[pbank[bs]], writes=[sgab[si]])
                    P.add("pe", lambda e, wi=wi, bs=bs, mmg=mmg: mmg(e, wgb_[wi], bs + 1), reads=[hTb[hq], wgbb[wi]],
                          writes=[pbank[bs + 1]])
                    P.add("act", lambda e, bs=bs, si=si: e.activation(out=sgb[si], in_=ps[bs + 1][:, :], func=AF.Sigmoid),
                          reads=[pbank[bs + 1]], writes=[sgbb[si]])
                    P.add("pe", lambda e, wi=wi, bs=bs, mmb=mmb: mmb(e, wba[wi], oaT, bs + 2), reads=[oaTb[hq], wbab[wi]],
                          writes=[pbank[bs + 2]])
                    P.add("dve", lambda e, bs=bs, si=si: e.tensor_tensor(out=sga[si], in0=ps[bs + 2][:, :], in1=sga[si],
                                                                         op=ALU.mult),
                          reads=[pbank[bs + 2], sgab[si]], writes=[sgab[si]])
                    P.add("pe", lambda e, wi=wi, bs=bs, mmb=mmb: mmb(e, wbb[wi], obT, bs + 3), reads=[obTb[hq], wbbb[wi]],
                          writes=[pbank[bs + 3]])
                    P.add("dve", lambda e, bs=bs, si=si: e.tensor_tensor(out=sgb[si], in0=ps[bs + 3][:, :], in1=sgb[si],
                                                                         op=ALU.mult),
                          reads=[pbank[bs + 3], sgbb[si]], writes=[sgbb[si]])
                    if hf == 0:
                        dst = mT1[:, c, seg * 512:(seg + 1) * 512]
                        dbuf = mT1b[seg]
                    else:
                        dst = hT[:, c, seg * 512:(seg + 1) * 512]
                        dbuf = hTb[seg]
                    P.add("dve", lambda e, si=si, dst=dst: e.tensor_tensor(out=dst, in0=sga[si], in1=sgb[si], op=ALU.add),
                          reads=[sgab[si], sgbb[si]], writes=[dbuf])
                it += 1
        if "mT" in dbg:
            tap("mT1", mT1, [128, KC, 1024], BF16, mT1b)
            tap("mT2", hT[:, :, 0:1024], [128, KC, 1024], BF16, hTb[0:2])
        apos[0] = ymark
        outsl = [Buf("outdram_s%d" % i) for i in range(2)]
        postw, postwb = carve("postw", [D])
        xn_junk, xnjb = carve("xnj", [D], BF16)
        xt, xtb = zip(*[carve("xt%d" % i, [D]) for i in range(2)])
        ysb, ysbb = carve("ysb", [D])
        dma("sp", postw, post_w.partition_broadcast(128), postwb)
        for q in range(4):
            dma("pool", oaT[:, :, q * 512:(q + 1) * 512],
                w_out[0:1024, q * 512:(q + 1) * 512].rearrange("(kc p) c -> p kc c", p=128), oaTb[q])
            dma("pool", obT[:, :, q * 512:(q + 1) * 512],
                w_out[1024:2048, q * 512:(q + 1) * 512].rearrange("(kc p) c -> p kc c", p=128), obTb[q])
        for tt in range(NT):
            sl = tt % 2
            dma("sp", xt[sl], x[tt * 128:(tt + 1) * 128, :], xtb[sl])
            for ct in range(4):
                bank = (tt % 2) * 4 + ct

                def mmy(e, tt=tt, ct=ct, bank=bank):
                    ins = None
                    for kc in range(KC):
                        if tt < 8:
                            lt = mT1[:, kc, tt * 128:(tt + 1) * 128]
                        else:
                            lt = hT[:, kc, (tt - 8) * 128:(tt - 7) * 128]
                        wsrc = oaT if kc < 8 else obT
                        ins = e.matmul(ps[bank][:, :], lhsT=lt, rhs=wsrc[:, kc % 8, ct * 512:(ct + 1) * 512],
                                       start=(kc == 0), stop=(kc == KC - 1))
                    return ins
                mb = mT1b[tt // 4] if tt < 8 else hTb[(tt - 8) // 4]
                P.add("pe", mmy, reads=[mb, oaTb[ct], obTb[ct]], writes=[pbank[bank]])
                P.add("act", lambda e, ct=ct, bank=bank: e.copy(out=ysb[:, ct * 512:(ct + 1) * 512], in_=ps[bank][:, :]),
                      reads=[pbank[bank]], writes=[ysbb])
            P.add("act", lambda e, sl=sl: e.activation(out=xn_junk, in_=ysb, func=AF.Square, accum_out=st[:, 0:1]),
                  reads=[ysbb], writes=[xnjb, stb])
            P.add("act", lambda e: e.activation(out=st[:, 1:2], in_=st[:, 0:1], func=AF.Sqrt, bias=epsc[:, 0:1],
                                                scale=1.0 / D), reads=[stb, epsb], writes=[stb])
            P.add("dve", lambda e: e.reciprocal(out=st[:, 2:3], in_=st[:, 1:2]), reads=[stb], writes=[stb])
            P.add("dve", lambda e: e.scalar_tensor_tensor(out=ysb, in0=ysb, scalar=st[:, 2:3], in1=postw, op0=ALU.mult,
                                                          op1=ALU.mult), reads=[ysbb, stb, postwb], writes=[ysbb])
            P.add("dve", lambda e, sl=sl: e.tensor_tensor(out=xt[sl], in0=xt[sl], in1=ysb, op=ALU.add),
                  reads=[xtb[sl], ysbb], writes=[xtb[sl]])
            dma("sp", out[tt * 128:(tt + 1) * 128, :], xt[sl], outsl[sl], reads=[xtb[sl]])
        apos[0] = dmark

    P.add("sp", lambda e: e.nop(), reads=[outb] + (outsl if "D" in phases else []))
    P.finalize()
    return nc, dbg_outs


def _in_maps(x, pre_norm_w, w_in, conv_w, a_log, dt_bias, gdn_norm_w, w_branch_a, w_branch_b, w_out, post_norm_w):
    cf, cm = make_consts()
    f = lambda a: np.ascontiguousarray(np.asarray(a, dtype=np.float32))
    shared = {
        "pre_w": f(pre_norm_w[0][None, :]),
        "post_w": f(post_norm_w[0][None, :]),
        "w_in": f(w_in[0]),
        "conv_wT": f(np.asarray(conv_w[0]).T),
        "a_log16": f(np.tile(np.asarray(a_log[0]), 16)[None, :]),
        "dt_bias16": f(np.tile(np.asarray(dt_bias[0]), 16)[None, :]),
        "gnw": f(np.asarray(gdn_norm_w[0])[:, None]),
        "w_a": f(w_branch_a[0]),
        "w_b": f(w_branch_b[0]),
        "w_out": f(w_out[0]),
        "cf": cf,
        "cm": cm,
    }
    return [dict(shared, x=f(x[b])) for b in range(x.shape[0])]


def kernel(**inputs):
    maps = _in_maps(**inputs)
    import os
    nc, _ = build(phases=os.environ.get("KPHASES", "ABCD"))
    res = run_bass_kernel_spmd(nc, maps, core_ids=list(range(len(maps))))
    return np.stack([np.asarray(r["out"], dtype=np.float32) for r in res.results], axis=0)
```

```python
from contextlib import ExitStack

import numpy as np
import concourse.bass as bass
import concourse.mybir as mybir
from concourse.bass_utils import run_bass_kernel_spmd

F32 = mybir.dt.float32
BF16 = mybir.dt.bfloat16
AF = mybir.ActivationFunctionType
ALU = mybir.AluOpType
AX = mybir.AxisListType

S = 2048
D = 2048
NT = S // 128
KC = D // 128
H = 8
IN_W = 12304
EPS = 1e-6
NEG = -30000.0
STAGE_LIMIT = 1000
PRE_TAPS = ("u", "wTn", "iT", "Ln", "kdec", "N0")

OFF_GQ, OFF_GK, OFF_GV = 0, 1024, 2048
OFF_GZ = 3072
OFF_GB = 4096
OFF_GA = 4104
OFF_MQ, OFF_MK, OFF_MV = 4112, 4112 + 1024, 4112 + 2048
OFF_MZ = 4112 + 3072
OFF_GATE_A = 8208
OFF_GATE_B = 8208 + 2048

C_ID, C_TRI, C_UG, C_ONES, C_NBS, C_NBI, C_MU0 = 0, 1, 2, 3, 4, 5, 6
NCB = 13


def make_consts():
    r = np.arange(128)[:, None]
    c = np.arange(128)[None, :]
    blocks = [None] * NCB
    blocks[C_ID] = (r == c)
    blocks[C_TRI] = (r <= c)
    blocks[C_UG] = (r > c)
    blocks[C_ONES] = np.ones((128, 128), bool)
    nbs = np.where(r > c, 0.0, NEG)
    nbi = np.where(r <= c, 0.0, NEG)
    out = []
    for k in range(NCB):
        if k == C_NBS:
            out.append(nbs.astype(np.float32))
        elif k == C_NBI:
            out.append(nbi.astype(np.float32))
        elif k >= C_MU0:
            l = k - C_MU0
            b = 1 << l
            m = ((r // (2 * b)) == (c // (2 * b))) & ((r % (2 * b)) < b) & ((c % (2 * b)) >= b)
            out.append(m.astype(np.float32))
        else:
            out.append(blocks[k].astype(np.float32))
    cf = np.concatenate(out, axis=1)
    cb = np.zeros((128, 256), np.float32)
    cb[:, :128] = nbi
    es = np.zeros((128, 8 * 128), np.float32)
    for n in range(8):
        es[n, n * 128:(n + 1) * 128] = 1.0
    return np.ascontiguousarray(cf), np.ascontiguousarray(np.concatenate([cb, es], axis=1))


class Buf:
    __slots__ = ("name", "last_w", "readers", "sem", "dcnt", "iv", "ov", "excl")
    REG = []

    def __init__(self, name, iv=None, excl=False):
        self.excl = excl
        self.name = name
        self.last_w = None
        self.readers = []
        self.sem = None
        self.dcnt = 0
        if iv is not None and not isinstance(iv, list):
            iv = [iv]
        self.iv = iv
        self.ov = [self]
        if iv is not None:
            for o in Buf.REG:
                if any(a[0] == b[0] and a[1] < b[2] and b[1] < a[2] for a in iv for b in o.iv):
                    o.ov.append(self)
                    self.ov.append(o)
            Buf.REG.append(self)


class Op:
    __slots__ = ("eng", "fn", "deps", "ndma", "sig", "sigval", "wbuf")


class Prog:
    ENGS = ("pe", "act", "dve", "pool", "sp")

    def __init__(self, nc, stack):
        self.nc = nc
        self.stack = stack
        self.ops = []
        self.dma_bufs = []

    def add(self, eng, fn, reads=(), writes=(), ndma=0, waw=True):
        idx = len(self.ops)
        deps = set()
        for b0 in reads:
            for b in b0.ov:
                if b.last_w is not None:
                    deps.add(b.last_w)
                if b.excl:
                    for r in b.readers:
                        if self.ops[r].eng != eng:
                            deps.add(r)
        for b0 in writes:
            for b in b0.ov:
                if b.last_w is not None and (waw or b is not b0):
                    deps.add(b.last_w)
                deps.update(b.readers)
        deps.discard(idx)
        for b in reads:
            b.readers.append(idx)
        for b in writes:
            b.last_w = idx
            b.readers = []
        op = Op()
        op.eng = eng
        op.fn = fn
        op.deps = deps
        op.ndma = ndma
        op.sig = False
        op.sigval = 0
        op.wbuf = None
        if ndma:
            assert len(writes) == 1
            op.wbuf = writes[0]
            if op.wbuf.sem is None:
                op.wbuf.sem = True
                self.dma_bufs.append(op.wbuf)
        self.ops.append(op)
        return idx

    def finalize(self):
        nc = self.nc
        ops = self.ops
        for op in ops:
            for d in op.deps:
                p = ops[d]
                if p.eng == "pe" and op.eng == "pe" and not p.ndma:
                    continue
                p.sig = True
        esem = {e: self.stack.enter_context(nc.semaphore("sem_" + e)) for e in self.ENGS}
        for b in self.dma_bufs:
            b.sem = self.stack.enter_context(nc.semaphore("dsem_" + b.name))
        cnt = {e: 0 for e in self.ENGS}
        for op in ops:
            if op.ndma:
                op.wbuf.dcnt += 16 * op.ndma
                op.sigval = op.wbuf.dcnt
            elif op.sig:
                cnt[op.eng] += 1
                op.sigval = cnt[op.eng]

        def emit(ename, e):
            waited = {}
            for op in ops:
                if op.eng != ename:
                    continue
                need = {}
                for d in op.deps:
                    p = ops[d]
                    if p.ndma:
                        sem = p.wbuf.sem
                    else:
                        if p.eng == "pe" and ename == "pe":
                            continue
                        sem = esem[p.eng]
                    k = id(sem)
                    if k not in need or need[k][1] < p.sigval:
                        need[k] = (sem, p.sigval)
                for k, (sem, v) in need.items():
                    if waited.get(k, 0) < v:
                        e.wait_ge(sem, v)
                        waited[k] = v
                ins = op.fn(e)
                if op.ndma:
                    pass
                elif op.sig:
                    ins.then_inc(esem[ename], 1)

        with nc.Block() as block:
            @block.tensor
            def _(e):
                emit("pe", e)

            @block.scalar
            def _(e):
                emit("act", e)

            @block.vector
            def _(e):
                emit("dve", e)

            @block.gpsimd
            def _(e):
                emit("pool", e)

            @block.sync
            def _(e):
                emit("sp", e)


def build(dbg=(), nheads_g=H, nheads_m=H, phases="ABCD"):
    Buf.REG = []
    nc = bass.Bass("TRN2", target_bir_lowering=False)
    stack = ExitStack()
    P = Prog(nc, stack)

    def dram(name, shape, dt=F32, kind="ExternalInput"):
        return nc.dram_tensor(name, list(shape), dt, kind=kind).ap()

    x = dram("x", [S, D])
    pre_w = dram("pre_w", [1, D])
    post_w = dram("post_w", [1, D])
    w_in = dram("w_in", [D, IN_W])
    conv_wT = dram("conv_wT", [3072, 4])
    a_log = dram("a_log16", [1, 128])
    dt_bias = dram("dt_bias16", [1, 128])
    gnw_d = dram("gnw", [128, 1])
    w_a = dram("w_a", [1024, D])
    w_b = dram("w_b", [1024, D])
    w_out = dram("w_out", [D, D])
    cf_d = dram("cf", [128, NCB * 128])
    cm_d = dram("cm", [128, 256 + 1024])
    out = dram("out", [S, D], kind="ExternalOutput")

    def sb(name, shape, dt=F32):
        return stack.enter_context(nc.sbuf_tensor(name, list(shape), dt))

    hT = sb("hT", [128, KC, S], BF16)
    oaT = sb("oaT", [128, H, S], BF16)
    cf = sb("cf_sb", [128, NCB * 128], F32)
    ARENA_BYTES = 100 * 1024
    arena = sb("arena", [128, ARENA_BYTES // 4], F32)
    ps = [stack.enter_context(nc.psum_tensor("ps%d" % i, [128, 512], F32)) for i in range(8)]
    hTb = [Buf("hT%d" % i) for i in range(4)]
    oaTb = [Buf("oaT%d" % i) for i in range(4)]
    cfb = Buf("cf")

    pbank = [Buf("psb%d" % i, excl=True) for i in range(8)]

    def psbuf(bank, c0=0, c1=512):
        return pbank[bank]

    apos = [0]

    def carve(name, free_shape, dt=F32):
        esz = 4 if dt == F32 else 2
        n = int(np.prod(free_shape))
        nb = (n * esz + 31) // 32 * 32
        off = apos[0]
        apos[0] += nb
        assert apos[0] <= ARENA_BYTES, (name, apos[0])
        ap = arena[:, off // 4:(off + nb) // 4]
        if dt != F32:
            ap = ap.bitcast(dt)
        ap = ap[:, 0:n]
        if len(free_shape) == 2:
            ap = ap.rearrange("p (a b) -> p a b", a=free_shape[0])
        elif len(free_shape) == 3:
            ap = ap.rearrange("p (a b c) -> p a b c", a=free_shape[0], b=free_shape[1])
        return ap, Buf(name, iv=("arena", off, off + nb))

    def subbufs(parent, name, pieces):
        base = parent.iv[0][1]
        return [Buf("%s_%d" % (name, i), iv=[("arena", base + lo, base + hi) for lo, hi in pc])
                for i, pc in enumerate(pieces)]

    def cblk(k):
        return cf[:, k * 128:(k + 1) * 128]

    def dma(eng, out_ap, in_ap, wbuf, reads=(), waw=True):
        def fn(e):
            return e.dma_start(out=out_ap, in_=in_ap).then_inc(wbuf.sem, 16)
        P.add(eng, fn, reads=reads, writes=[wbuf], ndma=1, waw=waw)

    def load_w(dst_ap, wdram, c0, ncols, wbuf):
        src = wdram[:, c0:c0 + ncols].rearrange("(kc p) c -> p kc c", p=128)
        dma("pool", dst_ap, src, wbuf)

    outb = Buf("outdram")
    outsl = []
    dbg_outs = {}

    def tap(name, ap, shape, dt, rbuf):
        d = dram("dbg_" + name, shape, dt, kind="ExternalOutput")
        dbg_outs[name] = d
        dma("sp", d, ap, outb, reads=rbuf, waw=False)

    dma("sp", cf[:], cf_d, cfb)
    small = sb("small", [128, 1280], F32)
    epsc = small[:, 0:2]
    epsb = Buf("epsc")
    P.add("dve", lambda e: e.memset(small[:, 0:1], EPS), writes=[epsb])
    P.add("dve", lambda e: e.memset(small[:, 1:2], 1.0), writes=[epsb])
    st = small[:, 8:16]
    stb = Buf("st")
    I_ = cblk(C_ID)

    amark = apos[0]
    xs, xsb = zip(*[carve("xs%d" % i, [D]) for i in range(2)])
    xn, xnb = carve("xn", [D])
    prew, prewb = carve("prew", [D])
    dma("sp", prew, pre_w.partition_broadcast(128), prewb)
    for tt in range(NT):
        sl = tt % 2
        dma("sp", xs[sl], x[tt * 128:(tt + 1) * 128, :], xsb[sl])
        P.add("act", lambda e, sl=sl: e.activation(out=xn, in_=xs[sl], func=AF.Square, accum_out=st[:, 0:1]),
              reads=[xsb[sl]], writes=[xnb, stb])
        P.add("act", lambda e: e.activation(out=st[:, 1:2], in_=st[:, 0:1], func=AF.Sqrt, bias=epsc[:, 0:1],
                                            scale=1.0 / D), reads=[stb, epsb], writes=[stb])
        P.add("dve", lambda e: e.reciprocal(out=st[:, 2:3], in_=st[:, 1:2]), reads=[stb], writes=[stb])
        P.add("dve", lambda e, sl=sl: e.scalar_tensor_tensor(out=xn, in0=xs[sl], scalar=st[:, 2:3], in1=prew,
                                                             op0=ALU.mult, op1=ALU.mult),
              reads=[xsb[sl], stb, prewb], writes=[xnb])
        for g in range(4):
            bank = (tt % 2) * 4 + g
            pb = psbuf(bank)

            def tr(e, g=g, bank=bank):
                ins = None
                for j in range(4):
                    kc = g * 4 + j
                    ins = e.transpose(out=ps[bank][:, j * 128:(j + 1) * 128], in_=xn[:, kc * 128:(kc + 1) * 128],
                                      identity=I_)
                return ins
            P.add("pe", tr, reads=[xnb, cfb], writes=[pb])
            dst = hT[:, g * 4:(g + 1) * 4, tt * 128:(tt + 1) * 128]
            src = ps[bank][:].rearrange("p (a b) -> p a b", a=4)
            if g % 2 == 0:
                P.add("act", lambda e, dst=dst, src=src: e.copy(out=dst, in_=src), reads=[pb], writes=[hTb[tt // 4]])
            else:
                P.add("dve", lambda e, dst=dst, src=src: e.tensor_copy(out=dst, in_=src), reads=[pb],
                      writes=[hTb[tt // 4]])
    if "hT" in dbg:
        tap("hT", hT[:], [128, KC, S], BF16, hTb)
    apos[0] = amark

    if "B" in phases:
        bmark = apos[0]
        betas = small[:, 16:144].rearrange("p (t h) -> p t h", t=NT)
        nbetas = small[:, 144:272].rearrange("p (t h) -> p t h", t=NT)
        gs = small[:, 272:400].rearrange("p (t h) -> p t h", t=NT)
        bgs = small[:, 400:528].rearrange("p (t h) -> p t h", t=NT)
        egs = small[:, 528:912].rearrange("p (t c) -> p t c", t=NT)
        negA = small[:, 912:1040]
        dtb = small[:, 1040:1168]
        gnw = small[:, 1168:1169]
        cw = small[:, 1172:1268].rearrange("p (c j) -> p c j", c=24)
        gqb = Buf("gatesq")
        cwb = Buf("cw")
        xa, xab = carve("xa", [128])
        wg, wgb = carve("wg", [KC, 16], BF16)
        load_w(wg, w_in, OFF_GB, 16, wgb)
        dma("sp", negA, a_log.partition_broadcast(128), gqb)
        dma("sp", dtb, dt_bias.partition_broadcast(128), gqb)
        dma("sp", gnw, gnw_d, gqb)
        dma("sp", cw, conv_wT.rearrange("(c p) j -> p c j", p=128), cwb)
        P.add("act", lambda e: e.activation(out=negA, in_=negA, func=AF.Exp), reads=[gqb], writes=[gqb])
        P.add("dve", lambda e: e.tensor_scalar(out=negA, in0=negA, scalar1=-1.0, scalar2=None, op0=ALU.mult),
              reads=[gqb], writes=[gqb])
        pg = psbuf(0)
        for tt in range(NT):
            def mm(e, tt=tt):
                ins = None
                for kc in range(KC):
                    ins = e.matmul(ps[0][:, tt * 16:(tt + 1) * 16], lhsT=hT[:, kc, tt * 128:(tt + 1) * 128],
                                   rhs=wg[:, kc, :], start=(kc == 0), stop=(kc == KC - 1))
                return ins
            P.add("pe", mm, reads=[hTb[tt // 4], wgb], writes=[pg])
        pgv = ps[0][:, 0:256].rearrange("p (t c) -> p t c", t=NT)
        P.add("act", lambda e: e.activation(out=betas, in_=pgv[:, :, 0:8], func=AF.Sigmoid), reads=[pg], writes=[gqb])
        P.add("dve", lambda e: e.tensor_tensor(out=xa.rearrange("p (t h) -> p t h", t=NT), in0=pgv[:, :, 8:16],
                                               in1=dtb.rearrange("p (t h) -> p t h", t=NT), op=ALU.add),
              reads=[pg, gqb], writes=[xab])
        P.add("act", lambda e: e.activation(out=xa, in_=xa, func=AF.Exp), reads=[xab], writes=[xab])
        P.add("act", lambda e: e.activation(out=xa, in_=xa, func=AF.Ln, bias=epsc[:, 1:2]), reads=[xab, epsb],
              writes=[xab])
        P.add("dve", lambda e: e.tensor_tensor(out=small[:, 272:400], in0=xa, in1=negA, op=ALU.mult),
              reads=[xab, gqb], writes=[gqb])
        P.add("dve", lambda e: e.tensor_scalar(out=small[:, 144:272], in0=small[:, 16:144], scalar1=-1.0, scalar2=None,
                                               op0=ALU.mult), reads=[gqb], writes=[gqb])
        pg2 = psbuf(1)
        for tt in range(NT):
            def mm2(e, tt=tt):
                e.matmul(ps[1][:, tt * 24:tt * 24 + 8], lhsT=cblk(C_TRI), rhs=gs[:, tt, :], start=True, stop=True)
                e.matmul(ps[1][:, tt * 24 + 8:tt * 24 + 16], lhsT=cblk(C_UG), rhs=gs[:, tt, :], start=True, stop=True)
                return e.matmul(ps[1][:, tt * 24 + 16:tt * 24 + 24], lhsT=cblk(C_ONES), rhs=gs[:, tt, :], start=True,
                                stop=True)
            P.add("pe", mm2, reads=[gqb, cfb], writes=[pg2])
        P.add("act", lambda e: e.activation(out=small[:, 528:912], in_=ps[1][:, 0:384], func=AF.Exp), reads=[pg2],
              writes=[gqb])
        P.add("dve", lambda e: e.tensor_tensor(out=bgs, in0=betas, in1=egs[:, :, 0:8], op=ALU.mult), reads=[gqb],
              writes=[gqb])
        if "gates" in dbg:
            tap("gates", small[:, 16:912], [128, 896], F32, [gqb])

        if "0" in phases:
            nheads_g = 0
        NW = 4
        wr, wrb = zip(*[carve("wr%d" % i, [KC, 128], BF16) for i in range(NW)])
        xraw, xrp = carve("xraw", [3 + S])
        xrb = subbufs(xrp, "xraw", [[(0, 12)]] + [[(12 + i * 2048, 12 + (i + 1) * 2048)] for i in range(4)])
        qT, qTp = carve("qT", [S])
        kT, kTp = carve("kT", [S])
        vT, vTp = carve("vT", [S])
        seg4 = [[(i * 2048, (i + 1) * 2048)] for i in range(4)]
        tTb = {"q": subbufs(qTp, "qT", seg4), "k": subbufs(kTp, "kT", seg4), "v": subbufs(vTp, "vT", seg4)}
        szT, szp = carve("szT", [S], BF16)
        szb = subbufs(szp, "szT", [[(i * 1024, (i + 1) * 1024)] for i in range(4)])
        sqt, sqtb = zip(*[carve("sqt%d" % i, [512], BF16) for i in range(2)])
        c16, c16b = carve("c16", [256], BF16)
        Ib16 = c16[:, 0:128]
        ones16 = c16[:, 128:256]
        P.add("dve", lambda e: e.tensor_copy(out=Ib16, in_=cblk(C_ID)), reads=[cfb], writes=[c16b])
        P.add("dve", lambda e: e.tensor_copy(out=ones16, in_=cblk(C_ONES)), reads=[cfb], writes=[c16b])
        Sb16, Sb16b = carve("S16", [128], BF16)
        rs, rsb = carve("rs", [512])
        Ssb, Sb = carve("S", [128])
        NSL = 4
        slot = []
        for i in range(NSL):
            d_ = {}
            for nm in ("Gt", "Ln", "N0", "N1", "M0", "M1", "Xs", "kbg", "vb"):
                d_[nm] = carve("%s_%d" % (nm, i), [128])
            for nm in ("iT", "kdec", "u", "wTn", "qb"):
                d_[nm] = [carve("%s_%d_%d" % (nm, i, p_), [128], F32 if nm == "u" else BF16) for p_ in range(2)]
            d_["EE"] = carve("EE_%d" % i, [256])
            slot.append(d_)
        vnew, vnewb = carve("vnew", [128], BF16)
        t1, t1b = carve("t1", [128])
        osb, osbb = carve("osb", [128])
        onb_, onbb = carve("on", [128], BF16)
        P.add("dve", lambda e: e.memset(xraw[:, 0:3], 0.0), writes=[xrb[0]])

        PBANKS = [0, 1, 3, 4, 5, 6]
        pl2 = psbuf(7)
        pscan = psbuf(2)
        pT = psbuf(7)

        def q4(bank, q):
            return ps[bank][:, q * 128:(q + 1) * 128]

        wcount = [0]

        def proj_fm(col0, dst_fn, pidx):
            wi = wcount[0] % NW
            wcount[0] += 1
            load_w(wr[wi], w_in, col0, 128, wrb[wi])
            for seg in range(4):
                bank = PBANKS[pidx[0] % len(PBANKS)]
                pidx[0] += 1

                def mm(e, seg=seg, bank=bank, wi=wi):
                    ins = None
                    for kc in range(KC):
                        ins = e.matmul(ps[bank][:, :], lhsT=wr[wi][:, kc, :], rhs=hT[:, kc, seg * 512:(seg + 1) * 512],
                                       start=(kc == 0), stop=(kc == KC - 1))
                    return ins
                P.add("pe", mm, reads=[hTb[seg], wrb[wi]], writes=[pbank[bank]])
                nxt = dst_fn(seg, bank)
                while l2pend:
                    l2pend.pop(0)()
                if nxt is not None:
                    l2pend.append(nxt)

        pidx = [0]
        l2pend = []
        SCQ = 128.0 ** -0.5
        for h in range(nheads_g):
            for ti, nm in enumerate("qkv"):
                ci = ti * 8 + h
                tT = {"q": qT, "k": kT, "v": vT}[nm]

                def consume(seg, bank, nm=nm, ci=ci, tT=tT):
                    s0 = seg * 512
                    P.add("act", lambda e: e.copy(out=xraw[:, 3 + s0:3 + s0 + 512], in_=ps[bank][:, :]),
                          reads=[pbank[bank]], writes=[xrb[seg + 1]])
                    dst = tT[:, s0:s0 + 512]
                    db = tTb[nm][seg]
                    P.add("dve", lambda e: e.tensor_scalar(out=dst, in0=xraw[:, 3 + s0:3 + s0 + 512],
                                                           scalar1=cw[:, ci, 3:4], scalar2=None, op0=ALU.mult),
                          reads=[xrb[seg + 1], cwb], writes=[db])
                    for j in range(3):
                        P.add("dve", lambda e, j=j: e.scalar_tensor_tensor(out=dst, in0=xraw[:, j + s0:j + s0 + 512],
                                                                           scalar=cw[:, ci, j:j + 1], in1=dst,
                                                                           op0=ALU.mult, op1=ALU.add),
                              reads=[xrb[seg + 1], xrb[seg], cwb, db], writes=[db])
                    P.add("act", lambda e: e.activation(out=dst, in_=dst, func=AF.Silu), reads=[db], writes=[db])
                    if nm in "qk":
                        si = seg % 2
                        P.add("act", lambda e: e.activation(out=sqt[si], in_=dst, func=AF.Square), reads=[db],
                              writes=[sqtb[si]])

                        def tail():
                            P.add("pe", lambda e: e.matmul(ps[7][:, :], lhsT=ones16, rhs=sqt[si], start=True,
                                                           stop=True), reads=[sqtb[si], c16b], writes=[pl2])
                            P.add("act", lambda e: e.activation(out=rs, in_=ps[7][:, :], func=AF.Sqrt,
                                                                bias=epsc[:, 0:1]), reads=[pl2, epsb], writes=[rsb])
                            P.add("dve", lambda e: e.reciprocal(out=rs, in_=rs), reads=[rsb], writes=[rsb])
                            if nm == "q":
                                P.add("dve", lambda e: e.scalar_tensor_tensor(out=dst, in0=dst, scalar=SCQ, in1=rs,
                                                                              op0=ALU.mult, op1=ALU.mult),
                                      reads=[db, rsb], writes=[db])
                            else:
                                P.add("dve", lambda e: e.tensor_tensor(out=dst, in0=dst, in1=rs, op=ALU.mult),
                                      reads=[db, rsb], writes=[db])
                        return tail
                    return None
                proj_fm([OFF_GQ, OFF_GK, OFF_GV][ti] + h * 128, consume, pidx)

            def consume_z(seg, bank):
                s0 = seg * 512
                P.add("act", lambda e: e.activation(out=szT[:, s0:s0 + 512], in_=ps[bank][:, :], func=AF.Silu),
                      reads=[pbank[bank]], writes=[szb[seg]])
            proj_fm(OFF_GZ + h * 128, consume_z, pidx)
            while l2pend:
                l2pend.pop(0)()
            if "qkv" in dbg and h == 0:
                tap("qT", qT, [128, S], F32, tTb["q"])
                tap("kT", kT, [128, S], F32, tTb["k"])
                tap("vT", vT, [128, S], F32, tTb["v"])

            P.add("dve", lambda e: e.memset(Ssb, 0.0), writes=[Sb])
            P.add("dve", lambda e: e.memset(Sb16, 0.0), writes=[Sb16b])

            def pre_stages(tt, sl, par, h=h):
                sd = dict(slot[sl])
                for nm_ in ("iT", "kdec", "u", "wTn", "qb"):
                    sd[nm_] = slot[sl][nm_][par]
                ts = slice(tt * 128, (tt + 1) * 128)
                sg = tt // 4
                bk = 3 + sl
                pb = pbank[bk]
                stages = []

                def s0():
                    def trKV(e):
                        e.matmul(q4(bk, 0), lhsT=kT[:, ts], rhs=I_, start=True, stop=True)
                        return e.matmul(q4(bk, 1), lhsT=vT[:, ts], rhs=I_, start=True, stop=True)
                    P.add("pe", trKV, reads=[tTb["k"][sg], tTb["v"][sg], cfb], writes=[pb])
                    P.add("dve", lambda e: e.tensor_scalar(out=sd["kbg"][0], in0=q4(bk, 0), scalar1=bgs[:, tt, h:h + 1],
                                                           scalar2=None, op0=ALU.mult),
                          reads=[pb, gqb], writes=[sd["kbg"][1]])
                    P.add("dve", lambda e: e.tensor_scalar(out=sd["kdec"][0], in0=q4(bk, 0),
                                                           scalar1=egs[:, tt, 8 + h:9 + h], scalar2=None, op0=ALU.mult),
                          reads=[pb, gqb], writes=[sd["kdec"][1]])
                    P.add("dve", lambda e: e.tensor_scalar(out=sd["vb"][0], in0=q4(bk, 1), scalar1=betas[:, tt, h:h + 1],
                                                           scalar2=None, op0=ALU.mult),
                          reads=[pb, gqb], writes=[sd["vb"][1]])
                    P.add("dve", lambda e: e.tensor_scalar(out=sd["Gt"][0], in0=cblk(C_TRI), scalar1=gs[:, tt, h:h + 1],
                                                           scalar2=None, op0=ALU.mult),
                          reads=[cfb, gqb], writes=[sd["Gt"][1]])
                stages.append(s0)

                def s1():
                    def mmD(e):
                        e.matmul(q4(bk, 2), lhsT=sd["Gt"][0], rhs=cblk(C_UG), start=True, stop=True)
                        e.matmul(q4(bk, 3), lhsT=cblk(C_UG), rhs=sd["Gt"][0], start=True, stop=False)
                        e.matmul(q4(bk, 3), lhsT=I_, rhs=cblk(C_NBI), start=False, stop=True)
                        e.matmul(q4(bk, 0), lhsT=kT[:, ts], rhs=kT[:, ts], start=True, stop=True)
                        return e.matmul(q4(bk, 1), lhsT=kT[:, ts], rhs=qT[:, ts], start=True, stop=True)
                    P.add("pe", mmD, reads=[sd["Gt"][1], cfb, tTb["k"][sg], tTb["q"][sg]], writes=[pb])
                    P.add("act", lambda e: e.activation(out=sd["EE"][0], in_=ps[bk][:, 256:512], func=AF.Exp),
                          reads=[pb], writes=[sd["EE"][1]])
                    P.add("dve", lambda e: e.scalar_tensor_tensor(out=sd["Ln"][0], in0=q4(bk, 0),
                                                                  scalar=nbetas[:, tt, h:h + 1], in1=sd["EE"][0][:, 0:128],
                                                                  op0=ALU.mult, op1=ALU.mult),
                          reads=[pb, gqb, sd["EE"][1]], writes=[sd["Ln"][1]])
                    P.add("dve", lambda e: e.tensor_tensor(out=sd["iT"][0], in0=q4(bk, 1), in1=sd["EE"][0][:, 128:256],
                                                           op=ALU.mult),
                          reads=[pb, sd["EE"][1]], writes=[sd["iT"][1]])
                    P.add("act", lambda e: e.copy(out=sd["qb"][0], in_=qT[:, ts]), reads=[tTb["q"][sg]],
                          writes=[sd["qb"][1]])
                stages.append(s1)

                cur = {"N": (I_, cfb), "M": (I_, cfb)}
                for l in range(7):
                    def sA(l=l):
                        Ncur, Nb = cur["N"]
                        P.add("pe", lambda e: e.matmul(q4(bk, 0), lhsT=sd["Ln"][0], rhs=Ncur, start=True, stop=True),
                              reads=[sd["Ln"][1], Nb], writes=[pb])
                        P.add("dve", lambda e: e.tensor_tensor(out=sd["Xs"][0], in0=q4(bk, 0), in1=cblk(C_MU0 + l),
                                                               op=ALU.mult),
                              reads=[pb, cfb], writes=[sd["Xs"][1]])
                    stages.append(sA)

                    def sB(l=l):
                        Ncur, Nb = cur["N"]
                        Mcur, Mb = cur["M"]
                        Nn = sd["N%d" % (l % 2)]
                        Mn = sd["M%d" % (l % 2)]

                        def mmNM(e):
                            ins = None
                            if l > 0:
                                ins = e.matmul(q4(bk, 1), lhsT=Mcur, rhs=sd["Xs"][0], start=True, stop=True)
                            if l < 6:
                                ins = e.matmul(q4(bk, 2), lhsT=sd["Xs"][0], rhs=Mcur, start=True, stop=True)
                            return ins
                        P.add("pe", mmNM, reads=[Nb, Mb, sd["Xs"][1], cfb], writes=[pb])
                        if l > 0:
                            P.add("dve", lambda e: e.tensor_tensor(out=Nn[0], in0=q4(bk, 1), in1=Ncur, op=ALU.add),
                                  reads=[pb, Nb], writes=[Nn[1]])
                        else:
                            P.add("dve", lambda e: e.tensor_tensor(out=Nn[0], in0=sd["Xs"][0], in1=Ncur, op=ALU.add),
                                  reads=[sd["Xs"][1], Nb], writes=[Nn[1]])
                        if l < 6:
                            P.add("dve", lambda e: e.tensor_tensor(out=Mn[0], in0=q4(bk, 2), in1=Mcur, op=ALU.add),
                                  reads=[pb, Mb], writes=[Mn[1]])
                        cur["N"] = Nn
                        cur["M"] = Mn
                    stages.append(sB)

                def sF():
                    Nf, Nfb = cur["N"]

                    def mmUW(e):
                        e.matmul(q4(bk, 0), lhsT=Nf, rhs=sd["vb"][0], start=True, stop=True)
                        return e.matmul(q4(bk, 1), lhsT=sd["kbg"][0], rhs=Nf, start=True, stop=True)
                    P.add("pe", mmUW, reads=[Nfb, sd["vb"][1], sd["kbg"][1]], writes=[pb])
                    P.add("act", lambda e: e.copy(out=sd["u"][0], in_=q4(bk, 0)), reads=[pb], writes=[sd["u"][1]])
                    P.add("act", lambda e: e.mul(out=sd["wTn"][0], in_=q4(bk, 1), mul=-1.0), reads=[pb],
                          writes=[sd["wTn"][1]])
                stages.append(sF)
                return stages

            def scan(tt, sl, par, h=h):
                sd = dict(slot[sl])
                for nm_ in ("iT", "kdec", "u", "wTn", "qb"):
                    sd[nm_] = slot[sl][nm_][par]
                ts = slice(tt * 128, (tt + 1) * 128)
                sg = tt // 4
                subs = []

                def subA():
                    P.add("pe", mmV, reads=[sd["wTn"][1], Sb16b], writes=[pscan])
                    P.add("dve", lambda e: e.tensor_tensor(out=vnew, in0=q4(2, 0), in1=sd["u"][0], op=ALU.add),
                          reads=[pscan, sd["u"][1]], writes=[vnewb])

                def subB():
                    P.add("pe", mmO, reads=[sd["qb"][1], Sb16b, sd["iT"][1], sd["kdec"][1], vnewb], writes=[pscan])
                    P.add("act", lambda e: e.mul(out=t1, in_=q4(2, 1), mul=egs[:, tt, h:h + 1]), reads=[pscan, gqb],
                          writes=[t1b])
                    P.add("dve", lambda e: e.tensor_tensor(out=osb, in0=t1, in1=q4(2, 2), op=ALU.add),
                          reads=[t1b, pscan], writes=[osbb])
                    P.add("dve", lambda e: e.scalar_tensor_tensor(out=Ssb, in0=Ssb, scalar=egs[:, tt, 16 + h:17 + h],
                                                                  in1=q4(2, 3), op0=ALU.mult, op1=ALU.add),
                          reads=[Sb, gqb, pscan], writes=[Sb])
                    P.add("act", lambda e: e.copy(out=Sb16, in_=Ssb), reads=[Sb], writes=[Sb16b])

                def subC():
                    P.add("act", lambda e: e.activation(out=t1, in_=osb, func=AF.Square, accum_out=st[:, 4:5]),
                          reads=[osbb], writes=[t1b, stb])
                    P.add("act", lambda e: e.activation(out=st[:, 5:6], in_=st[:, 4:5], func=AF.Sqrt, bias=epsc[:, 0:1],
                                                        scale=1.0 / 128), reads=[stb, epsb], writes=[stb])
                    P.add("dve", lambda e: e.reciprocal(out=st[:, 6:7], in_=st[:, 5:6]), reads=[stb], writes=[stb])
                    P.add("act", lambda e: e.mul(out=onb_, in_=osb, mul=st[:, 6:7]), reads=[osbb, stb], writes=[onbb])

                def subD():
                    P.add("pe", lambda e: e.matmul(q4(7, 0), lhsT=onb_, rhs=Ib16, start=True, stop=True),
                          reads=[onbb, c16b], writes=[pT])
                    P.add("dve", lambda e: e.scalar_tensor_tensor(out=oaT[:, h, ts], in0=q4(7, 0), scalar=gnw,
                                                                  in1=szT[:, ts], op0=ALU.mult, op1=ALU.mult),
                          reads=[pT, gqb, szb[sg]], writes=[oaTb[sg]])

                def mmV(e):
                    return e.matmul(q4(2, 0), lhsT=sd["wTn"][0], rhs=Sb16, start=True, stop=True)

                def mmO(e):
                    e.matmul(q4(2, 1), lhsT=sd["qb"][0], rhs=Sb16, start=True, stop=True)
                    e.matmul(q4(2, 2), lhsT=sd["iT"][0], rhs=vnew, start=True, stop=True)
                    return e.matmul(q4(2, 3), lhsT=sd["kdec"][0], rhs=vnew, start=True, stop=True)
                return [subA, subB, subC, subD]

            pend = []
            for gi, g0 in enumerate(range(0, NT, NSL)):
                sts = [pre_stages(g0 + i, i, gi % 2) for i in range(NSL)]
                for k in range(len(sts[0])):
                    for i in range(NSL):
                        sts[i][k]()
                    if pend:
                        pend.pop(0)()
                while pend:
                    pend.pop(0)()
                for i in range(NSL):
                    pend.extend(scan(g0 + i, i, gi % 2))
            while pend:
                pend.pop(0)()
        print("phase B arena bytes", apos[0])
        if "oaT" in dbg:
            tap("oaT", oaT[:], [128, H, S], BF16, oaTb)
        if "scan" in dbg:
            tap("osb", osb, [128, 128], F32, [osbb])
            tap("S", Ssb, [128, 128], F32, [Sb])
            tap("vnew", vnew, [128, 128], F32, [vnewb])
        apos[0] = bmark

    obT, obTp = carve("obT", [H, S], BF16)
    obTb = subbufs(obTp, "obT", [[(hh * 4096 + q * 1024, hh * 4096 + (q + 1) * 1024) for hh in range(H)] for q in range(4)])
    cmark = apos[0]
    if "C" in phases:
        NWC = 4
        cwr, cwrb = zip(*[carve("cwr%d" % i, [KC, 128], BF16) for i in range(NWC)])
        cmf, cmfb = carve("cmf", [1280])
        cmb, cmbb = carve("cmb", [1536], BF16)
        dma("sp", cmf, cm_d, cmfb)
        if "7" not in phases:
            P.add("dve", lambda e: e.tensor_copy(out=cmb[:, 0:1280], in_=cmf), reads=[cmfb], writes=[cmbb])
        P.add("dve", lambda e: e.tensor_copy(out=cmb[:, 1280:1408], in_=cblk(C_ID)), reads=[cfb], writes=[cmbb])
        P.add("dve", lambda e: e.tensor_copy(out=cmb[:, 1408:1536], in_=cblk(C_ONES)), reads=[cfb], writes=[cmbb])
        cbias = cmb[:, 0:256]
        Ib = cmb[:, 1280:1408]
        onesb = cmb[:, 1408:1536]
        qTb_, qTbb = carve("mqTb", [S], BF16)
        kTb_, kTbb = carve("mkTb", [S], BF16)
        qTf, qTfb = carve("mqTf", [S])
        Vt, Vtb = carve("mV", [NT, 128], BF16)
        mszT, mszb = carve("mszT", [S], BF16)
        nbT, nbTb = carve("nbT", [S], BF16)
        kmT, kmTb = carve("kmT", [8])
        gm, gmb = carve("gm", [NT, 8])
        nb, nbb = carve("nb", [NT, 8])
        top8, top8b = carve("top8", [8])
        pTr, pTrb = zip(*[carve("pT%d" % i, [256], BF16) for i in range(4)])
        rr, rrb = carve("rr", [256])
        zz, zzb = carve("zz", [256])
        cwcount = [0]

        def proj_c(col0, consume):
            wi = cwcount[0] % NWC
            cwcount[0] += 1
            load_w(cwr[wi], w_in, col0, 128, cwrb[wi])
            for seg in range(4):
                bank = seg % 2

                def mm(e, seg=seg, bank=bank, wi=wi):
                    ins = None
                    for kc in range(KC):
                        ins = e.matmul(ps[bank][:, :], lhsT=cwr[wi][:, kc, :], rhs=hT[:, kc, seg * 512:(seg + 1) * 512],
                                       start=(kc == 0), stop=(kc == KC - 1))
                    return ins
                P.add("pe", mm, reads=[hTb[seg], cwrb[wi]], writes=[pbank[bank]])
                consume(seg, bank)

        for h in range(nheads_m):
            def cons_q(seg, bank):
                s0 = seg * 512
                P.add("act", lambda e: e.mul(out=qTf[:, s0:s0 + 512], in_=ps[bank][:, :], mul=SCQ_M),
                      reads=[pbank[bank]], writes=[qTfb])
                P.add("dve", lambda e: e.tensor_copy(out=qTb_[:, s0:s0 + 512], in_=qTf[:, s0:s0 + 512]),
                      reads=[qTfb], writes=[qTbb])
            SCQ_M = 128.0 ** -0.5
            proj_c(OFF_MQ + h * 128, cons_q)

            def cons_k(seg, bank):
                s0 = seg * 512
                P.add("act", lambda e: e.copy(out=kTb_[:, s0:s0 + 512], in_=ps[bank][:, :]),
                      reads=[pbank[bank]], writes=[kTbb])
                P.add("dve", lambda e: e.tensor_reduce(out=kmT[:, seg * 2:seg * 2 + 2],
                                                       in_=ps[bank][:, :].rearrange("p (a b) -> p a b", a=2),
                                                       axis=AX.X, op=ALU.add),
                      reads=[pbank[bank]], writes=[kmTb])
            proj_c(OFF_MK + h * 128, cons_k)

            def cons_z(seg, bank):
                s0 = seg * 512
                P.add("act", lambda e: e.activation(out=mszT[:, s0:s0 + 512], in_=ps[bank][:, :], func=AF.Silu),
                      reads=[pbank[bank]], writes=[mszb])
            proj_c(OFF_MZ + h * 128, cons_z)

            wi = cwcount[0] % NWC
            cwcount[0] += 1
            load_w(cwr[wi], w_in, OFF_MV + h * 128, 128, cwrb[wi])
            for g in range(4):
                bank = 2 + g % 2

                def mmv(e, g=g, bank=bank, wi=wi):
                    ins = None
                    for j in range(4):
                        tt = g * 4 + j
                        for kc in range(KC):
                            ins = e.matmul(ps[bank][:, j * 128:(j + 1) * 128], lhsT=hT[:, kc, tt * 128:(tt + 1) * 128],
                                           rhs=cwr[wi][:, kc, :], start=(kc == 0), stop=(kc == KC - 1))
                    return ins
                P.add("pe", mmv, reads=[hTb[g], cwrb[wi]], writes=[pbank[bank]])
                P.add("act", lambda e, g=g, bank=bank: e.copy(out=Vt[:, g * 4:(g + 1) * 4, :],
                                                              in_=ps[bank][:, :].rearrange("p (a b) -> p a b", a=4)),
                      reads=[pbank[bank]], writes=[Vtb])

            P.add("dve", lambda e: e.memset(gm, -1e30), writes=[gmb])
            P.add("dve", lambda e: e.memset(nb, 0.0), writes=[nbb])

            def mmg(e):
                ins = None
                for tt in range(8, NT):
                    ins = e.matmul(ps[4][:, tt * 8:tt * 8 + 8], lhsT=qTf[:, tt * 128:(tt + 1) * 128], rhs=kmT,
                                   start=True, stop=True)
                return ins
            P.add("pe", mmg, reads=[qTfb, kmTb], writes=[pbank[4]])
            for tt in range(8, NT):
                qb = tt // 2
                P.add("dve", lambda e, tt=tt, qb=qb: e.tensor_copy(out=gm[:, tt, 0:qb], in_=ps[4][:, tt * 8:tt * 8 + qb]),
                      reads=[pbank[4]], writes=[gmb])
                P.add("dve", lambda e, tt=tt: e.max(out=top8, in_=gm[:, tt, :]), reads=[gmb], writes=[top8b])
                P.add("dve", lambda e, tt=tt: e.tensor_scalar(out=nb[:, tt, :], in0=gm[:, tt, :], scalar1=top8[:, 2:3],
                                                              scalar2=NEG, op0=ALU.is_lt, op1=ALU.mult),
                      reads=[gmb, top8b], writes=[nbb])
            for g in range(4):
                bank = 5 + g % 2

                def mmt(e, g=g, bank=bank):
                    ins = None
                    for j in range(4):
                        tt = g * 4 + j
                        ins = e.matmul(ps[bank][0:8, j * 128:(j + 1) * 128], lhsT=nb[:, tt, :], rhs=I_, start=True,
                                       stop=True)
                    return ins
                P.add("pe", mmt, reads=[nbb, cfb], writes=[pbank[bank]])
                P.add("act", lambda e, g=g, bank=bank: e.copy(out=nbT[0:8, g * 512:(g + 1) * 512], in_=ps[bank][0:8, :]),
                      reads=[pbank[bank]], writes=[nbTb])
            if "moba_pre" in dbg and h == 0:
                tap("nbT", nbT[0:8, :], [8, S], BF16, [nbTb])
                tap("mqT", qTf, [128, S], F32, [qTfb])
                tap("mV", Vt, [128, NT, 128], BF16, [Vtb])

            for qb in range(8):
                q0 = qb * 256
                nk = 2 * qb + 2
                ob_bank = 6 + qb % 2
                pob = pbank[ob_bank]

                def emit_S(kt, qb=qb, q0=q0):
                    n = kt // 2
                    sbank = kt % 4
                    if kt == 2 * qb + 1:
                        def mm(e):
                            e.matmul(ps[sbank][:, 128:256], lhsT=kTb_[:, kt * 128:(kt + 1) * 128],
                                     rhs=qTb_[:, q0 + 128:q0 + 256], start=True, stop=False)
                            return e.matmul(ps[sbank][:, 128:256], lhsT=Ib, rhs=cbias[:, 0:128], start=False, stop=True)
                    elif kt == 2 * qb:
                        def mm(e):
                            e.matmul(ps[sbank][:, 0:256], lhsT=kTb_[:, kt * 128:(kt + 1) * 128], rhs=qTb_[:, q0:q0 + 256],
                                     start=True, stop=False)
                            return e.matmul(ps[sbank][:, 0:256], lhsT=Ib, rhs=cbias, start=False, stop=True)
                    elif qb <= 3:
                        def mm(e):
                            return e.matmul(ps[sbank][:, 0:256], lhsT=kTb_[:, kt * 128:(kt + 1) * 128],
                                            rhs=qTb_[:, q0:q0 + 256], start=True, stop=True)
                    else:
                        def mm(e):
                            e.matmul(ps[sbank][:, 0:256], lhsT=kTb_[:, kt * 128:(kt + 1) * 128], rhs=qTb_[:, q0:q0 + 256],
                                     start=True, stop=False)
                            return e.matmul(ps[sbank][:, 0:256], lhsT=cmb[0:8, 256 + n * 128:256 + (n + 1) * 128],
                                            rhs=nbT[0:8, q0:q0 + 256], start=False, stop=True)
                    P.add("pe", mm, reads=[kTbb, qTbb, cmbb, nbTb], writes=[pbank[sbank]])

                def emit_PV(kt, qb=qb, q0=q0, nk=nk, ob_bank=ob_bank, pob=pob):
                    sbank = kt % 4
                    c0 = 128 if kt == 2 * qb + 1 else 0
                    pt = pTr[kt % 4]
                    P.add("act", lambda e: e.activation(out=pt[:, c0:256], in_=ps[sbank][:, c0:256], func=AF.Exp),
                          reads=[pbank[sbank]], writes=[pTrb[kt % 4]])

                    def mm(e):
                        e.matmul(ps[ob_bank][:, c0:256], lhsT=Vt[:, kt, :], rhs=pt[:, c0:256], start=(kt == 0),
                                 stop=(kt == nk - 1))
                        return e.matmul(ps[ob_bank][:, 256 + c0:512], lhsT=onesb, rhs=pt[:, c0:256], start=False,
                                        stop=(kt == nk - 1), skip_group_check=True)
                    P.add("pe", mm, reads=[Vtb, pTrb[kt % 4], cmbb], writes=[pob])

                emit_S(0)
                emit_S(1)
                for kt in range(nk):
                    emit_PV(kt)
                    if kt + 2 < nk:
                        emit_S(kt + 2)
                P.add("dve", lambda e, ob_bank=ob_bank: e.reciprocal(out=rr, in_=ps[ob_bank][:, 256:512]),
                      reads=[pob], writes=[rrb])
                P.add("dve", lambda e, q0=q0: e.tensor_tensor(out=zz, in0=rr, in1=mszT[:, q0:q0 + 256], op=ALU.mult),
                      reads=[rrb, mszb], writes=[zzb])
                P.add("dve", lambda e, q0=q0, ob_bank=ob_bank, h=h: e.tensor_tensor(out=obT[:, h, q0:q0 + 256],
                                                                                   in0=ps[ob_bank][:, 0:256], in1=zz,
                                                                                   op=ALU.mult),
                      reads=[pob, zzb], writes=[obTb[qb // 2]])
        if "obT" in dbg:
            tap("obT", obT, [128, H, S], BF16, obTb)
    apos[0] = cmark

    if "D" in phases:
        dmark = apos[0]
        mT1, mT1p = carve("mT1", [KC, 1024], BF16)
        mT1b = subbufs(mT1p, "mT1", [[(c * 2048 + q * 1024, c * 2048 + (q + 1) * 1024) for c in range(KC)] for q in range(2)])
        ymark = apos[0]
        NWD = 2
        wga, wgab = zip(*[carve("wga%d" % i, [KC, 128], BF16) for i in range(NWD)])
        wgb_, wgbb = zip(*[carve("wgb%d" % i, [KC, 128], BF16) for i in range(NWD)])
        wba, wbab = zip(*[carve("wba%d" % i, [8, 128], BF16) for i in range(NWD)])
        wbb, wbbb = zip(*[carve("wbb%d" % i, [8, 128], BF16) for i in range(NWD)])
        sga, sgab = zip(*[carve("sga%d" % i, [512]) for i in range(2)])
        sgb, sgbb = zip(*[carve("sgb%d" % i, [512]) for i in range(2)])
        it = 0
        for hf in range(2):
            for c in range(KC):
                wi = it % NWD
                load_w(wga[wi], w_in, OFF_GATE_A + c * 128, 128, wgab[wi])
                load_w(wgb_[wi], w_in, OFF_GATE_B + c * 128, 128, wgbb[wi])
                load_w(wba[wi], w_a, c * 128, 128, wbab[wi])
                load_w(wbb[wi], w_b, c * 128, 128, wbbb[wi])
                for seg in range(2):
                    t0 = hf * 1024 + seg * 512
                    hq = t0 // 512
                    bs = (it * 2 + seg) % 2 * 4
                    si = seg

                    def mmg(e, w, bank, t0=t0):
                        ins = None
                        for kc in range(KC):
                            ins = e.matmul(ps[bank][:, :], lhsT=w[:, kc, :], rhs=hT[:, kc, t0:t0 + 512], start=(kc == 0),
                                           stop=(kc == KC - 1))
                        return ins

                    def mmb(e, w, src, bank, t0=t0):
                        ins = None
                        for kc in range(8):
                            ins = e.matmul(ps[bank][:, :], lhsT=w[:, kc, :], rhs=src[:, kc, t0:t0 + 512], start=(kc == 0),
                                           stop=(kc == 7))
                        return ins
                    P.add("pe", lambda e, wi=wi, bs=bs, mmg=mmg: mmg(e, wga[wi], bs), reads=[hTb[hq], wgab[wi]],
                          writes=[pbank[bs]])
                    P.add("act", lambda e, bs=bs, si=si: e.activation(out=sga[si], in_=ps[bs][:, :], func=AF.Sigmoid),
                          reads=[pbank[bs]], writes=[sgab[si]])
                    P.add("pe", lambda e, wi=wi, bs=bs, mmg=mmg: mmg(e, wgb_[wi], bs + 1), reads=[hTb[hq], wgbb[wi]],
                          writes=[pbank[bs + 1]])
                    P.add("act", lambda e, bs=bs, si=si: e.activation(out=sgb[si], in_=ps[bs + 1][:, :], func=AF.Sigmoid),
                          reads=[pbank[bs + 1]], writes=[sgbb[si]])
                    P.add("pe", lambda e, wi=wi, bs=bs, mmb=mmb: mmb(e, wba[wi], oaT, bs + 2), reads=[oaTb[hq], wbab[wi]],
                          writes=[pbank[bs + 2]])
                    P.add("dve", lambda e, bs=bs, si=si: e.tensor_tensor(out=sga[si], in0=ps[bs + 2][:, :], in1=sga[si],
                                                                         op=ALU.mult),
                          reads=[pbank[bs + 2], sgab[si]], writes=[sgab[si]])
                    P.add("pe", lambda e, wi=wi, bs=bs, mmb=mmb: mmb(e, wbb[wi], obT, bs + 3), reads=[obTb[hq], wbbb[wi]],
                          writes=[pbank[bs + 3]])
                    P.add("dve", lambda e, bs=bs, si=si: e.tensor_tensor(out=sgb[si], in0=ps[bs + 3][:, :], in1=sgb[si],
                                                                         op=ALU.mult),
                          reads=[pbank[bs + 3], sgbb[si]], writes=[sgbb[si]])
                    if hf == 0:
                        dst = mT1[:, c, seg * 512:(seg + 1) * 512]
                        dbuf = mT1b[seg]
                    else:
                        dst = hT[:, c, seg * 512:(seg + 1) * 512]
                        dbuf = hTb[seg]
                    P.add("dve", lambda e, si=si, dst=dst: e.tensor_tensor(out=dst, in0=sga[si], in1=sgb[si], op=ALU.add),
                          reads=[sgab[si], sgbb[si]], writes=[dbuf])
                it += 1
        if "mT" in dbg:
            tap("mT1", mT1, [128, KC, 1024], BF16, mT1b)
            tap("mT2", hT[:, :, 0:1024], [128, KC, 1024], BF16, hTb[0:2])
        apos[0] = ymark
        outsl = [Buf("outdram_s%d" % i) for i in range(2)]
        postw, postwb = carve("postw", [D])
        xn_junk, xnjb = carve("xnj", [D], BF16)
        xt, xtb = zip(*[carve("xt%d" % i, [D]) for i in range(2)])
        ysb, ysbb = carve("ysb", [D])
        dma("sp", postw, post_w.partition_broadcast(128), postwb)
        for q in range(4):
            dma("pool", oaT[:, :, q * 512:(q + 1) * 512],
                w_out[0:1024, q * 512:(q + 1) * 512].rearrange("(kc p) c -> p kc c", p=128), oaTb[q])
            dma("pool", obT[:, :, q * 512:(q + 1) * 512],
                w_out[1024:2048, q * 512:(q + 1) * 512].rearrange("(kc p) c -> p kc c", p=128), obTb[q])
        for tt in range(NT):
            sl = tt % 2
            dma("sp", xt[sl], x[tt * 128:(tt + 1) * 128, :], xtb[sl])
            for ct in range(4):
                bank = (tt % 2) * 4 + ct

                def mmy(e, tt=tt, ct=ct, bank=bank):
                    ins = None
                    for kc in range(KC):
                        if tt < 8:
                            lt = mT1[:, kc, tt * 128:(tt + 1) * 128]
                        else:
                            lt = hT[:, kc, (tt - 8) * 128:(tt - 7) * 128]
                        wsrc = oaT if kc < 8 else obT
                        ins = e.matmul(ps[bank][:, :], lhsT=lt, rhs=wsrc[:, kc % 8, ct * 512:(ct + 1) * 512],
                                       start=(kc == 0), stop=(kc == KC - 1))
                    return ins
                mb = mT1b[tt // 4] if tt < 8 else hTb[(tt - 8) // 4]
                P.add("pe", mmy, reads=[mb, oaTb[ct], obTb[ct]], writes=[pbank[bank]])
                P.add("act", lambda e, ct=ct, bank=bank: e.copy(out=ysb[:, ct * 512:(ct + 1) * 512], in_=ps[bank][:, :]),
                      reads=[pbank[bank]], writes=[ysbb])
            P.add("act", lambda e, sl=sl: e.activation(out=xn_junk, in_=ysb, func=AF.Square, accum_out=st[:, 0:1]),
                  reads=[ysbb], writes=[xnjb, stb])
            P.add("act", lambda e: e.activation(out=st[:, 1:2], in_=st[:, 0:1], func=AF.Sqrt, bias=epsc[:, 0:1],
                                                scale=1.0 / D), reads=[stb, epsb], writes=[stb])
            P.add("dve", lambda e: e.reciprocal(out=st[:, 2:3], in_=st[:, 1:2]), reads=[stb], writes=[stb])
            P.add("dve", lambda e: e.scalar_tensor_tensor(out=ysb, in0=ysb, scalar=st[:, 2:3], in1=postw, op0=ALU.mult,
                                                          op1=ALU.mult), reads=[ysbb, stb, postwb], writes=[ysbb])
            P.add("dve", lambda e, sl=sl: e.tensor_tensor(out=xt[sl], in0=xt[sl], in1=ysb, op=ALU.add),
                  reads=[xtb[sl], ysbb], writes=[xtb[sl]])
            dma("sp", out[tt * 128:(tt + 1) * 128, :], xt[sl], outsl[sl], reads=[xtb[sl]])
        apos[0] = dmark

    P.add("sp", lambda e: e.nop(), reads=[outb] + (outsl if "D" in phases else []))
    P.finalize()
    return nc, dbg_outs


def _in_maps(x, pre_norm_w, w_in, conv_w, a_log, dt_bias, gdn_norm_w, w_branch_a, w_branch_b, w_out, post_norm_w):
    cf, cm = make_consts()
    f = lambda a: np.ascontiguousarray(np.asarray(a, dtype=np.float32))
    shared = {
        "pre_w": f(pre_norm_w[0][None, :]),
        "post_w": f(post_norm_w[0][None, :]),
        "w_in": f(w_in[0]),
        "conv_wT": f(np.asarray(conv_w[0]).T),
        "a_log16": f(np.tile(np.asarray(a_log[0]), 16)[None, :]),
        "dt_bias16": f(np.tile(np.asarray(dt_bias[0]), 16)[None, :]),
        "gnw": f(np.asarray(gdn_norm_w[0])[:, None]),
        "w_a": f(w_branch_a[0]),
        "w_b": f(w_branch_b[0]),
        "w_out": f(w_out[0]),
        "cf": cf,
        "cm": cm,
    }
    return [dict(shared, x=f(x[b])) for b in range(x.shape[0])]


def kernel(**inputs):
    maps = _in_maps(**inputs)
    import os
    nc, _ = build(phases=os.environ.get("KPHASES", "ABCD"))
    res = run_bass_kernel_spmd(nc, maps, core_ids=list(range(len(maps))))
    return np.stack([np.asarray(r["out"], dtype=np.float32) for r in res.results], axis=0)
```

```python
from contextlib import ExitStack

import numpy as np
import concourse.bass as bass
import concourse.mybir as mybir
from concourse.bass_utils import run_bass_kernel_spmd

F32 = mybir.dt.float32
BF16 = mybir.dt.bfloat16
AF = mybir.ActivationFunctionType
ALU = mybir.AluOpType
AX = mybir.AxisListType

S = 2048
D = 2048
NT = S // 128
KC = D // 128
H = 8
IN_W = 12304
EPS = 1e-6
NEG = -30000.0
STAGE_LIMIT = 1000
PRE_TAPS = ("u", "wTn", "iT", "Ln", "kdec", "N0")

OFF_GQ, OFF_GK, OFF_GV = 0, 1024, 2048
OFF_GZ = 3072
OFF_GB = 4096
OFF_GA = 4104
OFF_MQ, OFF_MK, OFF_MV = 4112, 4112 + 1024, 4112 + 2048
OFF_MZ = 4112 + 3072
OFF_GATE_A = 8208
OFF_GATE_B = 8208 + 2048

C_ID, C_TRI, C_UG, C_ONES, C_NBS, C_NBI, C_MU0 = 0, 1, 2, 3, 4, 5, 6
NCB = 13


def make_consts():
    r = np.arange(128)[:, None]
    c = np.arange(128)[None, :]
    blocks = [None] * NCB
    blocks[C_ID] = (r == c)
    blocks[C_TRI] = (r <= c)
    blocks[C_UG] = (r > c)
    blocks[C_ONES] = np.ones((128, 128), bool)
    nbs = np.where(r > c, 0.0, NEG)
    nbi = np.where(r <= c, 0.0, NEG)
    out = []
    for k in range(NCB):
        if k == C_NBS:
            out.append(nbs.astype(np.float32))
        elif k == C_NBI:
            out.append(nbi.astype(np.float32))
        elif k >= C_MU0:
            l = k - C_MU0
            b = 1 << l
            m = ((r // (2 * b)) == (c // (2 * b))) & ((r % (2 * b)) < b) & ((c % (2 * b)) >= b)
            out.append(m.astype(np.float32))
        else:
            out.append(blocks[k].astype(np.float32))
    cf = np.concatenate(out, axis=1)
    cb = np.zeros((128, 256), np.float32)
    cb[:, :128] = nbi
    es = np.zeros((128, 8 * 128), np.float32)
    for n in range(8):
        es[n, n * 128:(n + 1) * 128] = 1.0
    return np.ascontiguousarray(cf), np.ascontiguousarray(np.concatenate([cb, es], axis=1))


class Buf:
    __slots__ = ("name", "last_w", "readers", "sem", "dcnt", "iv", "ov", "excl")
    REG = []

    def __init__(self, name, iv=None, excl=False):
        self.excl = excl
        self.name = name
        self.last_w = None
        self.readers = []
        self.sem = None
        self.dcnt = 0
        if iv is not None and not isinstance(iv, list):
            iv = [iv]
        self.iv = iv
        self.ov = [self]
        if iv is not None:
            for o in Buf.REG:
                if any(a[0] == b[0] and a[1] < b[2] and b[1] < a[2] for a in iv for b in o.iv):
                    o.ov.append(self)
                    self.ov.append(o)
            Buf.REG.append(self)


class Op:
    __slots__ = ("eng", "fn", "deps", "ndma", "sig", "sigval", "wbuf")


class Prog:
    ENGS = ("pe", "act", "dve", "pool", "sp")

    def __init__(self, nc, stack):
        self.nc = nc
        self.stack = stack
        self.ops = []
        self.dma_bufs = []

    def add(self, eng, fn, reads=(), writes=(), ndma=0, waw=True):
        idx = len(self.ops)
        deps = set()
        for b0 in reads:
            for b in b0.ov:
                if b.last_w is not None:
                    deps.add(b.last_w)
                if b.excl:
                    for r in b.readers:
                        if self.ops[r].eng != eng:
                            deps.add(r)
        for b0 in writes:
            for b in b0.ov:
                if b.last_w is not None and (waw or b is not b0):
                    deps.add(b.last_w)
                deps.update(b.readers)
        deps.discard(idx)
        for b in reads:
            b.readers.append(idx)
        for b in writes:
            b.last_w = idx
            b.readers = []
        op = Op()
        op.eng = eng
        op.fn = fn
        op.deps = deps
        op.ndma = ndma
        op.sig = False
        op.sigval = 0
        op.wbuf = None
        if ndma:
            assert len(writes) == 1
            op.wbuf = writes[0]
            if op.wbuf.sem is None:
                op.wbuf.sem = True
                self.dma_bufs.append(op.wbuf)
        self.ops.append(op)
        return idx

    def finalize(self):
        nc = self.nc
        ops = self.ops
        for op in ops:
            for d in op.deps:
                p = ops[d]
                if p.eng == "pe" and op.eng == "pe" and not p.ndma:
                    continue
                p.sig = True
        esem = {e: self.stack.enter_context(nc.semaphore("sem_" + e)) for e in self.ENGS}
        for b in self.dma_bufs:
            b.sem = self.stack.enter_context(nc.semaphore("dsem_" + b.name))
        cnt = {e: 0 for e in self.ENGS}
        for op in ops:
            if op.ndma:
                op.wbuf.dcnt += 16 * op.ndma
                op.sigval = op.wbuf.dcnt
            elif op.sig:
                cnt[op.eng] += 1
                op.sigval = cnt[op.eng]

        def emit(ename, e):
            waited = {}
            for op in ops:
                if op.eng != ename:
                    continue
                need = {}
                for d in op.deps:
                    p = ops[d]
                    if p.ndma:
                        sem = p.wbuf.sem
                    else:
                        if p.eng == "pe" and ename == "pe":
                            continue
                        sem = esem[p.eng]
                    k = id(sem)
                    if k not in need or need[k][1] < p.sigval:
                        need[k] = (sem, p.sigval)
                for k, (sem, v) in need.items():
                    if waited.get(k, 0) < v:
                        e.wait_ge(sem, v)
                        waited[k] = v
                ins = op.fn(e)
                if op.ndma:
                    pass
                elif op.sig:
                    ins.then_inc(esem[ename], 1)

        with nc.Block() as block:
            @block.tensor
            def _(e):
                emit("pe", e)

            @block.scalar
            def _(e):
                emit("act", e)

            @block.vector
            def _(e):
                emit("dve", e)

            @block.gpsimd
            def _(e):
                emit("pool", e)

            @block.sync
            def _(e):
                emit("sp", e)


def build(dbg=(), nheads_g=H, nheads_m=H, phases="ABCD"):
    Buf.REG = []
    nc = bass.Bass("TRN2", target_bir_lowering=False)
    stack = ExitStack()
    P = Prog(nc, stack)

    def dram(name, shape, dt=F32, kind="ExternalInput"):
        return nc.dram_tensor(name, list(shape), dt, kind=kind).ap()

    x = dram("x", [S, D])
    pre_w = dram("pre_w", [1, D])
    post_w = dram("post_w", [1, D])
    w_in = dram("w_in", [D, IN_W])
    conv_wT = dram("conv_wT", [3072, 4])
    a_log = dram("a_log16", [1, 128])
    dt_bias = dram("dt_bias16", [1, 128])
    gnw_d = dram("gnw", [128, 1])
    w_a = dram("w_a", [1024, D])
    w_b = dram("w_b", [1024, D])
    w_out = dram("w_out", [D, D])
    cf_d = dram("cf", [128, NCB * 128])
    cm_d = dram("cm", [128, 256 + 1024])
    out = dram("out", [S, D], kind="ExternalOutput")

    def sb(name, shape, dt=F32):
        return stack.enter_context(nc.sbuf_tensor(name, list(shape), dt))

    hT = sb("hT", [128, KC, S], BF16)
    oaT = sb("oaT", [128, H, S], BF16)
    cf = sb("cf_sb", [128, NCB * 128], F32)
    ARENA_BYTES = 100 * 1024
    arena = sb("arena", [128, ARENA_BYTES // 4], F32)
    ps = [stack.enter_context(nc.psum_tensor("ps%d" % i, [128, 512], F32)) for i in range(8)]
    hTb = [Buf("hT%d" % i) for i in range(4)]
    oaTb = [Buf("oaT%d" % i) for i in range(4)]
    cfb = Buf("cf")

    pbank = [Buf("psb%d" % i, excl=True) for i in range(8)]

    def psbuf(bank, c0=0, c1=512):
        return pbank[bank]

    apos = [0]

    def carve(name, free_shape, dt=F32):
        esz = 4 if dt == F32 else 2
        n = int(np.prod(free_shape))
        nb = (n * esz + 31) // 32 * 32
        off = apos[0]
        apos[0] += nb
        assert apos[0] <= ARENA_BYTES, (name, apos[0])
        ap = arena[:, off // 4:(off + nb) // 4]
        if dt != F32:
            ap = ap.bitcast(dt)
        ap = ap[:, 0:n]
        if len(free_shape) == 2:
            ap = ap.rearrange("p (a b) -> p a b", a=free_shape[0])
        elif len(free_shape) == 3:
            ap = ap.rearrange("p (a b c) -> p a b c", a=free_shape[0], b=free_shape[1])
        return ap, Buf(name, iv=("arena", off, off + nb))

    def subbufs(parent, name, pieces):
        base = parent.iv[0][1]
        return [Buf("%s_%d" % (name, i), iv=[("arena", base + lo, base + hi) for lo, hi in pc])
                for i, pc in enumerate(pieces)]

    def cblk(k):
        return cf[:, k * 128:(k + 1) * 128]

    def dma(eng, out_ap, in_ap, wbuf, reads=(), waw=True):
        def fn(e):
            return e.dma_start(out=out_ap, in_=in_ap).then_inc(wbuf.sem, 16)
        P.add(eng, fn, reads=reads, writes=[wbuf], ndma=1, waw=waw)

    def load_w(dst_ap, wdram, c0, ncols, wbuf):
        src = wdram[:, c0:c0 + ncols].rearrange("(kc p) c -> p kc c", p=128)
        dma("pool", dst_ap, src, wbuf)

    outb = Buf("outdram")
    outsl = []
    dbg_outs = {}

    def tap(name, ap, shape, dt, rbuf):
        d = dram("dbg_" + name, shape, dt, kind="ExternalOutput")
        dbg_outs[name] = d
        dma("sp", d, ap, outb, reads=rbuf, waw=False)

    dma("sp", cf[:], cf_d, cfb)
    small = sb("small", [128, 1280], F32)
    epsc = small[:, 0:2]
    epsb = Buf("epsc")
    P.add("dve", lambda e: e.memset(small[:, 0:1], EPS), writes=[epsb])
    P.add("dve", lambda e: e.memset(small[:, 1:2], 1.0), writes=[epsb])
    st = small[:, 8:16]
    stb = Buf("st")
    I_ = cblk(C_ID)

    amark = apos[0]
    apos[0] = ARENA_BYTES - 4 * D * 4
    xs, xsb = zip(*[carve("xs%d" % i, [D]) for i in range(2)])
    xn, xnb = carve("xn", [D])
    prew, prewb = carve("prew", [D])
    dma("sp", prew, pre_w.partition_broadcast(128), prewb)
    for tt in range(NT):
        sl = tt % 2
        dma("sp", xs[sl], x[tt * 128:(tt + 1) * 128, :], xsb[sl])
        P.add("act", lambda e, sl=sl: e.activation(out=xn, in_=xs[sl], func=AF.Square, accum_out=st[:, 0:1]),
              reads=[xsb[sl]], writes=[xnb, stb])
        P.add("act", lambda e: e.activation(out=st[:, 1:2], in_=st[:, 0:1], func=AF.Sqrt, bias=epsc[:, 0:1],
                                            scale=1.0 / D), reads=[stb, epsb], writes=[stb])
        P.add("dve", lambda e: e.reciprocal(out=st[:, 2:3], in_=st[:, 1:2]), reads=[stb], writes=[stb])
        P.add("dve", lambda e, sl=sl: e.scalar_tensor_tensor(out=xn, in0=xs[sl], scalar=st[:, 2:3], in1=prew,
                                                             op0=ALU.mult, op1=ALU.mult),
              reads=[xsb[sl], stb, prewb], writes=[xnb])
        for g in range(4):
            bank = (tt % 2) * 4 + g
            pb = psbuf(bank)

            def tr(e, g=g, bank=bank):
                ins = None
                for j in range(4):
                    kc = g * 4 + j
                    ins = e.transpose(out=ps[bank][:, j * 128:(j + 1) * 128], in_=xn[:, kc * 128:(kc + 1) * 128],
                                      identity=I_)
                return ins
            P.add("pe", tr, reads=[xnb, cfb], writes=[pb])
            dst = hT[:, g * 4:(g + 1) * 4, tt * 128:(tt + 1) * 128]
            src = ps[bank][:].rearrange("p (a b) -> p a b", a=4)
            if g % 2 == 0:
                P.add("act", lambda e, dst=dst, src=src: e.copy(out=dst, in_=src), reads=[pb], writes=[hTb[tt // 4]])
            else:
                P.add("dve", lambda e, dst=dst, src=src: e.tensor_copy(out=dst, in_=src), reads=[pb],
                      writes=[hTb[tt // 4]])
    if "hT" in dbg:
        tap("hT", hT[:], [128, KC, S], BF16, hTb)
    apos[0] = amark

    if "B" in phases:
        bmark = apos[0]
        betas = small[:, 16:144].rearrange("p (t h) -> p t h", t=NT)
        nbetas = small[:, 144:272].rearrange("p (t h) -> p t h", t=NT)
        gs = small[:, 272:400].rearrange("p (t h) -> p t h", t=NT)
        bgs = small[:, 400:528].rearrange("p (t h) -> p t h", t=NT)
        egs = small[:, 528:912].rearrange("p (t c) -> p t c", t=NT)
        negA = small[:, 912:1040]
        dtb = small[:, 1040:1168]
        gnw = small[:, 1168:1169]
        cw = small[:, 1172:1268].rearrange("p (c j) -> p c j", c=24)
        gqb = Buf("gatesq")
        cwb = Buf("cw")
        xa, xab = carve("xa", [128])
        wg, wgb = carve("wg", [KC, 16], BF16)
        load_w(wg, w_in, OFF_GB, 16, wgb)
        dma("sp", negA, a_log.partition_broadcast(128), gqb)
        dma("sp", dtb, dt_bias.partition_broadcast(128), gqb)
        dma("sp", gnw, gnw_d, gqb)
        dma("sp", cw, conv_wT.rearrange("(c p) j -> p c j", p=128), cwb)
        P.add("act", lambda e: e.activation(out=negA, in_=negA, func=AF.Exp), reads=[gqb], writes=[gqb])
        P.add("dve", lambda e: e.tensor_scalar(out=negA, in0=negA, scalar1=-1.0, scalar2=None, op0=ALU.mult),
              reads=[gqb], writes=[gqb])
        pg = psbuf(0)
        for tt in range(NT):
            def mm(e, tt=tt):
                ins = None
                for kc in range(KC):
                    ins = e.matmul(ps[0][:, tt * 16:(tt + 1) * 16], lhsT=hT[:, kc, tt * 128:(tt + 1) * 128],
                                   rhs=wg[:, kc, :], start=(kc == 0), stop=(kc == KC - 1))
                return ins
            P.add("pe", mm, reads=[hTb[tt // 4], wgb], writes=[pg])
        pgv = ps[0][:, 0:256].rearrange("p (t c) -> p t c", t=NT)
        P.add("act", lambda e: e.activation(out=betas, in_=pgv[:, :, 0:8], func=AF.Sigmoid), reads=[pg], writes=[gqb])
        P.add("dve", lambda e: e.tensor_tensor(out=xa.rearrange("p (t h) -> p t h", t=NT), in0=pgv[:, :, 8:16],
                                               in1=dtb.rearrange("p (t h) -> p t h", t=NT), op=ALU.add),
              reads=[pg, gqb], writes=[xab])
        P.add("act", lambda e: e.activation(out=xa, in_=xa, func=AF.Exp), reads=[xab], writes=[xab])
        P.add("act", lambda e: e.activation(out=xa, in_=xa, func=AF.Ln, bias=epsc[:, 1:2]), reads=[xab, epsb],
              writes=[xab])
        P.add("dve", lambda e: e.tensor_tensor(out=small[:, 272:400], in0=xa, in1=negA, op=ALU.mult),
              reads=[xab, gqb], writes=[gqb])
        P.add("dve", lambda e: e.tensor_scalar(out=small[:, 144:272], in0=small[:, 16:144], scalar1=-1.0, scalar2=None,
                                               op0=ALU.mult), reads=[gqb], writes=[gqb])
        pg2 = psbuf(1)
        for tt in range(NT):
            def mm2(e, tt=tt):
                e.matmul(ps[1][:, tt * 24:tt * 24 + 8], lhsT=cblk(C_TRI), rhs=gs[:, tt, :], start=True, stop=True)
                e.matmul(ps[1][:, tt * 24 + 8:tt * 24 + 16], lhsT=cblk(C_UG), rhs=gs[:, tt, :], start=True, stop=True)
                return e.matmul(ps[1][:, tt * 24 + 16:tt * 24 + 24], lhsT=cblk(C_ONES), rhs=gs[:, tt, :], start=True,
                                stop=True)
            P.add("pe", mm2, reads=[gqb, cfb], writes=[pg2])
        P.add("act", lambda e: e.activation(out=small[:, 528:912], in_=ps[1][:, 0:384], func=AF.Exp), reads=[pg2],
              writes=[gqb])
        P.add("dve", lambda e: e.tensor_tensor(out=bgs, in0=betas, in1=egs[:, :, 0:8], op=ALU.mult), reads=[gqb],
              writes=[gqb])
        if "gates" in dbg:
            tap("gates", small[:, 16:912], [128, 896], F32, [gqb])

        if "0" in phases:
            nheads_g = 0
        NW = 4
        wr, wrb = zip(*[carve("wr%d" % i, [KC, 128], BF16) for i in range(NW)])
        xraw, xrp = carve("xraw", [3 + S])
        xrb = subbufs(xrp, "xraw", [[(0, 12)]] + [[(12 + i * 2048, 12 + (i + 1) * 2048)] for i in range(4)])
        qT, qTp = carve("qT", [S])
        kT, kTp = carve("kT", [S])
        vT, vTp = carve("vT", [S])
        seg4 = [[(i * 2048, (i + 1) * 2048)] for i in range(4)]
        tTb = {"q": subbufs(qTp, "qT", seg4), "k": subbufs(kTp, "kT", seg4), "v": subbufs(vTp, "vT", seg4)}
        szT, szp = carve("szT", [S], BF16)
        szb = subbufs(szp, "szT", [[(i * 1024, (i + 1) * 1024)] for i in range(4)])
        sqt, sqtb = zip(*[carve("sqt%d" % i, [512], BF16) for i in range(2)])
        c16, c16b = carve("c16", [256], BF16)
        Ib16 = c16[:, 0:128]
        ones16 = c16[:, 128:256]
        P.add("dve", lambda e: e.tensor_copy(out=Ib16, in_=cblk(C_ID)), reads=[cfb], writes=[c16b])
        P.add("dve", lambda e: e.tensor_copy(out=ones16, in_=cblk(C_ONES)), reads=[cfb], writes=[c16b])
        Sb16, Sb16b = carve("S16", [128], BF16)
        rs, rsb = carve("rs", [512])
        Ssb, Sb = carve("S", [128])
        NSL = 4
        slot = []
        for i in range(NSL):
            d_ = {}
            for nm in ("Gt", "Ln", "N0", "N1", "M0", "M1", "Xs", "kbg", "vb"):
                d_[nm] = carve("%s_%d" % (nm, i), [128])
            for nm in ("iT", "kdec", "u", "wTn", "qb"):
                d_[nm] = [carve("%s_%d_%d" % (nm, i, p_), [128], F32 if nm == "u" else BF16) for p_ in range(2)]
            d_["EE"] = carve("EE_%d" % i, [256])
            slot.append(d_)
        vnew, vnewb = carve("vnew", [128], BF16)
        t1, t1b = carve("t1", [128])
        osb, osbb = carve("osb", [128])
        onb_, onbb = carve("on", [128], BF16)
        P.add("dve", lambda e: e.memset(xraw[:, 0:3], 0.0), writes=[xrb[0]])

        PBANKS = [0, 1, 3, 4, 5, 6]
        pl2 = psbuf(7)
        pscan = psbuf(2)
        pT = psbuf(7)

        def q4(bank, q):
            return ps[bank][:, q * 128:(q + 1) * 128]

        wcount = [0]

        def proj_fm(col0, dst_fn, pidx):
            wi = wcount[0] % NW
            wcount[0] += 1
            load_w(wr[wi], w_in, col0, 128, wrb[wi])
            for seg in range(4):
                bank = PBANKS[pidx[0] % len(PBANKS)]
                pidx[0] += 1

                def mm(e, seg=seg, bank=bank, wi=wi):
                    ins = None
                    for kc in range(KC):
                        ins = e.matmul(ps[bank][:, :], lhsT=wr[wi][:, kc, :], rhs=hT[:, kc, seg * 512:(seg + 1) * 512],
                                       start=(kc == 0), stop=(kc == KC - 1))
                    return ins
                P.add("pe", mm, reads=[hTb[seg], wrb[wi]], writes=[pbank[bank]])
                nxt = dst_fn(seg, bank)
                while l2pend:
                    l2pend.pop(0)()
                if nxt is not None:
                    l2pend.append(nxt)

        pidx = [0]
        l2pend = []
        SCQ = 128.0 ** -0.5
        for h in range(nheads_g):
            for ti, nm in enumerate("qkv"):
                ci = ti * 8 + h
                tT = {"q": qT, "k": kT, "v": vT}[nm]

                def consume(seg, bank, nm=nm, ci=ci, tT=tT):
                    s0 = seg * 512
                    P.add("act", lambda e: e.copy(out=xraw[:, 3 + s0:3 + s0 + 512], in_=ps[bank][:, :]),
                          reads=[pbank[bank]], writes=[xrb[seg + 1]])
                    dst = tT[:, s0:s0 + 512]
                    db = tTb[nm][seg]
                    P.add("dve", lambda e: e.tensor_scalar(out=dst, in0=xraw[:, 3 + s0:3 + s0 + 512],
                                                           scalar1=cw[:, ci, 3:4], scalar2=None, op0=ALU.mult),
                          reads=[xrb[seg + 1], cwb], writes=[db])
                    for j in range(3):
                        P.add("dve", lambda e, j=j: e.scalar_tensor_tensor(out=dst, in0=xraw[:, j + s0:j + s0 + 512],
                                                                           scalar=cw[:, ci, j:j + 1], in1=dst,
                                                                           op0=ALU.mult, op1=ALU.add),
                              reads=[xrb[seg + 1], xrb[seg], cwb, db], writes=[db])
                    P.add("act", lambda e: e.activation(out=dst, in_=dst, func=AF.Silu), reads=[db], writes=[db])
                    if nm in "qk":
                        si = seg % 2
                        P.add("act", lambda e: e.activation(out=sqt[si], in_=dst, func=AF.Square), reads=[db],
                              writes=[sqtb[si]])

                        def tail():
                            P.add("pe", lambda e: e.matmul(ps[7][:, :], lhsT=ones16, rhs=sqt[si], start=True,
                                                           stop=True), reads=[sqtb[si], c16b], writes=[pl2])
                            P.add("act", lambda e: e.activation(out=rs, in_=ps[7][:, :], func=AF.Sqrt,
                                                                bias=epsc[:, 0:1]), reads=[pl2, epsb], writes=[rsb])
                            P.add("dve", lambda e: e.reciprocal(out=rs, in_=rs), reads=[rsb], writes=[rsb])
                            if nm == "q":
                                P.add("dve", lambda e: e.scalar_tensor_tensor(out=dst, in0=dst, scalar=SCQ, in1=rs,
                                                                              op0=ALU.mult, op1=ALU.mult),
                                      reads=[db, rsb], writes=[db])
                            else:
                                P.add("dve", lambda e: e.tensor_tensor(out=dst, in0=dst, in1=rs, op=ALU.mult),
                                      reads=[db, rsb], writes=[db])
                        return tail
                    return None
                proj_fm([OFF_GQ, OFF_GK, OFF_GV][ti] + h * 128, consume, pidx)

            def consume_z(seg, bank):
                s0 = seg * 512
                P.add("act", lambda e: e.activation(out=szT[:, s0:s0 + 512], in_=ps[bank][:, :], func=AF.Silu),
                      reads=[pbank[bank]], writes=[szb[seg]])
            proj_fm(OFF_GZ + h * 128, consume_z, pidx)
            while l2pend:
                l2pend.pop(0)()
            if "qkv" in dbg and h == 0:
                tap("qT", qT, [128, S], F32, tTb["q"])
                tap("kT", kT, [128, S], F32, tTb["k"])
                tap("vT", vT, [128, S], F32, tTb["v"])

            P.add("dve", lambda e: e.memset(Ssb, 0.0), writes=[Sb])
            P.add("dve", lambda e: e.memset(Sb16, 0.0), writes=[Sb16b])

            def pre_stages(tt, sl, par, h=h):
                sd = dict(slot[sl])
                for nm_ in ("iT", "kdec", "u", "wTn", "qb"):
                    sd[nm_] = slot[sl][nm_][par]
                ts = slice(tt * 128, (tt + 1) * 128)
                sg = tt // 4
                bk = 3 + sl
                pb = pbank[bk]
                stages = []

                def s0():
                    def trKV(e):
                        e.matmul(q4(bk, 0), lhsT=kT[:, ts], rhs=I_, start=True, stop=True)
                        return e.matmul(q4(bk, 1), lhsT=vT[:, ts], rhs=I_, start=True, stop=True)
                    P.add("pe", trKV, reads=[tTb["k"][sg], tTb["v"][sg], cfb], writes=[pb])
                    P.add("dve", lambda e: e.tensor_scalar(out=sd["kbg"][0], in0=q4(bk, 0), scalar1=bgs[:, tt, h:h + 1],
                                                           scalar2=None, op0=ALU.mult),
                          reads=[pb, gqb], writes=[sd["kbg"][1]])
                    P.add("dve", lambda e: e.tensor_scalar(out=sd["kdec"][0], in0=q4(bk, 0),
                                                           scalar1=egs[:, tt, 8 + h:9 + h], scalar2=None, op0=ALU.mult),
                          reads=[pb, gqb], writes=[sd["kdec"][1]])
                    P.add("dve", lambda e: e.tensor_scalar(out=sd["vb"][0], in0=q4(bk, 1), scalar1=betas[:, tt, h:h + 1],
                                                           scalar2=None, op0=ALU.mult),
                          reads=[pb, gqb], writes=[sd["vb"][1]])
                    P.add("dve", lambda e: e.tensor_scalar(out=sd["Gt"][0], in0=cblk(C_TRI), scalar1=gs[:, tt, h:h + 1],
                                                           scalar2=None, op0=ALU.mult),
                          reads=[cfb, gqb], writes=[sd["Gt"][1]])
                stages.append(s0)

                def s1():
                    def mmD(e):
                        e.matmul(q4(bk, 2), lhsT=sd["Gt"][0], rhs=cblk(C_UG), start=True, stop=True)
                        e.matmul(q4(bk, 3), lhsT=cblk(C_UG), rhs=sd["Gt"][0], start=True, stop=False)
                        e.matmul(q4(bk, 3), lhsT=I_, rhs=cblk(C_NBI), start=False, stop=True)
                        e.matmul(q4(bk, 0), lhsT=kT[:, ts], rhs=kT[:, ts], start=True, stop=True)
                        return e.matmul(q4(bk, 1), lhsT=kT[:, ts], rhs=qT[:, ts], start=True, stop=True)
                    P.add("pe", mmD, reads=[sd["Gt"][1], cfb, tTb["k"][sg], tTb["q"][sg]], writes=[pb])
                    P.add("act", lambda e: e.activation(out=sd["EE"][0], in_=ps[bk][:, 256:512], func=AF.Exp),
                          reads=[pb], writes=[sd["EE"][1]])
                    P.add("dve", lambda e: e.scalar_tensor_tensor(out=sd["Ln"][0], in0=q4(bk, 0),
                                                                  scalar=nbetas[:, tt, h:h + 1], in1=sd["EE"][0][:, 0:128],
                                                                  op0=ALU.mult, op1=ALU.mult),
                          reads=[pb, gqb, sd["EE"][1]], writes=[sd["Ln"][1]])
                    P.add("dve", lambda e: e.tensor_tensor(out=sd["iT"][0], in0=q4(bk, 1), in1=sd["EE"][0][:, 128:256],
                                                           op=ALU.mult),
                          reads=[pb, sd["EE"][1]], writes=[sd["iT"][1]])
                    P.add("act", lambda e: e.copy(out=sd["qb"][0], in_=qT[:, ts]), reads=[tTb["q"][sg]],
                          writes=[sd["qb"][1]])
                stages.append(s1)

                cur = {"N": (I_, cfb), "M": (I_, cfb)}
                for l in range(7):
                    def sA(l=l):
                        Ncur, Nb = cur["N"]
                        P.add("pe", lambda e: e.matmul(q4(bk, 0), lhsT=sd["Ln"][0], rhs=Ncur, start=True, stop=True),
                              reads=[sd["Ln"][1], Nb], writes=[pb])
                        P.add("dve", lambda e: e.tensor_tensor(out=sd["Xs"][0], in0=q4(bk, 0), in1=cblk(C_MU0 + l),
                                                               op=ALU.mult),
                              reads=[pb, cfb], writes=[sd["Xs"][1]])
                    stages.append(sA)

                    def sB(l=l):
                        Ncur, Nb = cur["N"]
                        Mcur, Mb = cur["M"]
                        Nn = sd["N%d" % (l % 2)]
                        Mn = sd["M%d" % (l % 2)]

                        def mmNM(e):
                            ins = None
                            if l > 0:
                                ins = e.matmul(q4(bk, 1), lhsT=Mcur, rhs=sd["Xs"][0], start=True, stop=True)
                            if l < 6:
                                ins = e.matmul(q4(bk, 2), lhsT=sd["Xs"][0], rhs=Mcur, start=True, stop=True)
                            return ins
                        P.add("pe", mmNM, reads=[Nb, Mb, sd["Xs"][1], cfb], writes=[pb])
                        if l > 0:
                            P.add("dve", lambda e: e.tensor_tensor(out=Nn[0], in0=q4(bk, 1), in1=Ncur, op=ALU.add),
                                  reads=[pb, Nb], writes=[Nn[1]])
                        else:
                            P.add("dve", lambda e: e.tensor_tensor(out=Nn[0], in0=sd["Xs"][0], in1=Ncur, op=ALU.add),
                                  reads=[sd["Xs"][1], Nb], writes=[Nn[1]])
                        if l < 6:
                            P.add("dve", lambda e: e.tensor_tensor(out=Mn[0], in0=q4(bk, 2), in1=Mcur, op=ALU.add),
                                  reads=[pb, Mb], writes=[Mn[1]])
                        cur["N"] = Nn
                        cur["M"] = Mn
                    stages.append(sB)

                def sF():
                    Nf, Nfb = cur["N"]

                    def mmUW(e):
                        e.matmul(q4(bk, 0), lhsT=Nf, rhs=sd["vb"][0], start=True, stop=True)
                        return e.matmul(q4(bk, 1), lhsT=sd["kbg"][0], rhs=Nf, start=True, stop=True)
                    P.add("pe", mmUW, reads=[Nfb, sd["vb"][1], sd["kbg"][1]], writes=[pb])
                    P.add("act", lambda e: e.copy(out=sd["u"][0], in_=q4(bk, 0)), reads=[pb], writes=[sd["u"][1]])
                    P.add("act", lambda e: e.mul(out=sd["wTn"][0], in_=q4(bk, 1), mul=-1.0), reads=[pb],
                          writes=[sd["wTn"][1]])
                stages.append(sF)
                return stages

            def scan(tt, sl, par, h=h):
                sd = dict(slot[sl])
                for nm_ in ("iT", "kdec", "u", "wTn", "qb"):
                    sd[nm_] = slot[sl][nm_][par]
                ts = slice(tt * 128, (tt + 1) * 128)
                sg = tt // 4
                subs = []

                def subA():
                    P.add("pe", mmV, reads=[sd["wTn"][1], Sb16b], writes=[pscan])
                    P.add("dve", lambda e: e.tensor_tensor(out=vnew, in0=q4(2, 0), in1=sd["u"][0], op=ALU.add),
                          reads=[pscan, sd["u"][1]], writes=[vnewb])

                def subB():
                    P.add("pe", mmO, reads=[sd["qb"][1], Sb16b, sd["iT"][1], sd["kdec"][1], vnewb], writes=[pscan])
                    P.add("act", lambda e: e.mul(out=t1, in_=q4(2, 1), mul=egs[:, tt, h:h + 1]), reads=[pscan, gqb],
                          writes=[t1b])
                    P.add("dve", lambda e: e.tensor_tensor(out=osb, in0=t1, in1=q4(2, 2), op=ALU.add),
                          reads=[t1b, pscan], writes=[osbb])
                    P.add("dve", lambda e: e.scalar_tensor_tensor(out=Ssb, in0=Ssb, scalar=egs[:, tt, 16 + h:17 + h],
                                                                  in1=q4(2, 3), op0=ALU.mult, op1=ALU.add),
                          reads=[Sb, gqb, pscan], writes=[Sb])
                    P.add("act", lambda e: e.copy(out=Sb16, in_=Ssb), reads=[Sb], writes=[Sb16b])

                def subC():
                    P.add("act", lambda e: e.activation(out=t1, in_=osb, func=AF.Square, accum_out=st[:, 4:5]),
                          reads=[osbb], writes=[t1b, stb])
                    P.add("act", lambda e: e.activation(out=st[:, 5:6], in_=st[:, 4:5], func=AF.Sqrt, bias=epsc[:, 0:1],
                                                        scale=1.0 / 128), reads=[stb, epsb], writes=[stb])
                    P.add("dve", lambda e: e.reciprocal(out=st[:, 6:7], in_=st[:, 5:6]), reads=[stb], writes=[stb])
                    P.add("act", lambda e: e.mul(out=onb_, in_=osb, mul=st[:, 6:7]), reads=[osbb, stb], writes=[onbb])

                def subD():
                    P.add("pe", lambda e: e.matmul(q4(7, 0), lhsT=onb_, rhs=Ib16, start=True, stop=True),
                          reads=[onbb, c16b], writes=[pT])
                    P.add("dve", lambda e: e.scalar_tensor_tensor(out=oaT[:, h, ts], in0=q4(7, 0), scalar=gnw,
                                                                  in1=szT[:, ts], op0=ALU.mult, op1=ALU.mult),
                          reads=[pT, gqb, szb[sg]], writes=[oaTb[sg]])

                def mmV(e):
                    return e.matmul(q4(2, 0), lhsT=sd["wTn"][0], rhs=Sb16, start=True, stop=True)

                def mmO(e):
                    e.matmul(q4(2, 1), lhsT=sd["qb"][0], rhs=Sb16, start=True, stop=True)
                    e.matmul(q4(2, 2), lhsT=sd["iT"][0], rhs=vnew, start=True, stop=True)
                    return e.matmul(q4(2, 3), lhsT=sd["kdec"][0], rhs=vnew, start=True, stop=True)
                return [subA, subB, subC, subD]

            pend = []
            for gi, g0 in enumerate(range(0, NT, NSL)):
                sts = [pre_stages(g0 + i, i, gi % 2) for i in range(NSL)]
                for k in range(len(sts[0])):
                    for i in range(NSL):
                        sts[i][k]()
                    if pend:
                        pend.pop(0)()
                while pend:
                    pend.pop(0)()
                for i in range(NSL):
                    pend.extend(scan(g0 + i, i, gi % 2))
            while pend:
                pend.pop(0)()
        print("phase B arena bytes", apos[0])
        if "oaT" in dbg:
            tap("oaT", oaT[:], [128, H, S], BF16, oaTb)
        if "scan" in dbg:
            tap("osb", osb, [128, 128], F32, [osbb])
            tap("S", Ssb, [128, 128], F32, [Sb])
            tap("vnew", vnew, [128, 128], F32, [vnewb])
        apos[0] = bmark

    obT, obTp = carve("obT", [H, S], BF16)
    obTb = subbufs(obTp, "obT", [[(hh * 4096 + q * 1024, hh * 4096 + (q + 1) * 1024) for hh in range(H)] for q in range(4)])
    cmark = apos[0]
    if "C" in phases:
        NWC = 4
        cwr, cwrb = zip(*[carve("cwr%d" % i, [KC, 128], BF16) for i in range(NWC)])
        cmf, cmfb = carve("cmf", [1280])
        cmb, cmbb = carve("cmb", [1536], BF16)
        dma("sp", cmf, cm_d, cmfb)
        if "7" not in phases:
            P.add("dve", lambda e: e.tensor_copy(out=cmb[:, 0:1280], in_=cmf), reads=[cmfb], writes=[cmbb])
        P.add("dve", lambda e: e.tensor_copy(out=cmb[:, 1280:1408], in_=cblk(C_ID)), reads=[cfb], writes=[cmbb])
        P.add("dve", lambda e: e.tensor_copy(out=cmb[:, 1408:1536], in_=cblk(C_ONES)), reads=[cfb], writes=[cmbb])
        cbias = cmb[:, 0:256]
        Ib = cmb[:, 1280:1408]
        onesb = cmb[:, 1408:1536]
        qTb_, qTbb = carve("mqTb", [S], BF16)
        kTb_, kTbb = carve("mkTb", [S], BF16)
        qTf, qTfb = carve("mqTf", [S])
        Vt, Vtb = carve("mV", [NT, 128], BF16)
        mszT, mszb = carve("mszT", [S], BF16)
        nbT, nbTb = carve("nbT", [S], BF16)
        kmT, kmTb = carve("kmT", [8])
        gm, gmb = carve("gm", [NT, 8])
        nb, nbb = carve("nb", [NT, 8])
        top8, top8b = carve("top8", [8])
        pTr, pTrb = zip(*[carve("pT%d" % i, [256], BF16) for i in range(4)])
        rr, rrb = carve("rr", [256])
        zz, zzb = carve("zz", [256])
        cwcount = [0]

        def proj_c(col0, consume):
            wi = cwcount[0] % NWC
            cwcount[0] += 1
            load_w(cwr[wi], w_in, col0, 128, cwrb[wi])
            for seg in range(4):
                bank = seg % 2

                def mm(e, seg=seg, bank=bank, wi=wi):
                    ins = None
                    for kc in range(KC):
                        ins = e.matmul(ps[bank][:, :], lhsT=cwr[wi][:, kc, :], rhs=hT[:, kc, seg * 512:(seg + 1) * 512],
                                       start=(kc == 0), stop=(kc == KC - 1))
                    return ins
                P.add("pe", mm, reads=[hTb[seg], cwrb[wi]], writes=[pbank[bank]])
                consume(seg, bank)

        for h in range(nheads_m):
            def cons_q(seg, bank):
                s0 = seg * 512
                P.add("act", lambda e: e.mul(out=qTf[:, s0:s0 + 512], in_=ps[bank][:, :], mul=SCQ_M),
                      reads=[pbank[bank]], writes=[qTfb])
                P.add("dve", lambda e: e.tensor_copy(out=qTb_[:, s0:s0 + 512], in_=qTf[:, s0:s0 + 512]),
                      reads=[qTfb], writes=[qTbb])
            SCQ_M = 128.0 ** -0.5
            proj_c(OFF_MQ + h * 128, cons_q)

            def cons_k(seg, bank):
                s0 = seg * 512
                P.add("act", lambda e: e.copy(out=kTb_[:, s0:s0 + 512], in_=ps[bank][:, :]),
                      reads=[pbank[bank]], writes=[kTbb])
                P.add("dve", lambda e: e.tensor_reduce(out=kmT[:, seg * 2:seg * 2 + 2],
                                                       in_=ps[bank][:, :].rearrange("p (a b) -> p a b", a=2),
                                                       axis=AX.X, op=ALU.add),
                      reads=[pbank[bank]], writes=[kmTb])
            proj_c(OFF_MK + h * 128, cons_k)

            def cons_z(seg, bank):
                s0 = seg * 512
                P.add("act", lambda e: e.activation(out=mszT[:, s0:s0 + 512], in_=ps[bank][:, :], func=AF.Silu),
                      reads=[pbank[bank]], writes=[mszb])
            proj_c(OFF_MZ + h * 128, cons_z)

            wi = cwcount[0] % NWC
            cwcount[0] += 1
            load_w(cwr[wi], w_in, OFF_MV + h * 128, 128, cwrb[wi])
            for g in range(4):
                bank = 2 + g % 2

                def mmv(e, g=g, bank=bank, wi=wi):
                    ins = None
                    for j in range(4):
                        tt = g * 4 + j
                        for kc in range(KC):
                            ins = e.matmul(ps[bank][:, j * 128:(j + 1) * 128], lhsT=hT[:, kc, tt * 128:(tt + 1) * 128],
                                           rhs=cwr[wi][:, kc, :], start=(kc == 0), stop=(kc == KC - 1))
                    return ins
                P.add("pe", mmv, reads=[hTb[g], cwrb[wi]], writes=[pbank[bank]])
                P.add("act", lambda e, g=g, bank=bank: e.copy(out=Vt[:, g * 4:(g + 1) * 4, :],
                                                              in_=ps[bank][:, :].rearrange("p (a b) -> p a b", a=4)),
                      reads=[pbank[bank]], writes=[Vtb])

            P.add("dve", lambda e: e.memset(gm, -1e30), writes=[gmb])
            P.add("dve", lambda e: e.memset(nb, 0.0), writes=[nbb])

            def mmg(e):
                ins = None
                for tt in range(8, NT):
                    ins = e.matmul(ps[4][:, tt * 8:tt * 8 + 8], lhsT=qTf[:, tt * 128:(tt + 1) * 128], rhs=kmT,
                                   start=True, stop=True)
                return ins
            P.add("pe", mmg, reads=[qTfb, kmTb], writes=[pbank[4]])
            for tt in range(8, NT):
                qb = tt // 2
                P.add("dve", lambda e, tt=tt, qb=qb: e.tensor_copy(out=gm[:, tt, 0:qb], in_=ps[4][:, tt * 8:tt * 8 + qb]),
                      reads=[pbank[4]], writes=[gmb])
                P.add("dve", lambda e, tt=tt: e.max(out=top8, in_=gm[:, tt, :]), reads=[gmb], writes=[top8b])
                P.add("dve", lambda e, tt=tt: e.tensor_scalar(out=nb[:, tt, :], in0=gm[:, tt, :], scalar1=top8[:, 2:3],
                                                              scalar2=NEG, op0=ALU.is_lt, op1=ALU.mult),
                      reads=[gmb, top8b], writes=[nbb])
            for g in range(4):
                bank = 5 + g % 2

                def mmt(e, g=g, bank=bank):
                    ins = None
                    for j in range(4):
                        tt = g * 4 + j
                        ins = e.matmul(ps[bank][0:8, j * 128:(j + 1) * 128], lhsT=nb[:, tt, :], rhs=I_, start=True,
                                       stop=True)
                    return ins
                P.add("pe", mmt, reads=[nbb, cfb], writes=[pbank[bank]])
                P.add("act", lambda e, g=g, bank=bank: e.copy(out=nbT[0:8, g * 512:(g + 1) * 512], in_=ps[bank][0:8, :]),
                      reads=[pbank[bank]], writes=[nbTb])
            if "moba_pre" in dbg and h == 0:
                tap("nbT", nbT[0:8, :], [8, S], BF16, [nbTb])
                tap("mqT", qTf, [128, S], F32, [qTfb])
                tap("mV", Vt, [128, NT, 128], BF16, [Vtb])

            for qb in range(8):
                q0 = qb * 256
                nk = 2 * qb + 2
                ob_bank = 6 + qb % 2
                pob = pbank[ob_bank]

                def emit_S(kt, qb=qb, q0=q0):
                    n = kt // 2
                    sbank = kt % 4
                    if kt == 2 * qb + 1:
                        def mm(e):
                            e.matmul(ps[sbank][:, 128:256], lhsT=kTb_[:, kt * 128:(kt + 1) * 128],
                                     rhs=qTb_[:, q0 + 128:q0 + 256], start=True, stop=False)
                            return e.matmul(ps[sbank][:, 128:256], lhsT=Ib, rhs=cbias[:, 0:128], start=False, stop=True)
                    elif kt == 2 * qb:
                        def mm(e):
                            e.matmul(ps[sbank][:, 0:256], lhsT=kTb_[:, kt * 128:(kt + 1) * 128], rhs=qTb_[:, q0:q0 + 256],
                                     start=True, stop=False)
                            return e.matmul(ps[sbank][:, 0:256], lhsT=Ib, rhs=cbias, start=False, stop=True)
                    elif qb <= 3:
                        def mm(e):
                            return e.matmul(ps[sbank][:, 0:256], lhsT=kTb_[:, kt * 128:(kt + 1) * 128],
                                            rhs=qTb_[:, q0:q0 + 256], start=True, stop=True)
                    else:
                        def mm(e):
                            e.matmul(ps[sbank][:, 0:256], lhsT=kTb_[:, kt * 128:(kt + 1) * 128], rhs=qTb_[:, q0:q0 + 256],
                                     start=True, stop=False)
                            return e.matmul(ps[sbank][:, 0:256], lhsT=cmb[0:8, 256 + n * 128:256 + (n + 1) * 128],
                                            rhs=nbT[0:8, q0:q0 + 256], start=False, stop=True)
                    P.add("pe", mm, reads=[kTbb, qTbb, cmbb, nbTb], writes=[pbank[sbank]])

                def emit_PV(kt, qb=qb, q0=q0, nk=nk, ob_bank=ob_bank, pob=pob):
                    sbank = kt % 4
                    c0 = 128 if kt == 2 * qb + 1 else 0
                    pt = pTr[kt % 4]
                    P.add("act", lambda e: e.activation(out=pt[:, c0:256], in_=ps[sbank][:, c0:256], func=AF.Exp),
                          reads=[pbank[sbank]], writes=[pTrb[kt % 4]])

                    def mm(e):
                        e.matmul(ps[ob_bank][:, c0:256], lhsT=Vt[:, kt, :], rhs=pt[:, c0:256], start=(kt == 0),
                                 stop=(kt == nk - 1))
                        return e.matmul(ps[ob_bank][:, 256 + c0:512], lhsT=onesb, rhs=pt[:, c0:256], start=False,
                                        stop=(kt == nk - 1), skip_group_check=True)
                    P.add("pe", mm, reads=[Vtb, pTrb[kt % 4], cmbb], writes=[pob])

                emit_S(0)
                emit_S(1)
                for kt in range(nk):
                    emit_PV(kt)
                    if kt + 2 < nk:
                        emit_S(kt + 2)
                P.add("dve", lambda e, ob_bank=ob_bank: e.reciprocal(out=rr, in_=ps[ob_bank][:, 256:512]),
                      reads=[pob], writes=[rrb])
                P.add("dve", lambda e, q0=q0: e.tensor_tensor(out=zz, in0=rr, in1=mszT[:, q0:q0 + 256], op=ALU.mult),
                      reads=[rrb, mszb], writes=[zzb])
                P.add("dve", lambda e, q0=q0, ob_bank=ob_bank, h=h: e.tensor_tensor(out=obT[:, h, q0:q0 + 256],
                                                                                   in0=ps[ob_bank][:, 0:256], in1=zz,
                                                                                   op=ALU.mult),
                      reads=[pob, zzb], writes=[obTb[qb // 2]])
        if "obT" in dbg:
            tap("obT", obT, [128, H, S], BF16, obTb)
    apos[0] = cmark

    if "D" in phases:
        dmark = apos[0]
        mT1, mT1p = carve("mT1", [KC, 1024], BF16)
        mT1b = subbufs(mT1p, "mT1", [[(c * 2048 + q * 1024, c * 2048 + (q + 1) * 1024) for c in range(KC)] for q in range(2)])
        ymark = apos[0]
        NWD = 2
        wga, wgab = zip(*[carve("wga%d" % i, [KC, 128], BF16) for i in range(NWD)])
        wgb_, wgbb = zip(*[carve("wgb%d" % i, [KC, 128], BF16) for i in range(NWD)])
        wba, wbab = zip(*[carve("wba%d" % i, [8, 128], BF16) for i in range(NWD)])
        wbb, wbbb = zip(*[carve("wbb%d" % i, [8, 128], BF16) for i in range(NWD)])
        sga, sgab = zip(*[carve("sga%d" % i, [512]) for i in range(2)])
        sgb, sgbb = zip(*[carve("sgb%d" % i, [512]) for i in range(2)])
        it = 0
        for hf in range(2):
            for c in range(KC):
                wi = it % NWD
                load_w(wga[wi], w_in, OFF_GATE_A + c * 128, 128, wgab[wi])
                load_w(wgb_[wi], w_in, OFF_GATE_B + c * 128, 128, wgbb[wi])
                load_w(wba[wi], w_a, c * 128, 128, wbab[wi])
                load_w(wbb[wi], w_b, c * 128, 128, wbbb[wi])
                for seg in range(2):
                    t0 = hf * 1024 + seg * 512
                    hq = t0 // 512
                    bs = (it * 2 + seg) % 2 * 4
                    si = seg

                    def mmg(e, w, bank, t0=t0):
                        ins = None
                        for kc in range(KC):
                            ins = e.matmul(ps[bank][:, :], lhsT=w[:, kc, :], rhs=hT[:, kc, t0:t0 + 512], start=(kc == 0),
                                           stop=(kc == KC - 1))
                        return ins

                    def mmb(e, w, src, bank, t0=t0):
                        ins = None
                        for kc in range(8):
                            ins = e.matmul(ps[bank][:, :], lhsT=w[:, kc, :], rhs=src[:, kc, t0:t0 + 512], start=(kc == 0),
                                           stop=(kc == 7))
                        return ins
                    P.add("pe", lambda e, wi=wi, bs=bs, mmg=mmg: mmg(e, wga[wi], bs), reads=[hTb[hq], wgab[wi]],
                          writes=[pbank[bs]])
                    P.add("act", lambda e, bs=bs, si=si: e.activation(out=sga[si], in_=ps[bs][:, :], func=AF.Sigmoid),
                          reads=[pbank[bs]], writes=[sgab[si]])
                    P.add("pe", lambda e, wi=wi, bs=bs, mmg=mmg: mmg(e, wgb_[wi], bs + 1), reads=[hTb[hq], wgbb[wi]],
                          writes=[pbank[bs + 1]])
                    P.add("act", lambda e, bs=bs, si=si: e.activation(out=sgb[si], in_=ps[bs + 1][:, :], func=AF.Sigmoid),
                          reads=[pbank[bs + 1]], writes=[sgbb[si]])
                    P.add("pe", lambda e, wi=wi, bs=bs, mmb=mmb: mmb(e, wba[wi], oaT, bs + 2), reads=[oaTb[hq], wbab[wi]],
                          writes=[pbank[bs + 2]])
                    P.add("dve", lambda e, bs=bs, si=si: e.tensor_tensor(out=sga[si], in0=ps[bs + 2][:, :], in1=sga[si],
                                                                         op=ALU.mult),
                          reads=[pbank[bs + 2], sgab[si]], writes=[sgab[si]])
                    P.add("pe", lambda e, wi=wi, bs=bs, mmb=mmb: mmb(e, wbb[wi], obT, bs + 3), reads=[obTb[hq], wbbb[wi]],
                          writes=[pbank[bs + 3]])
                    P.add("dve", lambda e, bs=bs, si=si: e.tensor_tensor(out=sgb[si], in0=ps[bs + 3][:, :], in1=sgb[si],
                                                                         op=ALU.mult),
                          reads=[pbank[bs + 3], sgbb[si]], writes=[sgbb[si]])
                    if hf == 0:
                        dst = mT1[:, c, seg * 512:(seg + 1) * 512]
                        dbuf = mT1b[seg]
                    else:
                        dst = hT[:, c, seg * 512:(seg + 1) * 512]
                        dbuf = hTb[seg]
                    P.add("dve", lambda e, si=si, dst=dst: e.tensor_tensor(out=dst, in0=sga[si], in1=sgb[si], op=ALU.add),
                          reads=[sgab[si], sgbb[si]], writes=[dbuf])
                it += 1
        if "mT" in dbg:
            tap("mT1", mT1, [128, KC, 1024], BF16, mT1b)
            tap("mT2", hT[:, :, 0:1024], [128, KC, 1024], BF16, hTb[0:2])
        apos[0] = ymark
        outsl = [Buf("outdram_s%d" % i) for i in range(2)]
        postw, postwb = carve("postw", [D])
        xn_junk, xnjb = carve("xnj", [D], BF16)
        xt, xtb = zip(*[carve("xt%d" % i, [D]) for i in range(2)])
        ysb, ysbb = carve("ysb", [D])
        dma("sp", postw, post_w.partition_broadcast(128), postwb)
        for q in range(4):
            dma("pool", oaT[:, :, q * 512:(q + 1) * 512],
                w_out[0:1024, q * 512:(q + 1) * 512].rearrange("(kc p) c -> p kc c", p=128), oaTb[q])
            dma("pool", obT[:, :, q * 512:(q + 1) * 512],
                w_out[1024:2048, q * 512:(q + 1) * 512].rearrange("(kc p) c -> p kc c", p=128), obTb[q])
        for tt in range(NT):
            sl = tt % 2
            dma("sp", xt[sl], x[tt * 128:(tt + 1) * 128, :], xtb[sl])
            for ct in range(4):
                bank = (tt % 2) * 4 + ct

                def mmy(e, tt=tt, ct=ct, bank=bank):
                    ins = None
                    for kc in range(KC):
                        if tt < 8:
                            lt = mT1[:, kc, tt * 128:(tt + 1) * 128]
                        else:
                            lt = hT[:, kc, (tt - 8) * 128:(tt - 7) * 128]
                        wsrc = oaT if kc < 8 else obT
                        ins = e.matmul(ps[bank][:, :], lhsT=lt, rhs=wsrc[:, kc % 8, ct * 512:(ct + 1) * 512],
                                       start=(kc == 0), stop=(kc == KC - 1))
                    return ins
                mb = mT1b[tt // 4] if tt < 8 else hTb[(tt - 8) // 4]
                P.add("pe", mmy, reads=[mb, oaTb[ct], obTb[ct]], writes=[pbank[bank]])
                P.add("act", lambda e, ct=ct, bank=bank: e.copy(out=ysb[:, ct * 512:(ct + 1) * 512], in_=ps[bank][:, :]),
                      reads=[pbank[bank]], writes=[ysbb])
            P.add("act", lambda e, sl=sl: e.activation(out=xn_junk, in_=ysb, func=AF.Square, accum_out=st[:, 0:1]),
                  reads=[ysbb], writes=[xnjb, stb])
            P.add("act", lambda e: e.activation(out=st[:, 1:2], in_=st[:, 0:1], func=AF.Sqrt, bias=epsc[:, 0:1],
                                                scale=1.0 / D), reads=[stb, epsb], writes=[stb])
            P.add("dve", lambda e: e.reciprocal(out=st[:, 2:3], in_=st[:, 1:2]), reads=[stb], writes=[stb])
            P.add("dve", lambda e: e.scalar_tensor_tensor(out=ysb, in0=ysb, scalar=st[:, 2:3], in1=postw, op0=ALU.mult,
                                                          op1=ALU.mult), reads=[ysbb, stb, postwb], writes=[ysbb])
            P.add("dve", lambda e, sl=sl: e.tensor_tensor(out=xt[sl], in0=xt[sl], in1=ysb, op=ALU.add),
                  reads=[xtb[sl], ysbb], writes=[xtb[sl]])
            dma("sp", out[tt * 128:(tt + 1) * 128, :], xt[sl], outsl[sl], reads=[xtb[sl]])
        apos[0] = dmark

    P.add("sp", lambda e: e.nop(), reads=[outb] + (outsl if "D" in phases else []))
    P.finalize()
    return nc, dbg_outs


def _in_maps(x, pre_norm_w, w_in, conv_w, a_log, dt_bias, gdn_norm_w, w_branch_a, w_branch_b, w_out, post_norm_w):
    cf, cm = make_consts()
    f = lambda a: np.ascontiguousarray(np.asarray(a, dtype=np.float32))
    shared = {
        "pre_w": f(pre_norm_w[0][None, :]),
        "post_w": f(post_norm_w[0][None, :]),
        "w_in": f(w_in[0]),
        "conv_wT": f(np.asarray(conv_w[0]).T),
        "a_log16": f(np.tile(np.asarray(a_log[0]), 16)[None, :]),
        "dt_bias16": f(np.tile(np.asarray(dt_bias[0]), 16)[None, :]),
        "gnw": f(np.asarray(gdn_norm_w[0])[:, None]),
        "w_a": f(w_branch_a[0]),
        "w_b": f(w_branch_b[0]),
        "w_out": f(w_out[0]),
        "cf": cf,
        "cm": cm,
    }
    return [dict(shared, x=f(x[b])) for b in range(x.shape[0])]


def kernel(**inputs):
    maps = _in_maps(**inputs)
    import os
    nc, _ = build(phases=os.environ.get("KPHASES", "ABCD"))
    res = run_bass_kernel_spmd(nc, maps, core_ids=list(range(len(maps))))
    return np.stack([np.asarray(r["out"], dtype=np.float32) for r in res.results], axis=0)
```

```python
from contextlib import ExitStack

import numpy as np
import concourse.bass as bass
import concourse.mybir as mybir
from concourse.bass_utils import run_bass_kernel_spmd

F32 = mybir.dt.float32
BF16 = mybir.dt.bfloat16
AF = mybir.ActivationFunctionType
ALU = mybir.AluOpType
AX = mybir.AxisListType

S = 2048
D = 2048
NT = S // 128
KC = D // 128
H = 8
IN_W = 12304
EPS = 1e-6
NEG = -30000.0
STAGE_LIMIT = 1000
PRE_TAPS = ("u", "wTn", "iT", "Ln", "kdec", "N0")

OFF_GQ, OFF_GK, OFF_GV = 0, 1024, 2048
OFF_GZ = 3072
OFF_GB = 4096
OFF_GA = 4104
OFF_MQ, OFF_MK, OFF_MV = 4112, 4112 + 1024, 4112 + 2048
OFF_MZ = 4112 + 3072
OFF_GATE_A = 8208
OFF_GATE_B = 8208 + 2048

C_ID, C_TRI, C_UG, C_ONES, C_NBS, C_NBI, C_MU0 = 0, 1, 2, 3, 4, 5, 6
NCB = 13


def make_consts():
    r = np.arange(128)[:, None]
    c = np.arange(128)[None, :]
    blocks = [None] * NCB
    blocks[C_ID] = (r == c)
    blocks[C_TRI] = (r <= c)
    blocks[C_UG] = (r > c)
    blocks[C_ONES] = np.ones((128, 128), bool)
    nbs = np.where(r > c, 0.0, NEG)
    nbi = np.where(r <= c, 0.0, NEG)
    out = []
    for k in range(NCB):
        if k == C_NBS:
            out.append(nbs.astype(np.float32))
        elif k == C_NBI:
            out.append(nbi.astype(np.float32))
        elif k >= C_MU0:
            l = k - C_MU0
            b = 1 << l
            m = ((r // (2 * b)) == (c // (2 * b))) & ((r % (2 * b)) < b) & ((c % (2 * b)) >= b)
            out.append(m.astype(np.float32))
        else:
            out.append(blocks[k].astype(np.float32))
    cf = np.concatenate(out, axis=1)
    cb = np.zeros((128, 256), np.float32)
    cb[:, :128] = nbi
    es = np.zeros((128, 8 * 128), np.float32)
    for n in range(8):
        es[n, n * 128:(n + 1) * 128] = 1.0
    return np.ascontiguousarray(cf), np.ascontiguousarray(np.concatenate([cb, es], axis=1))


class Buf:
    __slots__ = ("name", "last_w", "readers", "sem", "dcnt", "iv", "ov", "excl")
    REG = []

    def __init__(self, name, iv=None, excl=False):
        self.excl = excl
        self.name = name
        self.last_w = None
        self.readers = []
        self.sem = None
        self.dcnt = 0
        if iv is not None and not isinstance(iv, list):
            iv = [iv]
        self.iv = iv
        self.ov = [self]
        if iv is not None:
            for o in Buf.REG:
                if any(a[0] == b[0] and a[1] < b[2] and b[1] < a[2] for a in iv for b in o.iv):
                    o.ov.append(self)
                    self.ov.append(o)
            Buf.REG.append(self)


class Op:
    __slots__ = ("eng", "fn", "deps", "ndma", "sig", "sigval", "wbuf")


class Prog:
    ENGS = ("pe", "act", "dve", "pool", "sp")

    def __init__(self, nc, stack):
        self.nc = nc
        self.stack = stack
        self.ops = []
        self.dma_bufs = []

    def add(self, eng, fn, reads=(), writes=(), ndma=0, waw=True):
        idx = len(self.ops)
        deps = set()
        for b0 in reads:
            for b in b0.ov:
                if b.last_w is not None:
                    deps.add(b.last_w)
                if b.excl:
                    for r in b.readers:
                        if self.ops[r].eng != eng:
                            deps.add(r)
        for b0 in writes:
            for b in b0.ov:
                if b.last_w is not None and (waw or b is not b0):
                    deps.add(b.last_w)
                deps.update(b.readers)
        deps.discard(idx)
        for b in reads:
            b.readers.append(idx)
        for b in writes:
            b.last_w = idx
            b.readers = []
        op = Op()
        op.eng = eng
        op.fn = fn
        op.deps = deps
        op.ndma = ndma
        op.sig = False
        op.sigval = 0
        op.wbuf = None
        if ndma:
            assert len(writes) == 1
            op.wbuf = writes[0]
            if op.wbuf.sem is None:
                op.wbuf.sem = True
                self.dma_bufs.append(op.wbuf)
        self.ops.append(op)
        return idx

    def finalize(self):
        nc = self.nc
        ops = self.ops
        for op in ops:
            for d in op.deps:
                p = ops[d]
                if p.eng == "pe" and op.eng == "pe" and not p.ndma:
                    continue
                p.sig = True
        esem = {e: self.stack.enter_context(nc.semaphore("sem_" + e)) for e in self.ENGS}
        for b in self.dma_bufs:
            b.sem = self.stack.enter_context(nc.semaphore("dsem_" + b.name))
        cnt = {e: 0 for e in self.ENGS}
        for op in ops:
            if op.ndma:
                op.wbuf.dcnt += 16 * op.ndma
                op.sigval = op.wbuf.dcnt
            elif op.sig:
                cnt[op.eng] += 1
                op.sigval = cnt[op.eng]

        def emit(ename, e):
            waited = {}
            for op in ops:
                if op.eng != ename:
                    continue
                need = {}
                for d in op.deps:
                    p = ops[d]
                    if p.ndma:
                        sem = p.wbuf.sem
                    else:
                        if p.eng == "pe" and ename == "pe":
                            continue
                        sem = esem[p.eng]
                    k = id(sem)
                    if k not in need or need[k][1] < p.sigval:
                        need[k] = (sem, p.sigval)
                for k, (sem, v) in need.items():
                    if waited.get(k, 0) < v:
                        e.wait_ge(sem, v)
                        waited[k] = v
                ins = op.fn(e)
                if op.ndma:
                    pass
                elif op.sig:
                    ins.then_inc(esem[ename], 1)

        with nc.Block() as block:
            @block.tensor
            def _(e):
                emit("pe", e)

            @block.scalar
            def _(e):
                emit("act", e)

            @block.vector
            def _(e):
                emit("dve", e)

            @block.gpsimd
            def _(e):
                emit("pool", e)

            @block.sync
            def _(e):
                emit("sp", e)


def build(dbg=(), nheads_g=H, nheads_m=H, phases="ABCD"):
    Buf.REG = []
    nc = bass.Bass("TRN2", target_bir_lowering=False)
    stack = ExitStack()
    P = Prog(nc, stack)

    def dram(name, shape, dt=F32, kind="ExternalInput"):
        return nc.dram_tensor(name, list(shape), dt, kind=kind).ap()

    x = dram("x", [S, D])
    pre_w = dram("pre_w", [1, D])
    post_w = dram("post_w", [1, D])
    w_in = dram("w_in", [D, IN_W])
    conv_wT = dram("conv_wT", [3072, 4])
    a_log = dram("a_log16", [1, 128])
    dt_bias = dram("dt_bias16", [1, 128])
    gnw_d = dram("gnw", [128, 1])
    w_a = dram("w_a", [1024, D])
    w_b = dram("w_b", [1024, D])
    w_out = dram("w_out", [D, D])
    cf_d = dram("cf", [128, NCB * 128])
    cm_d = dram("cm", [128, 256 + 1024])
    out = dram("out", [S, D], kind="ExternalOutput")

    def sb(name, shape, dt=F32):
        return stack.enter_context(nc.sbuf_tensor(name, list(shape), dt))

    hT = sb("hT", [128, KC, S], BF16)
    oaT = sb("oaT", [128, H, S], BF16)
    cf = sb("cf_sb", [128, NCB * 128], F32)
    ARENA_BYTES = 100 * 1024
    arena = sb("arena", [128, ARENA_BYTES // 4], F32)
    ps = [stack.enter_context(nc.psum_tensor("ps%d" % i, [128, 512], F32)) for i in range(8)]
    hTb = [Buf("hT%d" % i) for i in range(4)]
    oaTb = [Buf("oaT%d" % i) for i in range(4)]
    cfb = Buf("cf")

    pbank = [Buf("psb%d" % i, excl=True) for i in range(8)]

    def psbuf(bank, c0=0, c1=512):
        return pbank[bank]

    apos = [0]

    def carve(name, free_shape, dt=F32):
        esz = 4 if dt == F32 else 2
        n = int(np.prod(free_shape))
        nb = (n * esz + 31) // 32 * 32
        off = apos[0]
        apos[0] += nb
        assert apos[0] <= ARENA_BYTES, (name, apos[0])
        ap = arena[:, off // 4:(off + nb) // 4]
        if dt != F32:
            ap = ap.bitcast(dt)
        ap = ap[:, 0:n]
        if len(free_shape) == 2:
            ap = ap.rearrange("p (a b) -> p a b", a=free_shape[0])
        elif len(free_shape) == 3:
            ap = ap.rearrange("p (a b c) -> p a b c", a=free_shape[0], b=free_shape[1])
        return ap, Buf(name, iv=("arena", off, off + nb))

    def subbufs(parent, name, pieces):
        base = parent.iv[0][1]
        return [Buf("%s_%d" % (name, i), iv=[("arena", base + lo, base + hi) for lo, hi in pc])
                for i, pc in enumerate(pieces)]

    def cblk(k):
        return cf[:, k * 128:(k + 1) * 128]

    def dma(eng, out_ap, in_ap, wbuf, reads=(), waw=True):
        def fn(e):
            return e.dma_start(out=out_ap, in_=in_ap).then_inc(wbuf.sem, 16)
        P.add(eng, fn, reads=reads, writes=[wbuf], ndma=1, waw=waw)

    def load_w(dst_ap, wdram, c0, ncols, wbuf):
        src = wdram[:, c0:c0 + ncols].rearrange("(kc p) c -> p kc c", p=128)
        dma("pool", dst_ap, src, wbuf)

    outb = Buf("outdram")
    outsl = []
    dbg_outs = {}

    def tap(name, ap, shape, dt, rbuf):
        d = dram("dbg_" + name, shape, dt, kind="ExternalOutput")
        dbg_outs[name] = d
        dma("sp", d, ap, outb, reads=rbuf, waw=False)

    dma("sp", cf[:], cf_d, cfb)
    small = sb("small", [128, 1280], F32)
    epsc = small[:, 0:2]
    epsb = Buf("epsc")
    P.add("dve", lambda e: e.memset(small[:, 0:1], EPS), writes=[epsb])
    P.add("dve", lambda e: e.memset(small[:, 1:2], 1.0), writes=[epsb])
    st = small[:, 8:16]
    stb = Buf("st")
    I_ = cblk(C_ID)

    amark = apos[0]
    apos[0] = ARENA_BYTES - 4 * D * 4
    xs, xsb = zip(*[carve("xs%d" % i, [D]) for i in range(2)])
    xn, xnb = carve("xn", [D])
    prew, prewb = carve("prew", [D])
    dma("sp", prew, pre_w.partition_broadcast(128), prewb)
    for tt in range(NT):
        sl = tt % 2
        dma("sp", xs[sl], x[tt * 128:(tt + 1) * 128, :], xsb[sl])
        P.add("act", lambda e, sl=sl: e.activation(out=xn, in_=xs[sl], func=AF.Square, accum_out=st[:, 0:1]),
              reads=[xsb[sl]], writes=[xnb, stb])
        P.add("act", lambda e: e.activation(out=st[:, 1:2], in_=st[:, 0:1], func=AF.Sqrt, bias=epsc[:, 0:1],
                                            scale=1.0 / D), reads=[stb, epsb], writes=[stb])
        P.add("dve", lambda e: e.reciprocal(out=st[:, 2:3], in_=st[:, 1:2]), reads=[stb], writes=[stb])
        P.add("dve", lambda e, sl=sl: e.scalar_tensor_tensor(out=xn, in0=xs[sl], scalar=st[:, 2:3], in1=prew,
                                                             op0=ALU.mult, op1=ALU.mult),
              reads=[xsb[sl], stb, prewb], writes=[xnb])
        for g in range(4):
            bank = (tt % 2) * 4 + g
            pb = psbuf(bank)

            def tr(e, g=g, bank=bank):
                ins = None
                for j in range(4):
                    kc = g * 4 + j
                    ins = e.transpose(out=ps[bank][:, j * 128:(j + 1) * 128], in_=xn[:, kc * 128:(kc + 1) * 128],
                                      identity=I_)
                return ins
            P.add("pe", tr, reads=[xnb, cfb], writes=[pb])
            dst = hT[:, g * 4:(g + 1) * 4, tt * 128:(tt + 1) * 128]
            src = ps[bank][:].rearrange("p (a b) -> p a b", a=4)
            if g % 2 == 0:
                P.add("act", lambda e, dst=dst, src=src: e.copy(out=dst, in_=src), reads=[pb], writes=[hTb[tt // 4]])
            else:
                P.add("dve", lambda e, dst=dst, src=src: e.tensor_copy(out=dst, in_=src), reads=[pb],
                      writes=[hTb[tt // 4]])
    if "hT" in dbg:
        tap("hT", hT[:], [128, KC, S], BF16, hTb)
    apos[0] = amark

    if "B" in phases:
        bmark = apos[0]
        betas = small[:, 16:144].rearrange("p (t h) -> p t h", t=NT)
        nbetas = small[:, 144:272].rearrange("p (t h) -> p t h", t=NT)
        gs = small[:, 272:400].rearrange("p (t h) -> p t h", t=NT)
        bgs = small[:, 400:528].rearrange("p (t h) -> p t h", t=NT)
        egs = small[:, 528:912].rearrange("p (t c) -> p t c", t=NT)
        negA = small[:, 912:1040]
        dtb = small[:, 1040:1168]
        gnw = small[:, 1168:1169]
        cw = small[:, 1172:1268].rearrange("p (c j) -> p c j", c=24)
        gqb = Buf("gatesq")
        cwb = Buf("cw")
        xa, xab = carve("xa", [128])
        wg, wgb = carve("wg", [KC, 16], BF16)
        load_w(wg, w_in, OFF_GB, 16, wgb)
        dma("sp", negA, a_log.partition_broadcast(128), gqb)
        dma("sp", dtb, dt_bias.partition_broadcast(128), gqb)
        dma("sp", gnw, gnw_d, gqb)
        dma("sp", cw, conv_wT.rearrange("(c p) j -> p c j", p=128), cwb)
        P.add("act", lambda e: e.activation(out=negA, in_=negA, func=AF.Exp), reads=[gqb], writes=[gqb])
        P.add("dve", lambda e: e.tensor_scalar(out=negA, in0=negA, scalar1=-1.0, scalar2=None, op0=ALU.mult),
              reads=[gqb], writes=[gqb])
        def emit_gates():
            pg = psbuf(0)
            for tt in range(NT):
                def mm(e, tt=tt):
                    ins = None
                    for kc in range(KC):
                        ins = e.matmul(ps[0][:, tt * 16:(tt + 1) * 16], lhsT=hT[:, kc, tt * 128:(tt + 1) * 128],
                                       rhs=wg[:, kc, :], start=(kc == 0), stop=(kc == KC - 1))
                    return ins
                P.add("pe", mm, reads=[hTb[tt // 4], wgb], writes=[pg])
            pgv = ps[0][:, 0:256].rearrange("p (t c) -> p t c", t=NT)
            P.add("act", lambda e: e.activation(out=betas, in_=pgv[:, :, 0:8], func=AF.Sigmoid), reads=[pg], writes=[gqb])
            P.add("dve", lambda e: e.tensor_tensor(out=xa.rearrange("p (t h) -> p t h", t=NT), in0=pgv[:, :, 8:16],
                                                   in1=dtb.rearrange("p (t h) -> p t h", t=NT), op=ALU.add),
                  reads=[pg, gqb], writes=[xab])
            P.add("act", lambda e: e.activation(out=xa, in_=xa, func=AF.Exp), reads=[xab], writes=[xab])
            P.add("act", lambda e: e.activation(out=xa, in_=xa, func=AF.Ln, bias=epsc[:, 1:2]), reads=[xab, epsb],
                  writes=[xab])
            P.add("dve", lambda e: e.tensor_tensor(out=small[:, 272:400], in0=xa, in1=negA, op=ALU.mult),
                  reads=[xab, gqb], writes=[gqb])
            P.add("dve", lambda e: e.tensor_scalar(out=small[:, 144:272], in0=small[:, 16:144], scalar1=-1.0, scalar2=None,
                                                   op0=ALU.mult), reads=[gqb], writes=[gqb])
            pg2 = psbuf(1)
            for tt in range(NT):
                def mm2(e, tt=tt):
                    e.matmul(ps[1][:, tt * 24:tt * 24 + 8], lhsT=cblk(C_TRI), rhs=gs[:, tt, :], start=True, stop=True)
                    e.matmul(ps[1][:, tt * 24 + 8:tt * 24 + 16], lhsT=cblk(C_UG), rhs=gs[:, tt, :], start=True, stop=True)
                    return e.matmul(ps[1][:, tt * 24 + 16:tt * 24 + 24], lhsT=cblk(C_ONES), rhs=gs[:, tt, :], start=True,
                                    stop=True)
                P.add("pe", mm2, reads=[gqb, cfb], writes=[pg2])
            P.add("act", lambda e: e.activation(out=small[:, 528:912], in_=ps[1][:, 0:384], func=AF.Exp), reads=[pg2],
                  writes=[gqb])
            P.add("dve", lambda e: e.tensor_tensor(out=bgs, in0=betas, in1=egs[:, :, 0:8], op=ALU.mult), reads=[gqb],
                  writes=[gqb])
            if "gates" in dbg:
                tap("gates", small[:, 16:912], [128, 896], F32, [gqb])

        if "0" in phases:
            nheads_g = 0
        NW = 4
        wr, wrb = zip(*[carve("wr%d" % i, [KC, 128], BF16) for i in range(NW)])
        xraw, xrp = carve("xraw", [3 + S])
        xrb = subbufs(xrp, "xraw", [[(0, 12)]] + [[(12 + i * 2048, 12 + (i + 1) * 2048)] for i in range(4)])
        qT, qTp = carve("qT", [S])
        kT, kTp = carve("kT", [S])
        vT, vTp = carve("vT", [S])
        seg4 = [[(i * 2048, (i + 1) * 2048)] for i in range(4)]
        tTb = {"q": subbufs(qTp, "qT", seg4), "k": subbufs(kTp, "kT", seg4), "v": subbufs(vTp, "vT", seg4)}
        szT, szp = carve("szT", [S], BF16)
        szb = subbufs(szp, "szT", [[(i * 1024, (i + 1) * 1024)] for i in range(4)])
        sqt, sqtb = zip(*[carve("sqt%d" % i, [512], BF16) for i in range(2)])
        c16, c16b = carve("c16", [256], BF16)
        Ib16 = c16[:, 0:128]
        ones16 = c16[:, 128:256]
        P.add("dve", lambda e: e.tensor_copy(out=Ib16, in_=cblk(C_ID)), reads=[cfb], writes=[c16b])
        P.add("dve", lambda e: e.tensor_copy(out=ones16, in_=cblk(C_ONES)), reads=[cfb], writes=[c16b])
        Sb16, Sb16b = carve("S16", [128], BF16)
        rs, rsb = carve("rs", [512])
        Ssb, Sb = carve("S", [128])
        NSL = 4
        slot = []
        for i in range(NSL):
            d_ = {}
            for nm in ("Gt", "Ln", "N0", "N1", "M0", "M1", "Xs", "kbg", "vb"):
                d_[nm] = carve("%s_%d" % (nm, i), [128])
            for nm in ("iT", "kdec", "u", "wTn", "qb"):
                d_[nm] = [carve("%s_%d_%d" % (nm, i, p_), [128], F32 if nm == "u" else BF16) for p_ in range(2)]
            d_["EE"] = carve("EE_%d" % i, [256])
            slot.append(d_)
        vnew, vnewb = carve("vnew", [128], BF16)
        t1, t1b = carve("t1", [128])
        osb, osbb = carve("osb", [128])
        onb_, onbb = carve("on", [128], BF16)
        P.add("dve", lambda e: e.memset(xraw[:, 0:3], 0.0), writes=[xrb[0]])

        PBANKS = [0, 1, 3, 4, 5, 6]
        pl2 = psbuf(7)
        pscan = psbuf(2)
        pT = psbuf(7)

        def q4(bank, q):
            return ps[bank][:, q * 128:(q + 1) * 128]

        wcount = [0]

        def proj_fm(col0, dst_fn, pidx):
            wi = wcount[0] % NW
            wcount[0] += 1
            load_w(wr[wi], w_in, col0, 128, wrb[wi])
            for seg in range(4):
                bank = PBANKS[pidx[0] % len(PBANKS)]
                pidx[0] += 1

                def mm(e, seg=seg, bank=bank, wi=wi):
                    ins = None
                    for kc in range(KC):
                        ins = e.matmul(ps[bank][:, :], lhsT=wr[wi][:, kc, :], rhs=hT[:, kc, seg * 512:(seg + 1) * 512],
                                       start=(kc == 0), stop=(kc == KC - 1))
                    return ins
                P.add("pe", mm, reads=[hTb[seg], wrb[wi]], writes=[pbank[bank]])
                nxt = dst_fn(seg, bank)
                while l2pend:
                    l2pend.pop(0)()
                if nxt is not None:
                    l2pend.append(nxt)

        pidx = [0]
        l2pend = []
        SCQ = 128.0 ** -0.5
        for h in range(nheads_g):
            for ti, nm in enumerate("qkv"):
                ci = ti * 8 + h
                tT = {"q": qT, "k": kT, "v": vT}[nm]

                def consume(seg, bank, nm=nm, ci=ci, tT=tT):
                    s0 = seg * 512
                    P.add("act", lambda e: e.copy(out=xraw[:, 3 + s0:3 + s0 + 512], in_=ps[bank][:, :]),
                          reads=[pbank[bank]], writes=[xrb[seg + 1]])
                    dst = tT[:, s0:s0 + 512]
                    db = tTb[nm][seg]
                    P.add("dve", lambda e: e.tensor_scalar(out=dst, in0=xraw[:, 3 + s0:3 + s0 + 512],
                                                           scalar1=cw[:, ci, 3:4], scalar2=None, op0=ALU.mult),
                          reads=[xrb[seg + 1], cwb], writes=[db])
                    for j in range(3):
                        P.add("dve", lambda e, j=j: e.scalar_tensor_tensor(out=dst, in0=xraw[:, j + s0:j + s0 + 512],
                                                                           scalar=cw[:, ci, j:j + 1], in1=dst,
                                                                           op0=ALU.mult, op1=ALU.add),
                              reads=[xrb[seg + 1], xrb[seg], cwb, db], writes=[db])
                    P.add("act", lambda e: e.activation(out=dst, in_=dst, func=AF.Silu), reads=[db], writes=[db])
                    if nm in "qk":
                        si = seg % 2
                        P.add("act", lambda e: e.activation(out=sqt[si], in_=dst, func=AF.Square), reads=[db],
                              writes=[sqtb[si]])

                        def tail():
                            P.add("pe", lambda e: e.matmul(ps[7][:, :], lhsT=ones16, rhs=sqt[si], start=True,
                                                           stop=True), reads=[sqtb[si], c16b], writes=[pl2])
                            P.add("act", lambda e: e.activation(out=rs, in_=ps[7][:, :], func=AF.Sqrt,
                                                                bias=epsc[:, 0:1]), reads=[pl2, epsb], writes=[rsb])
                            P.add("dve", lambda e: e.reciprocal(out=rs, in_=rs), reads=[rsb], writes=[rsb])
                            if nm == "q":
                                P.add("dve", lambda e: e.scalar_tensor_tensor(out=dst, in0=dst, scalar=SCQ, in1=rs,
                                                                              op0=ALU.mult, op1=ALU.mult),
                                      reads=[db, rsb], writes=[db])
                            else:
                                P.add("dve", lambda e: e.tensor_tensor(out=dst, in0=dst, in1=rs, op=ALU.mult),
                                      reads=[db, rsb], writes=[db])
                        return tail
                    return None
                proj_fm([OFF_GQ, OFF_GK, OFF_GV][ti] + h * 128, consume, pidx)

            def consume_z(seg, bank):
                s0 = seg * 512
                P.add("act", lambda e: e.activation(out=szT[:, s0:s0 + 512], in_=ps[bank][:, :], func=AF.Silu),
                      reads=[pbank[bank]], writes=[szb[seg]])
            proj_fm(OFF_GZ + h * 128, consume_z, pidx)
            while l2pend:
                l2pend.pop(0)()
            if h == 0:
                emit_gates()
            if "qkv" in dbg and h == 0:
                tap("qT", qT, [128, S], F32, tTb["q"])
                tap("kT", kT, [128, S], F32, tTb["k"])
                tap("vT", vT, [128, S], F32, tTb["v"])

            P.add("dve", lambda e: e.memset(Ssb, 0.0), writes=[Sb])
            P.add("dve", lambda e: e.memset(Sb16, 0.0), writes=[Sb16b])

            def pre_stages(tt, sl, par, h=h):
                sd = dict(slot[sl])
                for nm_ in ("iT", "kdec", "u", "wTn", "qb"):
                    sd[nm_] = slot[sl][nm_][par]
                ts = slice(tt * 128, (tt + 1) * 128)
                sg = tt // 4
                bk = 3 + sl
                pb = pbank[bk]
                stages = []

                def s0():
                    def trKV(e):
                        e.matmul(q4(bk, 0), lhsT=kT[:, ts], rhs=I_, start=True, stop=True)
                        return e.matmul(q4(bk, 1), lhsT=vT[:, ts], rhs=I_, start=True, stop=True)
                    P.add("pe", trKV, reads=[tTb["k"][sg], tTb["v"][sg], cfb], writes=[pb])
                    P.add("dve", lambda e: e.tensor_scalar(out=sd["kbg"][0], in0=q4(bk, 0), scalar1=bgs[:, tt, h:h + 1],
                                                           scalar2=None, op0=ALU.mult),
                          reads=[pb, gqb], writes=[sd["kbg"][1]])
                    P.add("dve", lambda e: e.tensor_scalar(out=sd["kdec"][0], in0=q4(bk, 0),
                                                           scalar1=egs[:, tt, 8 + h:9 + h], scalar2=None, op0=ALU.mult),
                          reads=[pb, gqb], writes=[sd["kdec"][1]])
                    P.add("dve", lambda e: e.tensor_scalar(out=sd["vb"][0], in0=q4(bk, 1), scalar1=betas[:, tt, h:h + 1],
                                                           scalar2=None, op0=ALU.mult),
                          reads=[pb, gqb], writes=[sd["vb"][1]])
                    P.add("dve", lambda e: e.tensor_scalar(out=sd["Gt"][0], in0=cblk(C_TRI), scalar1=gs[:, tt, h:h + 1],
                                                           scalar2=None, op0=ALU.mult),
                          reads=[cfb, gqb], writes=[sd["Gt"][1]])
                stages.append(s0)

                def s1():
                    def mmD(e):
                        e.matmul(q4(bk, 2), lhsT=sd["Gt"][0], rhs=cblk(C_UG), start=True, stop=True)
                        e.matmul(q4(bk, 3), lhsT=cblk(C_UG), rhs=sd["Gt"][0], start=True, stop=False)
                        e.matmul(q4(bk, 3), lhsT=I_, rhs=cblk(C_NBI), start=False, stop=True)
                        e.matmul(q4(bk, 0), lhsT=kT[:, ts], rhs=kT[:, ts], start=True, stop=True)
                        return e.matmul(q4(bk, 1), lhsT=kT[:, ts], rhs=qT[:, ts], start=True, stop=True)
                    P.add("pe", mmD, reads=[sd["Gt"][1], cfb, tTb["k"][sg], tTb["q"][sg]], writes=[pb])
                    P.add("act", lambda e: e.activation(out=sd["EE"][0], in_=ps[bk][:, 256:512], func=AF.Exp),
                          reads=[pb], writes=[sd["EE"][1]])
                    P.add("dve", lambda e: e.scalar_tensor_tensor(out=sd["Ln"][0], in0=q4(bk, 0),
                                                                  scalar=nbetas[:, tt, h:h + 1], in1=sd["EE"][0][:, 0:128],
                                                                  op0=ALU.mult, op1=ALU.mult),
                          reads=[pb, gqb, sd["EE"][1]], writes=[sd["Ln"][1]])
                    P.add("dve", lambda e: e.tensor_tensor(out=sd["iT"][0], in0=q4(bk, 1), in1=sd["EE"][0][:, 128:256],
                                                           op=ALU.mult),
                          reads=[pb, sd["EE"][1]], writes=[sd["iT"][1]])
                    P.add("act", lambda e: e.copy(out=sd["qb"][0], in_=qT[:, ts]), reads=[tTb["q"][sg]],
                          writes=[sd["qb"][1]])
                stages.append(s1)

                cur = {"N": (I_, cfb), "M": (I_, cfb)}
                for l in range(7):
                    def sA(l=l):
                        Ncur, Nb = cur["N"]
                        P.add("pe", lambda e: e.matmul(q4(bk, 0), lhsT=sd["Ln"][0], rhs=Ncur, start=True, stop=True),
                              reads=[sd["Ln"][1], Nb], writes=[pb])
                        P.add("dve", lambda e: e.tensor_tensor(out=sd["Xs"][0], in0=q4(bk, 0), in1=cblk(C_MU0 + l),
                                                               op=ALU.mult),
                              reads=[pb, cfb], writes=[sd["Xs"][1]])
                    stages.append(sA)

                    def sB(l=l):
                        Ncur, Nb = cur["N"]
                        Mcur, Mb = cur["M"]
                        Nn = sd["N%d" % (l % 2)]
                        Mn = sd["M%d" % (l % 2)]

                        def mmNM(e):
                            ins = None
                            if l > 0:
                                ins = e.matmul(q4(bk, 1), lhsT=Mcur, rhs=sd["Xs"][0], start=True, stop=True)
                            if l < 6:
                                ins = e.matmul(q4(bk, 2), lhsT=sd["Xs"][0], rhs=Mcur, start=True, stop=True)
                            return ins
                        P.add("pe", mmNM, reads=[Nb, Mb, sd["Xs"][1], cfb], writes=[pb])
                        if l > 0:
                            P.add("dve", lambda e: e.tensor_tensor(out=Nn[0], in0=q4(bk, 1), in1=Ncur, op=ALU.add),
                                  reads=[pb, Nb], writes=[Nn[1]])
                        else:
                            P.add("dve", lambda e: e.tensor_tensor(out=Nn[0], in0=sd["Xs"][0], in1=Ncur, op=ALU.add),
                                  reads=[sd["Xs"][1], Nb], writes=[Nn[1]])
                        if l < 6:
                            P.add("dve", lambda e: e.tensor_tensor(out=Mn[0], in0=q4(bk, 2), in1=Mcur, op=ALU.add),
                                  reads=[pb, Mb], writes=[Mn[1]])
                        cur["N"] = Nn
                        cur["M"] = Mn
                    stages.append(sB)

                def sF():
                    Nf, Nfb = cur["N"]

                    def mmUW(e):
                        e.matmul(q4(bk, 0), lhsT=Nf, rhs=sd["vb"][0], start=True, stop=True)
                        return e.matmul(q4(bk, 1), lhsT=sd["kbg"][0], rhs=Nf, start=True, stop=True)
                    P.add("pe", mmUW, reads=[Nfb, sd["vb"][1], sd["kbg"][1]], writes=[pb])
                    P.add("act", lambda e: e.copy(out=sd["u"][0], in_=q4(bk, 0)), reads=[pb], writes=[sd["u"][1]])
                    P.add("act", lambda e: e.mul(out=sd["wTn"][0], in_=q4(bk, 1), mul=-1.0), reads=[pb],
                          writes=[sd["wTn"][1]])
                stages.append(sF)
                return stages

            def scan(tt, sl, par, h=h):
                sd = dict(slot[sl])
                for nm_ in ("iT", "kdec", "u", "wTn", "qb"):
                    sd[nm_] = slot[sl][nm_][par]
                ts = slice(tt * 128, (tt + 1) * 128)
                sg = tt // 4
                subs = []

                def subA():
                    P.add("pe", mmV, reads=[sd["wTn"][1], Sb16b], writes=[pscan])
                    P.add("dve", lambda e: e.tensor_tensor(out=vnew, in0=q4(2, 0), in1=sd["u"][0], op=ALU.add),
                          reads=[pscan, sd["u"][1]], writes=[vnewb])

                def subB():
                    P.add("pe", mmO, reads=[sd["qb"][1], Sb16b, sd["iT"][1], sd["kdec"][1], vnewb], writes=[pscan])
                    P.add("act", lambda e: e.mul(out=t1, in_=q4(2, 1), mul=egs[:, tt, h:h + 1]), reads=[pscan, gqb],
                          writes=[t1b])
                    P.add("dve", lambda e: e.tensor_tensor(out=osb, in0=t1, in1=q4(2, 2), op=ALU.add),
                          reads=[t1b, pscan], writes=[osbb])
                    P.add("dve", lambda e: e.scalar_tensor_tensor(out=Ssb, in0=Ssb, scalar=egs[:, tt, 16 + h:17 + h],
                                                                  in1=q4(2, 3), op0=ALU.mult, op1=ALU.add),
                          reads=[Sb, gqb, pscan], writes=[Sb])
                    P.add("act", lambda e: e.copy(out=Sb16, in_=Ssb), reads=[Sb], writes=[Sb16b])

                def subC():
                    P.add("act", lambda e: e.activation(out=t1, in_=osb, func=AF.Square, accum_out=st[:, 4:5]),
                          reads=[osbb], writes=[t1b, stb])
                    P.add("act", lambda e: e.activation(out=st[:, 5:6], in_=st[:, 4:5], func=AF.Sqrt, bias=epsc[:, 0:1],
                                                        scale=1.0 / 128), reads=[stb, epsb], writes=[stb])
                    P.add("dve", lambda e: e.reciprocal(out=st[:, 6:7], in_=st[:, 5:6]), reads=[stb], writes=[stb])
                    P.add("act", lambda e: e.mul(out=onb_, in_=osb, mul=st[:, 6:7]), reads=[osbb, stb], writes=[onbb])

                def subD():
                    P.add("pe", lambda e: e.matmul(q4(7, 0), lhsT=onb_, rhs=Ib16, start=True, stop=True),
                          reads=[onbb, c16b], writes=[pT])
                    P.add("dve", lambda e: e.scalar_tensor_tensor(out=oaT[:, h, ts], in0=q4(7, 0), scalar=gnw,
                                                                  in1=szT[:, ts], op0=ALU.mult, op1=ALU.mult),
                          reads=[pT, gqb, szb[sg]], writes=[oaTb[sg]])

                def mmV(e):
                    return e.matmul(q4(2, 0), lhsT=sd["wTn"][0], rhs=Sb16, start=True, stop=True)

                def mmO(e):
                    e.matmul(q4(2, 1), lhsT=sd["qb"][0], rhs=Sb16, start=True, stop=True)
                    e.matmul(q4(2, 2), lhsT=sd["iT"][0], rhs=vnew, start=True, stop=True)
                    return e.matmul(q4(2, 3), lhsT=sd["kdec"][0], rhs=vnew, start=True, stop=True)
                return [subA, subB, subC, subD]

            pend = []
            for gi, g0 in enumerate(range(0, NT, NSL)):
                sts = [pre_stages(g0 + i, i, gi % 2) for i in range(NSL)]
                for k in range(len(sts[0])):
                    for i in range(NSL):
                        sts[i][k]()
                    if pend:
                        pend.pop(0)()
                while pend:
                    pend.pop(0)()
                for i in range(NSL):
                    pend.extend(scan(g0 + i, i, gi % 2))
            while pend:
                pend.pop(0)()
        print("phase B arena bytes", apos[0])
        if "oaT" in dbg:
            tap("oaT", oaT[:], [128, H, S], BF16, oaTb)
        if "scan" in dbg:
            tap("osb", osb, [128, 128], F32, [osbb])
            tap("S", Ssb, [128, 128], F32, [Sb])
            tap("vnew", vnew, [128, 128], F32, [vnewb])
        apos[0] = bmark

    obT, obTp = carve("obT", [H, S], BF16)
    obTb = subbufs(obTp, "obT", [[(hh * 4096 + q * 1024, hh * 4096 + (q + 1) * 1024) for hh in range(H)] for q in range(4)])
    cmark = apos[0]
    if "C" in phases:
        NWC = 4
        cwr, cwrb = zip(*[carve("cwr%d" % i, [KC, 128], BF16) for i in range(NWC)])
        cmf, cmfb = carve("cmf", [1280])
        cmb, cmbb = carve("cmb", [1536], BF16)
        dma("sp", cmf, cm_d, cmfb)
        if "7" not in phases:
            P.add("dve", lambda e: e.tensor_copy(out=cmb[:, 0:1280], in_=cmf), reads=[cmfb], writes=[cmbb])
        P.add("dve", lambda e: e.tensor_copy(out=cmb[:, 1280:1408], in_=cblk(C_ID)), reads=[cfb], writes=[cmbb])
        P.add("dve", lambda e: e.tensor_copy(out=cmb[:, 1408:1536], in_=cblk(C_ONES)), reads=[cfb], writes=[cmbb])
        cbias = cmb[:, 0:256]
        Ib = cmb[:, 1280:1408]
        onesb = cmb[:, 1408:1536]
        qTb_, qTbb = carve("mqTb", [S], BF16)
        kTb_, kTbb = carve("mkTb", [S], BF16)
        qTf, qTfb = carve("mqTf", [S])
        Vt, Vtb = carve("mV", [NT, 128], BF16)
        mszT, mszb = carve("mszT", [S], BF16)
        nbT, nbTb = carve("nbT", [S], BF16)
        kmT, kmTb = carve("kmT", [8])
        gm, gmb = carve("gm", [NT, 8])
        nb, nbb = carve("nb", [NT, 8])
        top8, top8b = carve("top8", [8])
        pTr, pTrb = zip(*[carve("pT%d" % i, [256], BF16) for i in range(4)])
        rr, rrb = carve("rr", [256])
        zz, zzb = carve("zz", [256])
        cwcount = [0]

        def proj_c(col0, consume):
            wi = cwcount[0] % NWC
            cwcount[0] += 1
            load_w(cwr[wi], w_in, col0, 128, cwrb[wi])
            for seg in range(4):
                bank = seg % 2

                def mm(e, seg=seg, bank=bank, wi=wi):
                    ins = None
                    for kc in range(KC):
                        ins = e.matmul(ps[bank][:, :], lhsT=cwr[wi][:, kc, :], rhs=hT[:, kc, seg * 512:(seg + 1) * 512],
                                       start=(kc == 0), stop=(kc == KC - 1))
                    return ins
                P.add("pe", mm, reads=[hTb[seg], cwrb[wi]], writes=[pbank[bank]])
                consume(seg, bank)

        for h in range(nheads_m):
            def cons_q(seg, bank):
                s0 = seg * 512
                P.add("act", lambda e: e.mul(out=qTf[:, s0:s0 + 512], in_=ps[bank][:, :], mul=SCQ_M),
                      reads=[pbank[bank]], writes=[qTfb])
                P.add("dve", lambda e: e.tensor_copy(out=qTb_[:, s0:s0 + 512], in_=qTf[:, s0:s0 + 512]),
                      reads=[qTfb], writes=[qTbb])
            SCQ_M = 128.0 ** -0.5
            proj_c(OFF_MQ + h * 128, cons_q)

            def cons_k(seg, bank):
                s0 = seg * 512
                P.add("act", lambda e: e.copy(out=kTb_[:, s0:s0 + 512], in_=ps[bank][:, :]),
                      reads=[pbank[bank]], writes=[kTbb])
                P.add("dve", lambda e: e.tensor_reduce(out=kmT[:, seg * 2:seg * 2 + 2],
                                                       in_=ps[bank][:, :].rearrange("p (a b) -> p a b", a=2),
                                                       axis=AX.X, op=ALU.add),
                      reads=[pbank[bank]], writes=[kmTb])
            proj_c(OFF_MK + h * 128, cons_k)

            def cons_z(seg, bank):
                s0 = seg * 512
                P.add("act", lambda e: e.activation(out=mszT[:, s0:s0 + 512], in_=ps[bank][:, :], func=AF.Silu),
                      reads=[pbank[bank]], writes=[mszb])
            proj_c(OFF_MZ + h * 128, cons_z)

            wi = cwcount[0] % NWC
            cwcount[0] += 1
            load_w(cwr[wi], w_in, OFF_MV + h * 128, 128, cwrb[wi])
            for g in range(4):
                bank = 2 + g % 2

                def mmv(e, g=g, bank=bank, wi=wi):
                    ins = None
                    for j in range(4):
                        tt = g * 4 + j
                        for kc in range(KC):
                            ins = e.matmul(ps[bank][:, j * 128:(j + 1) * 128], lhsT=hT[:, kc, tt * 128:(tt + 1) * 128],
                                           rhs=cwr[wi][:, kc, :], start=(kc == 0), stop=(kc == KC - 1))
                    return ins
                P.add("pe", mmv, reads=[hTb[g], cwrb[wi]], writes=[pbank[bank]])
                P.add("act", lambda e, g=g, bank=bank: e.copy(out=Vt[:, g * 4:(g + 1) * 4, :],
                                                              in_=ps[bank][:, :].rearrange("p (a b) -> p a b", a=4)),
                      reads=[pbank[bank]], writes=[Vtb])

            P.add("dve", lambda e: e.memset(gm, -1e30), writes=[gmb])
            P.add("dve", lambda e: e.memset(nb, 0.0), writes=[nbb])

            def mmg(e):
                ins = None
                for tt in range(8, NT):
                    ins = e.matmul(ps[4][:, tt * 8:tt * 8 + 8], lhsT=qTf[:, tt * 128:(tt + 1) * 128], rhs=kmT,
                                   start=True, stop=True)
                return ins
            P.add("pe", mmg, reads=[qTfb, kmTb], writes=[pbank[4]])
            for tt in range(8, NT):
                qb = tt // 2
                P.add("dve", lambda e, tt=tt, qb=qb: e.tensor_copy(out=gm[:, tt, 0:qb], in_=ps[4][:, tt * 8:tt * 8 + qb]),
                      reads=[pbank[4]], writes=[gmb])
                P.add("dve", lambda e, tt=tt: e.max(out=top8, in_=gm[:, tt, :]), reads=[gmb], writes=[top8b])
                P.add("dve", lambda e, tt=tt: e.tensor_scalar(out=nb[:, tt, :], in0=gm[:, tt, :], scalar1=top8[:, 2:3],
                                                              scalar2=NEG, op0=ALU.is_lt, op1=ALU.mult),
                      reads=[gmb, top8b], writes=[nbb])
            for g in range(4):
                bank = 5 + g % 2

                def mmt(e, g=g, bank=bank):
                    ins = None
                    for j in range(4):
                        tt = g * 4 + j
                        ins = e.matmul(ps[bank][0:8, j * 128:(j + 1) * 128], lhsT=nb[:, tt, :], rhs=I_, start=True,
                                       stop=True)
                    return ins
                P.add("pe", mmt, reads=[nbb, cfb], writes=[pbank[bank]])
                P.add("act", lambda e, g=g, bank=bank: e.copy(out=nbT[0:8, g * 512:(g + 1) * 512], in_=ps[bank][0:8, :]),
                      reads=[pbank[bank]], writes=[nbTb])
            if "moba_pre" in dbg and h == 0:
                tap("nbT", nbT[0:8, :], [8, S], BF16, [nbTb])
                tap("mqT", qTf, [128, S], F32, [qTfb])
                tap("mV", Vt, [128, NT, 128], BF16, [Vtb])

            for qb in range(8):
                q0 = qb * 256
                nk = 2 * qb + 2
                ob_bank = 6 + qb % 2
                pob = pbank[ob_bank]

                def emit_S(kt, qb=qb, q0=q0):
                    n = kt // 2
                    sbank = kt % 4
                    if kt == 2 * qb + 1:
                        def mm(e):
                            e.matmul(ps[sbank][:, 128:256], lhsT=kTb_[:, kt * 128:(kt + 1) * 128],
                                     rhs=qTb_[:, q0 + 128:q0 + 256], start=True, stop=False)
                            return e.matmul(ps[sbank][:, 128:256], lhsT=Ib, rhs=cbias[:, 0:128], start=False, stop=True)
                    elif kt == 2 * qb:
                        def mm(e):
                            e.matmul(ps[sbank][:, 0:256], lhsT=kTb_[:, kt * 128:(kt + 1) * 128], rhs=qTb_[:, q0:q0 + 256],
                                     start=True, stop=False)
                            return e.matmul(ps[sbank][:, 0:256], lhsT=Ib, rhs=cbias, start=False, stop=True)
                    elif qb <= 3:
                        def mm(e):
                            return e.matmul(ps[sbank][:, 0:256], lhsT=kTb_[:, kt * 128:(kt + 1) * 128],
                                            rhs=qTb_[:, q0:q0 + 256], start=True, stop=True)
                    else:
                        def mm(e):
                            e.matmul(ps[sbank][:, 0:256], lhsT=kTb_[:, kt * 128:(kt + 1) * 128], rhs=qTb_[:, q0:q0 + 256],
                                     start=True, stop=False)
                            return e.matmul(ps[sbank][:, 0:256], lhsT=cmb[0:8, 256 + n * 128:256 + (n + 1) * 128],
                                            rhs=nbT[0:8, q0:q0 + 256], start=False, stop=True)
                    P.add("pe", mm, reads=[kTbb, qTbb, cmbb, nbTb], writes=[pbank[sbank]])

                def emit_PV(kt, qb=qb, q0=q0, nk=nk, ob_bank=ob_bank, pob=pob):
                    sbank = kt % 4
                    c0 = 128 if kt == 2 * qb + 1 else 0
                    pt = pTr[kt % 4]
                    P.add("act", lambda e: e.activation(out=pt[:, c0:256], in_=ps[sbank][:, c0:256], func=AF.Exp),
                          reads=[pbank[sbank]], writes=[pTrb[kt % 4]])

                    def mm(e):
                        e.matmul(ps[ob_bank][:, c0:256], lhsT=Vt[:, kt, :], rhs=pt[:, c0:256], start=(kt == 0),
                                 stop=(kt == nk - 1))
                        return e.matmul(ps[ob_bank][:, 256 + c0:512], lhsT=onesb, rhs=pt[:, c0:256], start=False,
                                        stop=(kt == nk - 1), skip_group_check=True)
                    P.add("pe", mm, reads=[Vtb, pTrb[kt % 4], cmbb], writes=[pob])

                emit_S(0)
                emit_S(1)
                for kt in range(nk):
                    emit_PV(kt)
                    if kt + 2 < nk:
                        emit_S(kt + 2)
                P.add("dve", lambda e, ob_bank=ob_bank: e.reciprocal(out=rr, in_=ps[ob_bank][:, 256:512]),
                      reads=[pob], writes=[rrb])
                P.add("dve", lambda e, q0=q0: e.tensor_tensor(out=zz, in0=rr, in1=mszT[:, q0:q0 + 256], op=ALU.mult),
                      reads=[rrb, mszb], writes=[zzb])
                P.add("dve", lambda e, q0=q0, ob_bank=ob_bank, h=h: e.tensor_tensor(out=obT[:, h, q0:q0 + 256],
                                                                                   in0=ps[ob_bank][:, 0:256], in1=zz,
                                                                                   op=ALU.mult),
                      reads=[pob, zzb], writes=[obTb[qb // 2]])
        if "obT" in dbg:
            tap("obT", obT, [128, H, S], BF16, obTb)
    apos[0] = cmark

    if "D" in phases:
        dmark = apos[0]
        mT1, mT1p = carve("mT1", [KC, 1024], BF16)
        mT1b = subbufs(mT1p, "mT1", [[(c * 2048 + q * 1024, c * 2048 + (q + 1) * 1024) for c in range(KC)] for q in range(2)])
        ymark = apos[0]
        NWD = 2
        wga, wgab = zip(*[carve("wga%d" % i, [KC, 128], BF16) for i in range(NWD)])
        wgb_, wgbb = zip(*[carve("wgb%d" % i, [KC, 128], BF16) for i in range(NWD)])
        wba, wbab = zip(*[carve("wba%d" % i, [8, 128], BF16) for i in range(NWD)])
        wbb, wbbb = zip(*[carve("wbb%d" % i, [8, 128], BF16) for i in range(NWD)])
        sga, sgab = zip(*[carve("sga%d" % i, [512]) for i in range(2)])
        sgb, sgbb = zip(*[carve("sgb%d" % i, [512]) for i in range(2)])
        it = 0
        for hf in range(2):
            for c in range(KC):
                wi = it % NWD
                load_w(wga[wi], w_in, OFF_GATE_A + c * 128, 128, wgab[wi])
                load_w(wgb_[wi], w_in, OFF_GATE_B + c * 128, 128, wgbb[wi])
                load_w(wba[wi], w_a, c * 128, 128, wbab[wi])
                load_w(wbb[wi], w_b, c * 128, 128, wbbb[wi])
                for seg in range(2):
                    t0 = hf * 1024 + seg * 512
                    hq = t0 // 512
                    bs = (it * 2 + seg) % 2 * 4
                    si = seg

                    def mmg(e, w, bank, t0=t0):
                        ins = None
                        for kc in range(KC):
                            ins = e.matmul(ps[bank][:, :], lhsT=w[:, kc, :], rhs=hT[:, kc, t0:t0 + 512], start=(kc == 0),
                                           stop=(kc == KC - 1))
                        return ins

                    def mmb(e, w, src, bank, t0=t0):
                        ins = None
                        for kc in range(8):
                            ins = e.matmul(ps[bank][:, :], lhsT=w[:, kc, :], rhs=src[:, kc, t0:t0 + 512], start=(kc == 0),
                                           stop=(kc == 7))
                        return ins
                    P.add("pe", lambda e, wi=wi, bs=bs, mmg=mmg: mmg(e, wga[wi], bs), reads=[hTb[hq], wgab[wi]],
                          writes=[pbank[bs]])
                    P.add("act", lambda e, bs=bs, si=si: e.activation(out=sga[si], in_=ps[bs][:, :], func=AF.Sigmoid),
                          reads=[pbank[bs]], writes=[sgab[si]])
                    P.add("pe", lambda e, wi=wi, bs=bs, mmg=mmg: mmg(e, wgb_[wi], bs + 1), reads=[hTb[hq], wgbb[wi]],
                          writes=[pbank[bs + 1]])
                    P.add("act", lambda e, bs=bs, si=si: e.activation(out=sgb[si], in_=ps[bs + 1][:, :], func=AF.Sigmoid),
                          reads=[pbank[bs + 1]], writes=[sgbb[si]])
                    P.add("pe", lambda e, wi=wi, bs=bs, mmb=mmb: mmb(e, wba[wi], oaT, bs + 2), reads=[oaTb[hq], wbab[wi]],
                          writes=[pbank[bs + 2]])
                    P.add("dve", lambda e, bs=bs, si=si: e.tensor_tensor(out=sga[si], in0=ps[bs + 2][:, :], in1=sga[si],
                                                                         op=ALU.mult),
                          reads=[pbank[bs + 2], sgab[si]], writes=[sgab[si]])
                    P.add("pe", lambda e, wi=wi, bs=bs, mmb=mmb: mmb(e, wbb[wi], obT, bs + 3), reads=[obTb[hq], wbbb[wi]],
                          writes=[pbank[bs + 3]])
                    P.add("dve", lambda e, bs=bs, si=si: e.tensor_tensor(out=sgb[si], in0=ps[bs + 3][:, :], in1=sgb[si],
                                                                         op=ALU.mult),
                          reads=[pbank[bs + 3], sgbb[si]], writes=[sgbb[si]])
                    if hf == 0:
                        dst = mT1[:, c, seg * 512:(seg + 1) * 512]
                        dbuf = mT1b[seg]
                    else:
                        dst = hT[:, c, seg * 512:(seg + 1) * 512]
                        dbuf = hTb[seg]
                    P.add("dve", lambda e, si=si, dst=dst: e.tensor_tensor(out=dst, in0=sga[si], in1=sgb[si], op=ALU.add),
                          reads=[sgab[si], sgbb[si]], writes=[dbuf])
                it += 1
        if "mT" in dbg:
            tap("mT1", mT1, [128, KC, 1024], BF16, mT1b)
            tap("mT2", hT[:, :, 0:1024], [128, KC, 1024], BF16, hTb[0:2])
        apos[0] = ymark
        outsl = [Buf("outdram_s%d" % i) for i in range(2)]
        postw, postwb = carve("postw", [D])
        xn_junk, xnjb = carve("xnj", [D], BF16)
        xt, xtb = zip(*[carve("xt%d" % i, [D]) for i in range(2)])
        ysb, ysbb = carve("ysb", [D])
        dma("sp", postw, post_w.partition_broadcast(128), postwb)
        for q in range(4):
            dma("pool", oaT[:, :, q * 512:(q + 1) * 512],
                w_out[0:1024, q * 512:(q + 1) * 512].rearrange("(kc p) c -> p kc c", p=128), oaTb[q])
            dma("pool", obT[:, :, q * 512:(q + 1) * 512],
                w_out[1024:2048, q * 512:(q + 1) * 512].rearrange("(kc p) c -> p kc c", p=128), obTb[q])
        for tt in range(NT):
            sl = tt % 2
            dma("sp", xt[sl], x[tt * 128:(tt + 1) * 128, :], xtb[sl])
            for ct in range(4):
                bank = (tt % 2) * 4 + ct

                def mmy(e, tt=tt, ct=ct, bank=bank):
                    ins = None
                    for kc in range(KC):
                        if tt < 8:
                            lt = mT1[:, kc, tt * 128:(tt + 1) * 128]
                        else:
                            lt = hT[:, kc, (tt - 8) * 128:(tt - 7) * 128]
                        wsrc = oaT if kc < 8 else obT
                        ins = e.matmul(ps[bank][:, :], lhsT=lt, rhs=wsrc[:, kc % 8, ct * 512:(ct + 1) * 512],
                                       start=(kc == 0), stop=(kc == KC - 1))
                    return ins
                mb = mT1b[tt // 4] if tt < 8 else hTb[(tt - 8) // 4]
                P.add("pe", mmy, reads=[mb, oaTb[ct], obTb[ct]], writes=[pbank[bank]])
                P.add("act", lambda e, ct=ct, bank=bank: e.copy(out=ysb[:, ct * 512:(ct + 1) * 512], in_=ps[bank][:, :]),
                      reads=[pbank[bank]], writes=[ysbb])
            P.add("act", lambda e, sl=sl: e.activation(out=xn_junk, in_=ysb, func=AF.Square, accum_out=st[:, 0:1]),
                  reads=[ysbb], writes=[xnjb, stb])
            P.add("act", lambda e: e.activation(out=st[:, 1:2], in_=st[:, 0:1], func=AF.Sqrt, bias=epsc[:, 0:1],
                                                scale=1.0 / D), reads=[stb, epsb], writes=[stb])
            P.add("dve", lambda e: e.reciprocal(out=st[:, 2:3], in_=st[:, 1:2]), reads=[stb], writes=[stb])
            P.add("dve", lambda e: e.scalar_tensor_tensor(out=ysb, in0=ysb, scalar=st[:, 2:3], in1=postw, op0=ALU.mult,
                                                          op1=ALU.mult), reads=[ysbb, stb, postwb], writes=[ysbb])
            P.add("dve", lambda e, sl=sl: e.tensor_tensor(out=xt[sl], in0=xt[sl], in1=ysb, op=ALU.add),
                  reads=[xtb[sl], ysbb], writes=[xtb[sl]])
            dma("sp", out[tt * 128:(tt + 1) * 128, :], xt[sl], outsl[sl], reads=[xtb[sl]])
        apos[0] = dmark

    P.add("sp", lambda e: e.nop(), reads=[outb] + (outsl if "D" in phases else []))
    P.finalize()
    return nc, dbg_outs


def _in_maps(x, pre_norm_w, w_in, conv_w, a_log, dt_bias, gdn_norm_w, w_branch_a, w_branch_b, w_out, post_norm_w):
    cf, cm = make_consts()
    f = lambda a: np.ascontiguousarray(np.asarray(a, dtype=np.float32))
    shared = {
        "pre_w": f(pre_norm_w[0][None, :]),
        "post_w": f(post_norm_w[0][None, :]),
        "w_in": f(w_in[0]),
        "conv_wT": f(np.asarray(conv_w[0]).T),
        "a_log16": f(np.tile(np.asarray(a_log[0]), 16)[None, :]),
        "dt_bias16": f(np.tile(np.asarray(dt_bias[0]), 16)[None, :]),
        "gnw": f(np.asarray(gdn_norm_w[0])[:, None]),
        "w_a": f(w_branch_a[0]),
        "w_b": f(w_branch_b[0]),
        "w_out": f(w_out[0]),
        "cf": cf,
        "cm": cm,
    }
    return [dict(shared, x=f(x[b])) for b in range(x.shape[0])]


def kernel(**inputs):
    maps = _in_maps(**inputs)
    import os
    nc, _ = build(phases=os.environ.get("KPHASES", "ABCD"))
    res = run_bass_kernel_spmd(nc, maps, core_ids=list(range(len(maps))))
    return np.stack([np.asarray(r["out"], dtype=np.float32) for r in res.results], axis=0)
```
